# Optimizing a Trainium2 kernel written in Bass

```python
import jax, jax.numpy as jnp
from jax import lax
import numpy as np

D_MODEL = 1024
BATCH = 32
SEQ = 256
DEPTH = 1
DEC_BATCH = 2
DEC_SEQ = 4096
PAST_LEN = 512

GRID_W = 64
CHUNK = 128
D_G = 1024
G_HEADS = 8
G_DIM = D_G // G_HEADS
D_RNN = 1024
LRU_HEADS = 16
LRU_DIM = D_RNN // LRU_HEADS
CONV_W = 4
CONV_PAD_L = 2
LRU_C = 8.0
D_FF = 4 * D_MODEL
N_MOD = 6
EPS = 1e-6
SPLITS = [D_G, 2 * D_G, 2 * D_G + D_RNN, 2 * D_G + 2 * D_RNN, 2 * D_G + 2 * D_RNN + D_MODEL]
IN_COLS = 2 * D_G + 2 * D_RNN + 2 * D_MODEL

kernel_name = "hybrid_gmlp_rglru_diffusion_step"


def _rmsnorm(x, g):
    xf = x.astype(jnp.float32)
    y = xf * lax.rsqrt(jnp.mean(xf * xf, axis=-1, keepdims=True) + EPS)
    return (y * g.astype(jnp.float32)).astype(x.dtype)


def _centred_dwconv(x, w, b):
    L = x.shape[1]
    xp = jnp.pad(x, ((0, 0), (CONV_PAD_L, CONV_W - 1 - CONV_PAD_L), (0, 0)))
    y = xp[:, 0:L] * w[0]
    for k in range(1, CONV_W):
        y = y + xp[:, k:k + L] * w[k]
    return y + b


def _lru_combine(e1, e2):
    a1, b1 = e1
    a2, b2 = e2
    return a1 * a2, a2 * b1 + b2


def _scan_dir(a, bx, h0, reverse):
    a_cum, h_zero = lax.associative_scan(_lru_combine, (a, bx), axis=1, reverse=reverse)
    return h_zero + a_cum * h0[:, None, :]


def _sgu(u, v, g_sgu, w_sp, b_sp):
    B, L, _ = v.shape
    vn = _rmsnorm(v, g_sgu).reshape(B, L // CHUNK, CHUNK, G_HEADS, G_DIM)
    s = jnp.einsum('gqp,bnpgc->bnqgc', w_sp, vn) + b_sp.T[None, None, :, :, None]
    return u * s.reshape(B, L, D_G)


def _rglru(xr, h0_f, h0_b, conv_w, conv_b, w_ra, b_ra, w_ri, b_ri, lam):
    B, L, _ = xr.shape
    xc = _centred_dwconv(xr, conv_w, conv_b)
    xh = xc.reshape(B, L, LRU_HEADS, LRU_DIM)
    r = jax.nn.sigmoid((jnp.einsum('blhi,dhij->bldhj', xh, w_ra).reshape(B, L, 2, D_RNN) + b_ra).astype(jnp.float32))
    ig = jax.nn.sigmoid((jnp.einsum('blhi,dhij->bldhj', xh, w_ri).reshape(B, L, 2, D_RNN) + b_ri).astype(jnp.float32))
    log_a = -LRU_C * r * jax.nn.softplus(-lam.astype(jnp.float32))
    a = jnp.exp(log_a)
    bx = jnp.sqrt(jnp.maximum(-jnp.expm1(2.0 * log_a), 1e-12)) * ig * xc.astype(jnp.float32)[:, :, None, :]
    h_f = _scan_dir(a[:, :, 0], bx[:, :, 0], h0_f, False)
    h_b = _scan_dir(a[:, :, 1], bx[:, :, 1], h0_b, True)
    return h_f + h_b, h_f[:, -1], h_b[:, 0]


def _layer(x, c_vec, h0_f, h0_b, p):
    mod = (jax.nn.silu(c_vec) @ p['w_ada'] + p['b_ada'])[:, None, :]
    sh1, sc1, g1, sh2, sc2, g2 = jnp.split(mod, N_MOD, axis=-1)
    h = _rmsnorm(x, p['g_pre_mix']) * (1.0 + sc1) + sh1
    z = h @ p['w_in']
    u, v, xr, gr, ga, gb = jnp.split(z, SPLITS, axis=-1)
    y_g = _sgu(jax.nn.gelu(u), jax.nn.gelu(v), p['g_sgu'], p['w_sp'], p['b_sp'])
    h_rnn, hf, hb = _rglru(xr, h0_f, h0_b, p['conv_w'], p['conv_b'], p['w_ra'], p['b_ra'],
                           p['w_ri'], p['b_ri'], p['lam'])
    y_r = h_rnn.astype(x.dtype) * jax.nn.gelu(gr)
    merged = jax.nn.sigmoid(ga) * (y_g @ p['w_br_g']) + jax.nn.sigmoid(gb) * (y_r @ p['w_br_r'])
    x = x + g1 * _rmsnorm(merged @ p['w_out'], p['g_post_mix'])
    h = _rmsnorm(x, p['g_pre_mlp']) * (1.0 + sc2) + sh2
    f = jnp.square(jax.nn.relu(h @ p['w_ff1'])) @ p['w_ff2']
    x = x + g2 * _rmsnorm(f, p['g_post_mlp'])
    return x, hf, hb


def setup_inputs(seed: int = 0) -> dict:
    key = jax.random.key(seed)
    ks = jax.random.split(key, 32)
    nrm = lambda k, s, sc: jax.random.normal(k, s, jnp.float32) * sc
    p_lam = jax.random.uniform(ks[20], (DEPTH, 2, D_RNN), jnp.float32, 0.9, 0.999)
    return {
        'x_prompt': nrm(ks[0], (BATCH, SEQ, D_MODEL), 1.0),
        'x_sample': nrm(ks[1], (DEC_BATCH, DEC_SEQ, D_MODEL), 1.0),
        'state_lru': nrm(ks[2], (DEC_BATCH, DEPTH, 2, D_RNN), 0.5),
        'c': nrm(ks[3], (DEC_BATCH, D_MODEL), 1.0),
        'c_ctx': nrm(ks[4], (D_MODEL,), 1.0),
        'w_ada': nrm(ks[5], (DEPTH, D_MODEL, N_MOD * D_MODEL), 0.5 * D_MODEL ** -0.5),
        'b_ada': nrm(ks[6], (DEPTH, N_MOD * D_MODEL), 0.01),
        'g_pre_mix': 1.0 + nrm(ks[7], (DEPTH, D_MODEL), 0.05),
        'g_post_mix': 1.0 + nrm(ks[8], (DEPTH, D_MODEL), 0.05),
        'g_pre_mlp': 1.0 + nrm(ks[9], (DEPTH, D_MODEL), 0.05),
        'g_post_mlp': 1.0 + nrm(ks[10], (DEPTH, D_MODEL), 0.05),
        'w_in': nrm(ks[11], (DEPTH, D_MODEL, IN_COLS), D_MODEL ** -0.5),
        'g_sgu': 1.0 + nrm(ks[12], (DEPTH, D_G), 0.05),
        'w_sp': nrm(ks[13], (DEPTH, G_HEADS, CHUNK, CHUNK), CHUNK ** -0.5),
        'b_sp': 1.0 + nrm(ks[14], (DEPTH, G_HEADS, CHUNK), 0.05),
        'conv_w': nrm(ks[15], (DEPTH, CONV_W, D_RNN), CONV_W ** -0.5),
        'conv_b': nrm(ks[16], (DEPTH, D_RNN), 0.01),
        'w_ra': nrm(ks[17], (DEPTH, 2, LRU_HEADS, LRU_DIM, LRU_DIM), LRU_DIM ** -0.5),
        'b_ra': nrm(ks[18], (DEPTH, 2, D_RNN), 0.1),
        'w_ri': nrm(ks[19], (DEPTH, 2, LRU_HEADS, LRU_DIM, LRU_DIM), LRU_DIM ** -0.5),
        'b_ri': nrm(ks[21], (DEPTH, 2, D_RNN), 0.1),
        'lam': jnp.log(p_lam) - jnp.log1p(-p_lam),
        'w_br_g': nrm(ks[22], (DEPTH, D_G, D_MODEL), D_G ** -0.5),
        'w_br_r': nrm(ks[23], (DEPTH, D_RNN, D_MODEL), D_RNN ** -0.5),
        'w_out': nrm(ks[24], (DEPTH, D_MODEL, D_MODEL), D_MODEL ** -0.5),
        'w_ff1': nrm(ks[25], (DEPTH, D_MODEL, D_FF), D_MODEL ** -0.5),
        'w_ff2': nrm(ks[26], (DEPTH, D_FF, D_MODEL), D_FF ** -0.5),
    }


def reference(x_prompt, x_sample, state_lru, c, c_ctx, w_ada, b_ada, g_pre_mix, g_post_mix,
              g_pre_mlp, g_post_mlp, w_in, g_sgu, w_sp, b_sp, conv_w, conv_b, w_ra, b_ra,
              w_ri, b_ri, lam, w_br_g, w_br_r, w_out, w_ff1, w_ff2):
    y_prompt = x_prompt
    y_sample = x_sample
    B_ctx = x_prompt.shape[0]
    zeros_h = jnp.zeros((B_ctx, D_RNN), jnp.float32)
    ctx_states = []
    for l in range(DEPTH):
        p = {
            'w_ada': w_ada[l], 'b_ada': b_ada[l], 'g_pre_mix': g_pre_mix[l], 'g_post_mix': g_post_mix[l],
            'g_pre_mlp': g_pre_mlp[l], 'g_post_mlp': g_post_mlp[l], 'w_in': w_in[l], 'g_sgu': g_sgu[l],
            'w_sp': w_sp[l], 'b_sp': b_sp[l], 'conv_w': conv_w[l], 'conv_b': conv_b[l],
            'w_ra': w_ra[l], 'b_ra': b_ra[l], 'w_ri': w_ri[l], 'b_ri': b_ri[l], 'lam': lam[l],
            'w_br_g': w_br_g[l], 'w_br_r': w_br_r[l], 'w_out': w_out[l],
            'w_ff1': w_ff1[l], 'w_ff2': w_ff2[l],
        }
        y_prompt, hf, hb = _layer(y_prompt, c_ctx[None, :], zeros_h, zeros_h, p)
        ctx_states.append(jnp.stack([hf, hb], axis=1).astype(x_prompt.dtype))
        h0 = state_lru[:, l].astype(jnp.float32)
        y_sample, _, _ = _layer(y_sample, c, h0[:, 0], h0[:, 1], p)
    new_state_lru = jnp.stack(ctx_states, axis=1)
    return (y_prompt, y_sample, new_state_lru)
```

```python
from contextlib import ExitStack
import numpy as np
import concourse.bass as bass
import concourse.mybir as mybir
from concourse.bass_utils import run_bass_kernel_spmd

F32 = mybir.dt.float32
BF16 = mybir.dt.bfloat16
I32 = mybir.dt.int32
AF = mybir.ActivationFunctionType
ALU = mybir.AluOpType
AP = bass.AP

D = 1024
T = 1024
NT = 8
EPS = 1e-6
ENG_NAMES = ("pe", "act", "dve", "pool", "sp")


class Prog:
    def __init__(self, n_dma_sems=28, record_only=False):
        self.streams = {e: [] for e in ENG_NAMES}
        self.count = {e: 0 for e in ENG_NAMES}
        self.n_dma_sems = n_dma_sems
        self.dma_cnt = [0] * n_dma_sems
        self.dma_rr = 0
        self.cc_cnt = 0
        self.state = {}
        self.waited = {e: {} for e in ENG_NAMES}
        self.overlaps = {}
        self.record_only = record_only
        self.sw_ids = {}
        self.sw_gen = {}

    def _split(self, key):
        return key if isinstance(key, tuple) else (key, None)

    def _own_states(self, name, idx, create):
        d = self.state.setdefault(name, {})
        if idx is None:
            if create and None not in d:
                d[None] = [{}, {}]
            return list(d.values())
        if idx not in d and create:
            if None in d:
                d[idx] = [dict(d[None][0]), dict(d[None][1])]
            else:
                d[idx] = [{}, {}]
        out = []
        if idx in d:
            out.append(d[idx])
        if None in d:
            out.append(d[None])
        return out

    def _dep_states(self, key):
        name, idx = self._split(key)
        out = self._own_states(name, idx, False)
        for o in self.overlaps.get(name, ()):
            out.extend(self.state.get(o, {}).values())
        return out

    def _deps(self, reads, writes):
        need = {}

        def add(d):
            for src, seq in d.items():
                if need.get(src, -1) < seq:
                    need[src] = seq
        for k in reads:
            for st in self._dep_states(k):
                add(st[0])
        for k in writes:
            for st in self._dep_states(k):
                add(st[0])
                add(st[1])
        return need

    def _commit(self, me, reads, writes):
        src, seq = me
        for k in reads:
            name, idx = self._split(k)
            sts = self._own_states(name, idx, True)
            if idx is None:
                for st in sts:
                    st[1][src] = seq
            else:
                sts[0][1][src] = seq
        for k in writes:
            name, idx = self._split(k)
            sts = self._own_states(name, idx, True)
            tgt = sts if idx is None else sts[:1]
            for st in tgt:
                st[0] = {src: seq}
                st[1] = {}

    def _emit_waits(self, eng, need):
        w = self.waited[eng]
        for src, seq in need.items():
            if src == eng and eng == "pe":
                continue
            if w.get(src, 0) >= seq:
                continue
            w[src] = seq
            self.streams[eng].append(("wait", src, seq))

    def op(self, eng, fn, reads=(), writes=()):
        need = self._deps(reads, writes)
        self._emit_waits(eng, need)
        self.count[eng] += 1
        me = (eng, self.count[eng])
        self.streams[eng].append(("ins", fn))
        self._commit(me, reads, writes)
        return me

    def dma(self, out, in_, reads=(), writes=(), queue="sp", **kw):
        need = self._deps(reads, writes)
        if queue == "pool":
            dk = writes[0]
            if dk not in self.sw_ids:
                self.sw_ids[dk] = len(self.sw_ids)
                self.sw_gen[dk] = 0
            sid = self.sw_ids[dk]
            self.sw_gen[dk] += 1
            self._emit_waits(queue, need)
            self.count["pool"] += 1
            marker = self.count["pool"]
            self.streams[queue].append(("swdma", out, in_, sid, kw, getattr(self, "last_marker", 0)))
            self.last_marker = marker
            src = ("sw", sid, self.sw_gen[dk])
            for k in reads:
                name, idx = self._split(k)
                for st in (self._own_states(name, idx, True) if idx is None else self._own_states(name, idx, True)[:1]):
                    st[1][src] = 16
            for k in writes:
                name, idx = self._split(k)
                sts = self._own_states(name, idx, True)
                for st in (sts if idx is None else sts[:1]):
                    st[0] = {src: 16, "pool": marker}
                    st[1] = {}
            return (src, 16)
        k = self.dma_rr
        self.dma_rr = (self.dma_rr + 1) % self.n_dma_sems
        src = ("dma", k)
        if self.dma_cnt[k] > 0:
            need[src] = max(need.get(src, 0), self.dma_cnt[k])
        self._emit_waits(queue, need)
        self.dma_cnt[k] += 16
        me = (src, self.dma_cnt[k])
        self.streams[queue].append(("dma", out, in_, k, kw))
        self._commit(me, reads, writes)
        return me

    def cc(self, fn, reads=(), writes=()):
        need = self._deps(reads, writes)
        self._emit_waits("pool", need)
        self.cc_cnt += 1
        me = ("cc", self.cc_cnt)
        self.streams["pool"].append(("cc", fn))
        self._commit(me, reads, writes)
        return me

    def wait_all(self, eng, keys):
        self._emit_waits(eng, self._deps(keys, ()))

    EPOCH = 1024

    def build(self, nc, stack):
        E = self.EPOCH
        sem = {e: [stack.enter_context(nc.semaphore(f"s_{e}{i}")) for i in range(self.count[e] // E + 1)] for e in ENG_NAMES}
        dsem = [stack.enter_context(nc.semaphore(f"s_dma{k}")) for k in range(self.n_dma_sems)]
        cc_sem = stack.enter_context(nc.semaphore("s_cc"))
        block = stack.enter_context(nc.Block())
        deco = {"pe": block.tensor, "act": block.scalar, "dve": block.vector,
                "pool": block.gpsimd, "sp": block.sync}

        def semval(src, seq):
            if src == "cc":
                return cc_sem, seq
            if isinstance(src, tuple):
                return dsem[src[1]], seq
            return sem[src][(seq - 1) // E], (seq - 1) % E + 1

        for e in ENG_NAMES:
            def body(eng, items=self.streams[e], e=e):
                n = 0
                for it in items:
                    if it[0] == "wait":
                        s_, v_ = semval(it[1], it[2])
                        eng.wait_ge(s_, v_)
                    elif it[0] == "ins":
                        it[1](eng).then_inc(sem[e][n // E], 1)
                        n += 1
                    elif it[0] == "cc":
                        it[1](eng).then_inc(cc_sem)
                    else:
                        _, out, in_, k, kw = it
                        eng.dma_start(out=out, in_=in_, **kw).then_inc(dsem[k], 16)
            deco[e](body)


ALL_PH = ("C", "A", "L", "G", "M1", "M2", "F1", "F2")


class Arena:
    def __init__(self):
        self.bufs = []

    def add(self, name, nbytes, phases):
        nbytes = (nbytes + 63) // 64 * 64
        self.bufs.append([name, nbytes, set(ALL_PH if phases == "*" else phases), None])

    def pack(self):
        order = sorted(self.bufs, key=lambda b: (-len(b[2]), -b[1]))
        placed = []
        for b in order:
            cands = sorted([0] + [p[3] + p[1] for p in placed])
            for off in cands:
                ok = True
                for p in placed:
                    if p[2] & b[2] and off < p[3] + p[1] and p[3] < off + b[1]:
                        ok = False
                        break
                if ok:
                    b[3] = off
                    break
            placed.append(b)
        self.total = max(b[3] + b[1] for b in self.bufs)
        self.off = {b[0]: b[3] for b in self.bufs}
        self.size = {b[0]: b[1] for b in self.bufs}
        ov = {}
        for a in self.bufs:
            for b in self.bufs:
                if a is not b and a[3] < b[3] + b[1] and b[3] < a[3] + a[1]:
                    ov.setdefault(a[0], []).append(b[0])
        self.ov = ov


def rev_ap(ap2d):
    apl = [list(d) for d in ap2d.ap]
    n = apl[-1][1]
    st = apl[-1][0]
    apl[-1] = [-st, n]
    return AP(ap2d.tensor, ap2d.offset + st * (n - 1), apl)


def build_program():
    nc = bass.Bass("TRN2", target_bir_lowering=False)

    def din(name, shape, dt=F32):
        return nc.dram_tensor(name, list(shape), dt, kind="ExternalInput").ap()

    def dout(name, shape, dt=F32):
        return nc.dram_tensor(name, list(shape), dt, kind="ExternalOutput").ap()

    xin = din("xin", [2048, D])
    xhalo = din("xhalo", [3, D])
    xs_all = din("xs_all", [4096, D])
    hmask = din("hmask", [128, 3])
    cselT = din("cselT", [128, 16])
    st0T = din("st0T", [128, 16])
    mskf = din("mskf", [128, 64])
    mskb = din("mskb", [128, 64])
    kc_sel = din("kc_sel", [2, 256])
    w_ada = din("w_ada", [D, 6 * D])
    b_ada2 = din("b_ada2", [2, 6 * D])
    gvecT = din("gvecT", [128, 16])
    gpost_b = din("gpost_b", [128, 2 * D])
    gsgu_bd = din("gsgu_b", [128, D])
    lamT = din("lamT", [128, 16])
    hbT = din("hbT", [128, 32])
    cwT = din("cwT", [128, 32])
    cbT = din("cbT", [128, 8])
    bd_w = din("bd_w", [128, 32 * 128])
    wspT = din("wspT", [128, 8 * 128])
    bsp = din("bsp", [1, D])
    w_st = din("w_st", [44, 128, 2048])
    w_v = din("w_v", [D, D])
    w_out = din("w_out", [D, D])
    w_ff1 = din("w_ff1", [16, 128, 2048])
    w_ff2 = din("w_ff2", [4 * D, D])
    y_out = dout("y_out", [2048, D])
    ns_out = dout("ns_out", [8, D])

    ar = Arena()
    A_ = ar.add
    A_("ident_b", 256, "*"); A_("ident_f", 512, "*"); A_("sel", 1024, "*"); A_("ones_b", 256, "*")
    A_("cT", 64, "*"); A_("sT", 64, "*"); A_("modT", 384, "*"); A_("modc", 256, "*")
    A_("lamc", 256, "*"); A_("hb", 128, "*"); A_("cw", 128, "*"); A_("cb", 32, "*")
    A_("bd", 8192, "*"); A_("wsp", 2048, "*"); A_("bspf", 4096, ["C"]); A_("bhi", 2048, "*"); A_("blo", 2048, "*")
    A_("gate", 16384, "*"); A_("gsgu", 4096, ["G"]); A_("hmask", 64, "*")
    A_("h0t", 64, "*"); A_("st0", 64, "*"); A_("msk", 512, "*"); A_("summ", 128, "*"); A_("allg", 1024, "*")
    A_("cand", 1024, "*"); A_("stT", 256, "*"); A_("sto", 4096, ["L"])
    A_("rs", 16 * 32, "*"); A_("ss", 512, "*"); A_("junk", 2048, "*"); A_("gvec", 64, "*")
    A_("ws", 4 * 4096, "*"); A_("wsf", 2 * 8192, "*")
    A_("wa", 2 * 16384, ["C"]); A_("modrow", 2 * 2048, ["C"]); A_("gpost", 8192, ["C"]); A_("bada", 2 * 2048, ["C"])
    A_("xt", 2 * 4096, ["A", "M2"]); A_("xn", 2 * 2048, ["A", "M2"]); A_("xh", 4096, ["A"]); A_("xhn", 2048, ["A"])
    A_("hT", 8 * 1027 * 2, ["A", "L", "G", "M1"])
    for sfx in ("", "_b"):
        A_("xrp" + sfx, 1036 * 4, ["L"]); A_("xc" + sfx, 4096, ["L"]); A_("xcb" + sfx, 2048, ["L"]); A_("gg" + sfx, 4096, ["L"])
        for d_ in range(2):
            A_(f"tr{d_}" + sfx, 4096, ["L"]); A_(f"aa{d_}" + sfx, 4096, ["L"]); A_(f"ti{d_}" + sfx, 4096, ["L"]); A_(f"hh{d_}" + sfx, 4096, ["L"])
    A_("yr", 16384, ["L", "G", "M1"])
    A_("gu", 16384, ["G"]); A_("wv", 16384, ["G"]); A_("gv", 2 * 4096, ["G"]); A_("vn", 2 * 2048, ["G"])
    A_("yg", 16384, ["G", "M1"])
    A_("ta", 2 * 2048, ["M1"]); A_("tb", 2 * 2048, ["M1"]); A_("m1", 2 * 2048, ["M1"]); A_("m2", 2 * 2048, ["M1"])
    A_("mg", 16384, ["M1", "M2"])
    A_("wo", 16384, ["M2"]); A_("tmp", 4096, ["M2"])
    A_("x1", 32768, ["M2", "F1", "F2"]); A_("h2T", 16384, ["M2", "F1"])
    A_("f1T", 65536, ["F1", "F2"]); A_("rt", 2 * 1024, ["F1"]); A_("f2", 32768, ["F2"])
    ar.pack()
    import os as _os2
    if _os2.environ.get("KARENA"):
        for ph in ALL_PH:
            print(ph, sum(b[1] for b in ar.bufs if ph in b[2]))
        print("total", ar.total)
    assert ar.total <= 206 * 1024, ar.total
    arena = nc.alloc_sbuf_tensor("arena", [128, ar.total // 4], F32)

    def V(name, dt=F32, pattern=None, **kw):
        o = ar.off[name] // 4
        n = ar.size[name] // 4
        v = arena[:, o:o + n]
        if dt != F32:
            v = v.bitcast(dt)
        if pattern:
            v = v.rearrange(pattern, **kw)
        return v

    ident_b = V("ident_b", BF16); ident_f = V("ident_f"); sel = V("sel")[0:2, :]; ones_b = V("ones_b", BF16)
    cT = V("cT"); sT = V("sT"); modT = V("modT", F32, "p (c r) -> p c r", r=2)
    modc = V("modc", F32, "p (w r k) -> p w r k", w=4, r=2)
    lamc = V("lamc", F32, "p (w d j) -> p w d j", w=4, d=2)
    hb = V("hb", F32, "p (g j) -> p g j", g=4); cw = V("cw", F32, "p (j k) -> p j k", k=4); cb = V("cb")
    bd = V("bd", BF16, "p (g j m) -> p g j m", g=4, j=8); wsp = V("wsp", BF16, "p (g q) -> p g q", g=8)
    bspf = V("bspf"); bhi = V("bhi", BF16); blo = V("blo", BF16)
    gate = V("gate", F32, "p (w r n) -> p w r n", w=2, r=2); gsgu = V("gsgu"); hmk = V("hmask")
    h0t = V("h0t", F32, "p (d j) -> p d j", d=2); st0 = V("st0", F32, "p (d j) -> p d j", d=2)
    msk = V("msk", F32, "p (w r j) -> p w r j", w=2, r=8)
    summ = V("summ", F32, "p (q j) -> p q j", q=4); allg = V("allg", F32, "p (r q j) -> p r q j", r=8, q=4)
    cand = V("cand", F32, "p (w r j) -> p w r j", w=4, r=8)
    stT = V("stT", F32, "p (j s d) -> p j s d", j=8, s=4); sto = V("sto")
    rs = V("rs", F32, "p (s c) -> p s c", c=8); ssq = V("ss", F32, "p (s c) -> p s c", c=4)
    junk = V("junk", BF16); gvec = V("gvec", F32, "p (w k) -> p w k", w=2)
    ws = V("ws", BF16, "p (s n) -> p s n", s=4)
    wsf = V("wsf", F32, "p (s n) -> p s n", s=2)
    wa = V("wa", F32, "p (s k n) -> p s k n", s=2, k=8); modrow = V("modrow", F32, "p (s n) -> p s n", s=2)
    gpost = V("gpost", F32, "p (w n) -> p w n", w=2); bada = V("bada", F32, "p (s n) -> p s n", s=2)
    xt = V("xt", F32, "p (s n) -> p s n", s=2); xn = V("xn", BF16, "p (s n) -> p s n", s=2)
    xh = V("xh"); xhn = V("xhn", BF16)
    hT = V("hT", BF16, "p (k n) -> p k n", k=8)
    LS = []
    for sfx in ("", "_b"):
        LS.append(dict(xrp=V("xrp" + sfx), xc=V("xc" + sfx), xcb=V("xcb" + sfx, BF16), gg=V("gg" + sfx),
                       trb=[V(f"tr{d_}" + sfx) for d_ in range(2)], aab=[V(f"aa{d_}" + sfx) for d_ in range(2)],
                       tib=[V(f"ti{d_}" + sfx) for d_ in range(2)], hhb=[V(f"hh{d_}" + sfx) for d_ in range(2)], sfx=sfx))
    yr = V("yr", BF16, "p (k n) -> p k n", k=8); yg = V("yg", BF16, "p (k n) -> p k n", k=8)
    gu = V("gu", BF16, "p (k n) -> p k n", k=8); wv = V("wv", BF16, "p (k n) -> p k n", k=8)
    gv = V("gv", F32, "p (s n) -> p s n", s=2); vn = V("vn", BF16, "p (s n) -> p s n", s=2)
    ta = V("ta", F32, "p (s n) -> p s n", s=2); tb = V("tb", F32, "p (s n) -> p s n", s=2)
    m1 = V("m1", F32, "p (s n) -> p s n", s=2); m2 = V("m2", F32, "p (s n) -> p s n", s=2)
    mg = V("mg", BF16, "p (k n) -> p k n", k=8); wo = V("wo", BF16, "p (k n) -> p k n", k=8)
    tmp = V("tmp"); x1 = V("x1", F32, "p (t n) -> p t n", t=8); h2T = V("h2T", BF16, "p (k n) -> p k n", k=8)
    f1T = V("f1T", BF16, "p (k n) -> p k n", k=32); rt = V("rt", BF16, "p (s n) -> p s n", s=2)
    f2 = V("f2", F32, "p (t n) -> p t n", t=8)

    pp = [nc.alloc_psum_tensor(f"pp{i}", [128, 1024], F32) for i in range(4)]

    def bank(b):
        return pp[b // 2][:, (b % 2) * 512:(b % 2) * 512 + 512]

    def emit(p, wplan):
        st = {"bank": 0, "pair": 0, "rs": 0, "ss": 0, "wi": 0, "issued": 0, "alt": 0, "sf": 0}

        def nb():
            b = st["bank"]; st["bank"] = (b + 1) % 8
            return b

        def npair():
            b = st["pair"]; st["pair"] = (b + 1) % 4
            return b

        def alt():
            st["alt"] ^= 1
            return "act" if st["alt"] else "dve"

        PSK = lambda b: ("ps", b)

        def stage_piece(src, dst, dkey):
            k = st["sf"]; st["sf"] = (k + 1) % 2
            n = 1
            for d_ in src.shape[1:]:
                n *= d_
            sfl = wsf[:, k, 0:n]
            sv = sfl
            if len(src.shape) == 3:
                sv = sfl.rearrange("p (a n) -> p a n", a=src.shape[1])
            p.dma(sv, src, writes=[("wsf", k)])
            dfl = dst
            p.op("pool", lambda g_: g_.tensor_copy(out=dfl, in_=(sv if len(dst.shape) == 3 else sfl)), reads=[("wsf", k)], writes=[dkey])

        def wunit(desc):
            i = st["wi"]; st["wi"] += 1
            if wplan is None:
                p.wlist.append(desc)
            else:
                while st["issued"] < min(len(wplan), i + 3):
                    k = st["issued"]; st["issued"] += 1
                    src = wplan[k][0]
                    dst = ws[:, k % 4, :]
                    if len(src.shape) == 3:
                        dst = dst.rearrange("p (a n) -> p a n", a=src.shape[1])
                    stage_piece(src, dst, ("ws", k % 4))
            return ws[:, i % 4, :], ("ws", i % 4)

        def rstd(ss_list, keys, np_, scale, eps):
            s = st["rs"]; st["rs"] = (s + 1) % 16
            R = rs[0:np_, s, :]
            k = ("rs", s)
            xx, tii, yy, hh_, zz = (R[:, c:c + 1] for c in range(5))
            if len(ss_list) == 2:
                p.op("dve", lambda v: v.tensor_tensor(out=xx, in0=ss_list[0], in1=ss_list[1], op=ALU.add), reads=keys, writes=[k])
                src = xx
                rk = [k]
            else:
                src = ss_list[0]
                rk = keys
            p.op("dve", lambda v: v.tensor_scalar(out=xx, in0=src, scalar1=scale, scalar2=eps, op0=ALU.mult, op1=ALU.add), reads=rk, writes=[k])
            p.op("dve", lambda v: v.tensor_scalar(out=tii.bitcast(I32), in0=xx.bitcast(I32), scalar1=1, scalar2=None, op0=ALU.arith_shift_right), reads=[k], writes=[k])
            p.op("dve", lambda v: v.tensor_scalar(out=yy.bitcast(I32), in0=tii.bitcast(I32), scalar1=-1, scalar2=0x5f3759df, op0=ALU.mult, op1=ALU.add), reads=[k], writes=[k])
            p.op("dve", lambda v: v.tensor_scalar(out=hh_, in0=xx, scalar1=0.5, scalar2=None, op0=ALU.mult), reads=[k], writes=[k])
            for _ in range(3):
                p.op("dve", lambda v: v.scalar_tensor_tensor(out=zz, in0=yy, scalar=yy, in1=hh_, op0=ALU.mult, op1=ALU.mult), reads=[k], writes=[k])
                p.op("dve", lambda v: v.tensor_scalar(out=zz, in0=zz, scalar1=-1.0, scalar2=1.5, op0=ALU.mult, op1=ALU.add), reads=[k], writes=[k])
                p.op("dve", lambda v: v.tensor_tensor(out=yy, in0=yy, in1=zz, op=ALU.mult), reads=[k], writes=[k])
            return yy, k

        def sscol():
            s = st["ss"]; st["ss"] = (s + 1) % 32
            return ssq[:, s, :], ("ss", s)

        def stage_consts():
            sp_loads = [(cT, cselT, "cT"), (st0.rearrange("p d j -> p (d j)"), st0T, "st0"),
                        (msk[:, 0].rearrange("p r j -> p (r j)"), mskf, ("msk", 0)), (msk[:, 1].rearrange("p r j -> p (r j)"), mskb, ("msk", 1)),
                        (sel, kc_sel, "sel"), (gvec.rearrange("p w k -> p (w k)"), gvecT, "gvec"),
                        (lamc[:, 3].rearrange("p d j -> p (d j)"), lamT, "lamc"), (hb.rearrange("p g j -> p (g j)"), hbT, "hb"),
                        (cw.rearrange("p j k -> p (j k)"), cwT, "cw"), (cb[:, 0:8], cbT, "cb"), (hmk[:, 0:3], hmask, "hmask"),
                        (bspf[0:1, 0:D], bsp, "bspf"),
                        (gpost.rearrange("p w n -> p (w n)"), gpost_b, "gpost")]
            for dst, src, key in sp_loads:
                p.dma(dst, src, writes=[key])
            bdf = bd.rearrange("p g j m -> p (g j m)")
            for i_ in range(2):
                stage_piece(bd_w[:, i_ * 2048:(i_ + 1) * 2048], bdf[:, i_ * 2048:(i_ + 1) * 2048], ("bd", i_))
            stage_piece(wspT, wsp.rearrange("p g q -> p (g q)"), "wsp")
            p.op("pool", lambda g_: g_.memset(ident_f, 0.0), writes=["ident_f"])
            p.op("pool", lambda g_: g_.affine_select(out=ident_f, in_=ident_f, compare_op=ALU.not_equal, fill=1.0, base=0,
                                                      pattern=[[-1, 128]], channel_multiplier=1), reads=["ident_f"], writes=["ident_f"])
            p.op("dve", lambda v: v.tensor_copy(out=ident_b, in_=ident_f), reads=["ident_f"], writes=["ident_b"])
            p.op("pool", lambda g_: g_.memset(ones_b, 1.0), writes=["ones_b"])
            B0 = bspf[0:1, 0:D]; B1 = bspf[0:1, D:2 * D] if False else None
            p.op("dve", lambda v: v.tensor_copy(out=bhi[0:1, 0:D], in_=B0), reads=["bspf"], writes=["bhi"])
            p.op("dve", lambda v: v.tensor_tensor(out=B0, in0=B0, in1=bhi[0:1, 0:D], op=ALU.subtract), reads=["bspf", "bhi"], writes=["bspf"])
            p.op("dve", lambda v: v.tensor_copy(out=blo[0:1, 0:D], in_=B0), reads=["bspf"], writes=["blo"])
            p.op("act", lambda a: a.activation(out=sT, in_=cT, func=AF.Tanh, scale=0.5), reads=["cT"], writes=["sT"])
            p.op("dve", lambda v: v.scalar_tensor_tensor(out=sT, in0=sT, scalar=1.0, in1=cT, op0=ALU.add, op1=ALU.mult), reads=["sT", "cT"], writes=["sT"])
            p.op("dve", lambda v: v.tensor_scalar(out=sT, in0=sT, scalar1=0.5, scalar2=None, op0=ALU.mult), reads=["sT"], writes=["sT"])
            sT3 = sT.rearrange("p (k r) -> p k r", r=2)
            L3 = lamc[:, 3]
            p.op("act", lambda a: a.activation(out=L3, in_=L3, func=AF.Exp, scale=-1.0), reads=["lamc"], writes=["lamc"])
            p.op("act", lambda a: a.activation(out=L3, in_=L3, func=AF.Ln, bias=1.0, scale=1.0), reads=["lamc"], writes=["lamc"])
            p.op("dve", lambda v: v.tensor_scalar(out=lamc[:, 0], in0=L3, scalar1=-8.0, scalar2=None, op0=ALU.mult), reads=["lamc"], writes=["lamc"])
            p.op("dve", lambda v: v.tensor_scalar(out=lamc[:, 1], in0=L3, scalar1=-4.0, scalar2=None, op0=ALU.mult), reads=["lamc"], writes=["lamc"])
            p.op("dve", lambda v: v.tensor_scalar(out=lamc[:, 2], in0=L3, scalar1=-4096.0, scalar2=None, op0=ALU.mult), reads=["lamc"], writes=["lamc"])
            p.op("dve", lambda v: v.tensor_scalar(out=hb.rearrange("p g j -> p (g j)"), in0=hb.rearrange("p g j -> p (g j)"), scalar1=0.5, scalar2=None, op0=ALU.mult), reads=["hb"], writes=["hb"])
            order = [2, 3, 0, 1, 4, 5, 6, 7, 8, 9, 10, 11]
            mt_bank = nb()
            mt_ps = bank(mt_bank)[:, 0:96].rearrange("p (c r) -> p c r", r=2)
            for n_, cbk in enumerate(order):
                s = n_ % 2
                p.dma(wa[:, s], w_ada[:, cbk * 512:(cbk + 1) * 512].rearrange("(k p) n -> p k n", p=128), writes=[("wa", s)])
                p.dma(bada[0:2, s, :], b_ada2[:, cbk * 512:(cbk + 1) * 512], writes=[("bada", s)])
                b = nb()
                if b == mt_bank:
                    b = nb()
                for kc in range(8):
                    p.op("pe", lambda t_, kc=kc, s=s, b=b: t_.matmul(bank(b)[0:2, :], lhsT=sT3[:, kc, :], rhs=wa[:, s, kc, :], start=(kc == 0), stop=(kc == 7)),
                         reads=["sT", ("wa", s)], writes=[PSK(b)])
                p.op("dve", lambda v, s=s, b=b: v.tensor_tensor(out=modrow[0:2, s, :], in0=bank(b)[0:2, :], in1=bada[0:2, s, :], op=ALU.add),
                     reads=[PSK(b), ("bada", s)], writes=[("modrow", s)])
                for i in range(4):
                    ci = cbk * 4 + i
                    p.op("pe", lambda t_, s=s, i=i, ci=ci: t_.matmul(mt_ps[:, ci, :], lhsT=modrow[0:2, s, i * 128:(i + 1) * 128], rhs=ident_f[0:2, 0:2], start=True, stop=True),
                         reads=[("modrow", s), "ident_f"], writes=[PSK(mt_bank)])
                if cbk in (4, 5, 10, 11):
                    w_ = 0 if cbk < 6 else 1
                    hf = cbk % 2
                    for r in range(2):
                        b2 = nb()
                        if b2 == mt_bank:
                            b2 = nb()
                        p.op("pe", lambda t_, s=s, r=r, b2=b2: t_.matmul(bank(b2), lhsT=sel[:, r * 128:(r + 1) * 128], rhs=modrow[0:2, s, :], start=True, stop=True),
                             reads=[("modrow", s), "sel"], writes=[PSK(b2)])
                        p.op("dve", lambda v, w_=w_, r=r, hf=hf, b2=b2: v.tensor_tensor(out=gate[:, w_, r, hf * 512:(hf + 1) * 512], in0=bank(b2), in1=gpost[:, w_, hf * 512:(hf + 1) * 512], op=ALU.mult),
                             reads=[PSK(b2), "gpost"], writes=[("gate", w_ * 2 + r)])
            p.op("dve", lambda v: v.tensor_copy(out=modT, in_=mt_ps), reads=[PSK(mt_bank)], writes=["modT"])
            for r in range(2):
                for w_, (sc_i, sh_i) in enumerate(((8, 0), (32, 24))):
                    p.op("dve", lambda v, r=r, w_=w_, sc_i=sc_i: v.scalar_tensor_tensor(out=modc[:, 2 * w_, r, :], in0=modT[:, sc_i:sc_i + 8, r], scalar=1.0, in1=gvec[:, w_, :], op0=ALU.add, op1=ALU.mult),
                         reads=["modT", "gvec"], writes=["modc"])
                    p.op("dve", lambda v, r=r, w_=w_, sh_i=sh_i: v.tensor_copy(out=modc[:, 2 * w_ + 1, r, :], in_=modT[:, sh_i:sh_i + 8, r]),
                         reads=["modT"], writes=["modc"])

        def norm_transpose(src_tile, src_key, np_, r, widx, dstT, dst_key, col0, xn_tile, xn_key):
            sc, sk = sscol()
            p.op("act", lambda a: a.activation(out=junk[0:np_, :], in_=src_tile, func=AF.Square, accum_out=sc[0:np_, 0:1]),
                 reads=[src_key], writes=["junk", sk])
            rr, rk = rstd([sc[0:np_, 0:1]], [sk], np_, 1.0 / D, EPS)
            p.op("act", lambda a: a.activation(out=xn_tile, in_=src_tile, func=AF.Identity, scale=rr), reads=[src_key, rk], writes=[xn_key])
            b = nb()
            pt = bank(b).bitcast(BF16).rearrange("p (k n) -> p k n", k=8)
            for kc in range(8):
                p.op("pe", lambda t_, kc=kc: t_.transpose(pt[:, kc, 0:np_], xn_tile[:, kc * 128:(kc + 1) * 128], ident_b[0:np_, 0:np_]),
                     reads=[xn_key, "ident_b"], writes=[PSK(b)])
            use_act = alt() == "act"
            for kc in range(8):
                scl = modc[:, 2 * widx, r, kc:kc + 1]; shf = modc[:, 2 * widx + 1, r, kc:kc + 1]
                o = dstT[:, kc, col0:col0 + np_]
                if kc % 2 == 0:
                    p.op("act", lambda a, o=o, kc=kc, scl=scl, shf=shf: a.activation(out=o, in_=pt[:, kc, 0:np_], func=AF.Identity, scale=scl, bias=shf),
                         reads=[PSK(b), "modc"], writes=[dst_key])
                else:
                    p.op("dve", lambda v, o=o, kc=kc, scl=scl, shf=shf: v.tensor_scalar(out=o, in0=pt[:, kc, 0:np_], scalar1=scl, scalar2=shf, op0=ALU.mult, op1=ALU.add),
                         reads=[PSK(b), "modc"], writes=[dst_key])

        def stage_A(half, chunk=None):
            r = half
            src, base = (xin, half * T) if chunk is None else (xs_all, chunk * T)
            for t in range(NT):
                s = t % 2
                p.dma(xt[:, s, :], src[base + t * 128: base + (t + 1) * 128, :], writes=[("xt", s)])
                norm_transpose(xt[:, s, :], ("xt", s), 128, r, 0, hT, ("hT", t), t * 128, xn[:, s, :], ("xn", s))
            if half == 1:
                if chunk is None:
                    p.dma(xh[0:3, 0:D], xhalo, writes=["xh"])
                else:
                    p.op("dve", lambda v: v.memset(xh[0:3, 0:D], 0.0), writes=["xh"])
                    if chunk > 0:
                        p.dma(xh[0:2, 0:D], xs_all[base - 2:base, :], writes=["xh"])
                    if chunk < 3:
                        p.dma(xh[2:3, 0:D], xs_all[base + T:base + T + 1, :], writes=["xh"])
                norm_transpose(xh[0:3, 0:D], "xh", 3, r, 0, hT, ("hT", 8), T, xhn[0:3, 0:D], "xhn")

        HTB = lambda blk: [("hT", 4 * blk + i) for i in range(4)]

        def mm_st(b, wview, rhs_fn, rkeys, wkey, ncols=512):
            for kc in range(8):
                p.op("pe", lambda t_, kc=kc: t_.matmul(bank(b)[:, 0:ncols], lhsT=wview[:, kc, :], rhs=rhs_fn(kc), start=(kc == 0), stop=(kc == 7)),
                     reads=[wkey] + rkeys, writes=[PSK(b)])

        def stage_L(half, prepass, chunk=None):
            nseq, L = (4, 256) if half == 0 else (1, 1024)
            XW = L + 3
            v3 = lambda tl: tl.rearrange("p (s n) -> p s n", s=nseq)
            for S_ in LS:
                p.op("dve", lambda v, S_=S_: v.memset(S_["xrp"], 0.0), writes=["xrp" + S_["sfx"]])
            def do_chunk(j, S_):
                xrp, xc, xcb, gg = S_["xrp"], S_["xc"], S_["xcb"], S_["gg"]
                trb, aab, tib, hhb = S_["trb"], S_["aab"], S_["tib"], S_["hhb"]
                KS = lambda n: n + S_["sfx"]
                xrp3 = xrp[:, 0:nseq * XW].rearrange("p (s n) -> p s n", s=nseq)
                wsl, wk = wunit((w_st[8 + j], None))
                wx = wsl.rearrange("p (u k n) -> p u k n", u=2, k=8)
                for blk in range(2):
                    b = nb()
                    mm_st(b, wx[:, 0], lambda kc, blk=blk: hT[:, kc, blk * 512:(blk + 1) * 512], HTB(blk), wk)
                    if half == 0:
                        dst = xrp3[:, 2 * blk:2 * blk + 2, 2:2 + L]
                        src = bank(b).rearrange("p (s n) -> p s n", s=2)
                    else:
                        dst = xrp[:, 2 + blk * 512:2 + (blk + 1) * 512]
                        src = bank(b)
                    p.op("dve", lambda v, dst=dst, src=src: v.tensor_copy(out=dst, in_=src), reads=[PSK(b)], writes=[KS("xrp")])
                if half == 1:
                    b = nb()
                    mm_st(b, wx[:, 0], lambda kc: hT[:, kc, T:T + 3], [("hT", 8)], wk, ncols=3)
                    if prepass:
                        if chunk > 0:
                            p.op("dve", lambda v, b=b: v.tensor_copy(out=xrp[:, 0:2], in_=bank(b)[:, 0:2]), reads=[PSK(b)], writes=[KS("xrp")])
                        if chunk < 3:
                            p.op("dve", lambda v, b=b: v.tensor_copy(out=xrp[:, 2 + T:3 + T], in_=bank(b)[:, 2:3]), reads=[PSK(b)], writes=[KS("xrp")])
                    else:
                        p.op("dve", lambda v, b=b: v.tensor_tensor(out=xrp[:, 0:2], in0=bank(b)[:, 0:2], in1=hmk[:, 0:2], op=ALU.mult), reads=[PSK(b), "hmask"], writes=[KS("xrp")])
                        p.op("dve", lambda v, b=b: v.tensor_tensor(out=xrp[:, 2 + T:3 + T], in0=bank(b)[:, 2:3], in1=hmk[:, 2:3], op=ALU.mult), reads=[PSK(b), "hmask"], writes=[KS("xrp")])
                if not prepass:
                    for blk in range(2):
                        b = nb()
                        mm_st(b, wx[:, 1], lambda kc, blk=blk: hT[:, kc, blk * 512:(blk + 1) * 512], HTB(blk), wk)
                        p.op("act", lambda a, b=b, blk=blk: a.activation(out=gg[:, blk * 512:(blk + 1) * 512], in_=bank(b), func=AF.Gelu_apprx_tanh),
                             reads=[PSK(b)], writes=[KS("gg")])
                xc3 = v3(xc)
                p.op("dve", lambda v, j=j: v.tensor_scalar(out=xc3, in0=xrp3[:, :, 0:L], scalar1=cw[:, j, 0:1], scalar2=cb[:, j:j + 1], op0=ALU.mult, op1=ALU.add),
                     reads=[KS("xrp"), "cw", "cb"], writes=[KS("xc")])
                for k in range(1, 4):
                    p.op("dve", lambda v, j=j, k=k: v.scalar_tensor_tensor(out=xc3, in0=xrp3[:, :, k:k + L], scalar=cw[:, j, k:k + 1], in1=xc3, op0=ALU.mult, op1=ALU.add),
                         reads=[KS("xrp"), "cw", KS("xc")], writes=[KS("xc")])
                p.op("act", lambda a: a.copy(out=xcb, in_=xc), reads=[KS("xc")], writes=[KS("xcb")])
                tsum = []
                for g in range(4):
                    d_ = g % 2
                    dstb = trb[d_] if g < 2 else tib[d_]
                    dkey = KS(f"tr{d_}") if g < 2 else KS(f"ti{d_}")
                    for blk in range(2):
                        b = nb()
                        p.op("pe", lambda t_, g=g, j=j, blk=blk, b=b: t_.matmul(bank(b), lhsT=bd[:, g, j, :], rhs=xcb[:, blk * 512:(blk + 1) * 512], start=True, stop=True),
                             reads=["bd", KS("xcb")], writes=[PSK(b)])
                        if prepass and g < 2:
                            sc, sk = sscol()
                            tsum.append((d_, sc, sk))
                            p.op("act", lambda a, g=g, j=j, blk=blk, b=b, dstb=dstb, sc=sc: a.activation(out=dstb[:, blk * 512:(blk + 1) * 512], in_=bank(b), func=AF.Tanh, scale=0.5, bias=hb[:, g, j:j + 1], accum_out=sc[:, 0:1]),
                                 reads=[PSK(b), "hb"], writes=[dkey, sk])
                        else:
                            p.op("act", lambda a, g=g, j=j, blk=blk, b=b, dstb=dstb: a.activation(out=dstb[:, blk * 512:(blk + 1) * 512], in_=bank(b), func=AF.Tanh, scale=0.5, bias=hb[:, g, j:j + 1]),
                                 reads=[PSK(b), "hb"], writes=[dkey])
                for d_ in range(2):
                    p.op("act", lambda a, d_=d_, j=j: a.activation(out=aab[d_], in_=trb[d_], func=AF.Exp, scale=lamc[:, 1, d_, j:j + 1], bias=lamc[:, 1, d_, j:j + 1]),
                         reads=[KS(f"tr{d_}"), "lamc"], writes=[KS(f"aa{d_}")])
                    p.op("act", lambda a, d_=d_, j=j: a.activation(out=trb[d_], in_=trb[d_], func=AF.Exp, scale=lamc[:, 0, d_, j:j + 1], bias=lamc[:, 0, d_, j:j + 1]),
                         reads=[KS(f"tr{d_}"), "lamc"], writes=[KS(f"tr{d_}")])
                for d_ in range(2):
                    p.op("act", lambda a, d_=d_: a.activation(out=trb[d_], in_=trb[d_], func=AF.Sqrt, scale=-0.25, bias=0.25),
                         reads=[KS(f"tr{d_}")], writes=[KS(f"tr{d_}")])
                for d_ in range(2):
                    p.op("dve", lambda v, d_=d_: v.scalar_tensor_tensor(out=tib[d_], in0=tib[d_], scalar=1.0, in1=xc, op0=ALU.add, op1=ALU.mult),
                         reads=[KS(f"ti{d_}"), KS("xc")], writes=[KS(f"ti{d_}")])
                    p.op("dve", lambda v, d_=d_: v.tensor_tensor(out=tib[d_], in0=tib[d_], in1=trb[d_], op=ALU.mult),
                         reads=[KS(f"ti{d_}"), KS(f"tr{d_}")], writes=[KS(f"ti{d_}")])
                for d_ in range(2):
                    a3 = v3(aab[d_]); b3 = v3(tib[d_]); h3 = v3(hhb[d_])
                    for s in range(nseq):
                        init = 0.0
                        ik = []
                        if not (prepass or half == 0):
                            ix = 0 if d_ == 0 else T - 1
                            p.op("dve", lambda v, d_=d_, j=j, ix=ix: v.scalar_tensor_tensor(out=tib[d_][:, ix:ix + 1], in0=aab[d_][:, ix:ix + 1], scalar=h0t[:, d_, j:j + 1], in1=tib[d_][:, ix:ix + 1], op0=ALU.mult, op1=ALU.add),
                                 reads=[KS(f"aa{d_}"), KS(f"ti{d_}"), "h0t"], writes=[KS(f"ti{d_}")])
                        if d_ == 0:
                            oa, da, db = h3[:, s, :], a3[:, s, :], b3[:, s, :]
                        else:
                            oa, da, db = rev_ap(h3[:, s, :]), rev_ap(a3[:, s, :]), rev_ap(b3[:, s, :])
                        p.op("dve", lambda v, oa=oa, da=da, db=db, init=init: v.tensor_tensor_scan(out=oa, data0=da, data1=db, initial=init, op0=ALU.mult, op1=ALU.add),
                             reads=[KS(f"aa{d_}"), KS(f"ti{d_}")] + ik, writes=[KS(f"hh{d_}")])
                hf3 = v3(hhb[0]); hb3 = v3(hhb[1])
                if prepass:
                    for d_ in range(2):
                        cols = [(sc, sk) for (dd, sc, sk) in tsum if dd == d_]
                        tcol, tk = sscol()
                        p.op("dve", lambda v, cols=cols, tcol=tcol: v.tensor_tensor(out=tcol[:, 0:1], in0=cols[0][0][:, 0:1], in1=cols[1][0][:, 0:1], op=ALU.add),
                             reads=[cols[0][1], cols[1][1]], writes=[tk])
                        p.op("act", lambda a, d_=d_, j=j, tcol=tcol: a.activation(out=allg[:, chunk, 2 * d_, j:j + 1], in_=tcol[:, 0:1], func=AF.Exp, scale=lamc[:, 1, d_, j:j + 1], bias=lamc[:, 2, d_, j:j + 1]),
                             reads=[tk, "lamc"], writes=[("allg", chunk * 32 + 16 * d_ + j)])
                    p.op("dve", lambda v, j=j: v.tensor_copy(out=allg[:, chunk, 1, j:j + 1], in_=hhb[0][:, T - 1:T]), reads=[KS("hh0")], writes=[("allg", chunk * 32 + 8 + j)])
                    p.op("dve", lambda v, j=j: v.tensor_copy(out=allg[:, chunk, 3, j:j + 1], in_=hhb[1][:, 0:1]), reads=[KS("hh1")], writes=[("allg", chunk * 32 + 24 + j)])
                else:
                    if half == 0:
                        p.op("dve", lambda v, j=j: v.tensor_copy(out=stT[:, j, :, 0], in_=hf3[:, :, L - 1]), reads=[KS("hh0")], writes=["stT"])
                        p.op("dve", lambda v, j=j: v.tensor_copy(out=stT[:, j, :, 1], in_=hb3[:, :, 0]), reads=[KS("hh1")], writes=["stT"])
                    p.op("dve", lambda v: v.tensor_tensor(out=hhb[0], in0=hhb[0], in1=hhb[1], op=ALU.add), reads=[KS("hh0"), KS("hh1")], writes=[KS("hh0")])
                    p.op("dve", lambda v, j=j: v.tensor_tensor(out=yr[:, j, :], in0=hhb[0], in1=gg, op=ALU.mult), reads=[KS("hh0"), KS("gg")], writes=[("yr", j)])
            for j in range(8):
                do_chunk(j, LS[j % 2])
            import os as _os5
            if (not prepass) and half == 0 and not _os5.environ.get("KNONS"):
                pr = npair()
                for j in range(8):
                    p.op("pe", lambda t_, j=j, pr=pr: t_.matmul(pp[pr][0:8, j * 128:(j + 1) * 128], lhsT=stT[:, j].rearrange("p s d -> p (s d)"), rhs=ident_f, start=True, stop=True),
                         reads=["stT", "ident_f"], writes=[PSK(2 * pr + j // 4)])
                p.op("dve", lambda v, pr=pr: v.tensor_copy(out=sto[0:8, 0:D], in_=pp[pr][0:8, :]), reads=[PSK(2 * pr), PSK(2 * pr + 1)], writes=["sto"])
                p.dma(ns_out, sto[0:8, 0:D], reads=["sto"], writes=["ns_out"])

        def sto_ps(b, j):
            pr = pp[b // 2]
            return pr[0:8, j * 128:(j + 1) * 128]

        def stage_exchange():
            for d_ in range(2):
                Ad = allg[:, 0:4, 2 * d_, :]; Hd = allg[:, 0:4, 2 * d_ + 1, :]
                Ap = cand[:, 2 * d_, 0:4]; Hp = cand[:, 2 * d_ + 1, 0:4]
                p.op("dve", lambda v, Ad=Ad, Ap=Ap, d_=d_: v.scalar_tensor_tensor(out=Ap, in0=Ad, scalar=-1.0, in1=msk[:, d_, 0:4], op0=ALU.add, op1=ALU.mult), reads=["allg", "msk"], writes=["cand"])
                p.op("dve", lambda v, Ap=Ap: v.tensor_scalar(out=Ap, in0=Ap, scalar1=1.0, scalar2=None, op0=ALU.add), reads=["cand"], writes=["cand"])
                p.op("dve", lambda v, Hd=Hd, Hp=Hp, d_=d_: v.tensor_tensor(out=Hp, in0=Hd, in1=msk[:, d_, 0:4], op=ALU.mult), reads=["allg", "msk"], writes=["cand"])
                p.op("dve", lambda v, d_=d_: v.tensor_copy(out=h0t[:, d_, :], in_=st0[:, d_, :]), reads=["st0"], writes=["h0t"])
                for r in (range(4) if d_ == 0 else range(3, -1, -1)):
                    p.op("dve", lambda v, d_=d_, r=r, Ap=Ap: v.tensor_tensor(out=h0t[:, d_, :], in0=h0t[:, d_, :], in1=Ap[:, r, :], op=ALU.mult), reads=["h0t", "cand"], writes=["h0t"])
                    p.op("dve", lambda v, d_=d_, r=r, Hp=Hp: v.tensor_tensor(out=h0t[:, d_, :], in0=h0t[:, d_, :], in1=Hp[:, r, :], op=ALU.add), reads=["h0t", "cand"], writes=["h0t"])

        def stage_G(half):
            p.dma(gsgu, gsgu_bd, writes=["gsgu"])
            for u in range(4):
                wsl, wk = wunit((w_st[u], None))
                wu = wsl.rearrange("p (u k n) -> p u k n", u=2, k=8)
                for uu in range(2):
                    j = 2 * u + uu
                    for blk in range(2):
                        b = nb()
                        mm_st(b, wu[:, uu], lambda kc, blk=blk: hT[:, kc, blk * 512:(blk + 1) * 512], HTB(blk), wk)
                        p.op("act", lambda a, b=b, j=j, blk=blk: a.activation(out=gu[:, j, blk * 512:(blk + 1) * 512], in_=bank(b), func=AF.Gelu_apprx_tanh),
                             reads=[PSK(b)], writes=[("gu", j)])
            for i_ in range(4):
                stage_piece(w_v[i_ * 256:(i_ + 1) * 256, :].rearrange("(k p) n -> p k n", p=128), wv[:, 2 * i_:2 * i_ + 2, :], ("wv", i_))
            for t in range(NT):
                s = t % 2
                pr = npair()
                for kc in range(8):
                    for hf in range(2):
                        p.op("pe", lambda t_, kc=kc, hf=hf, t=t, pr=pr: t_.matmul(pp[pr][:, hf * 512:(hf + 1) * 512], lhsT=hT[:, kc, t * 128:(t + 1) * 128], rhs=wv[:, kc, hf * 512:(hf + 1) * 512], start=(kc == 0), stop=(kc == 7)),
                             reads=[("hT", t), "wv"], writes=[PSK(2 * pr + hf)])
                scs = []
                for hf in range(2):
                    p.op("act", lambda a, hf=hf, s=s, pr=pr: a.activation(out=gv[:, s, hf * 512:(hf + 1) * 512], in_=pp[pr][:, hf * 512:(hf + 1) * 512], func=AF.Gelu_apprx_tanh),
                         reads=[PSK(2 * pr + hf)], writes=[("gv", s)])
                sc, sk = sscol()
                p.op("act", lambda a, s=s, sc=sc: a.activation(out=junk, in_=gv[:, s, :], func=AF.Square, accum_out=sc[:, 0:1]), reads=[("gv", s)], writes=["junk", sk])
                rr, rk = rstd([sc[:, 0:1]], [sk], 128, 1.0 / D, EPS)
                p.op("dve", lambda v, s=s, rr=rr: v.scalar_tensor_tensor(out=vn[:, s, :], in0=gv[:, s, :], scalar=rr, in1=gsgu, op0=ALU.mult, op1=ALU.mult),
                     reads=[("gv", s), rk, "gsgu"], writes=[("vn", s)])
                pr2 = npair()
                ps3 = pp[pr2].rearrange("p (g q) -> p g q", g=8)
                for g in range(8):
                    kk = [PSK(2 * pr2 + g // 4)]
                    p.op("pe", lambda t_, g=g, s=s: t_.matmul(ps3[:, g, :], lhsT=vn[:, s, g * 128:(g + 1) * 128], rhs=wsp[:, g, :], start=True, stop=False),
                         reads=[("vn", s), "wsp"], writes=kk)
                    p.op("pe", lambda t_, g=g: t_.matmul(ps3[:, g, :], lhsT=ones_b[0:1, 0:128], rhs=bhi[0:1, g * 128:(g + 1) * 128], start=False, stop=False),
                         reads=["ones_b", "bhi"], writes=kk)
                    p.op("pe", lambda t_, g=g: t_.matmul(ps3[:, g, :], lhsT=ones_b[0:1, 0:128], rhs=blo[0:1, g * 128:(g + 1) * 128], start=False, stop=True),
                         reads=["ones_b", "blo"], writes=kk)
                for hf in range(2):
                    p.op("dve", lambda v, hf=hf, t=t: v.tensor_tensor(out=yg[:, 4 * hf:4 * hf + 4, t * 128:(t + 1) * 128], in0=ps3[:, 4 * hf:4 * hf + 4, :], in1=gu[:, 4 * hf:4 * hf + 4, t * 128:(t + 1) * 128], op=ALU.mult),
                         reads=[PSK(2 * pr2 + hf)] + [("gu", 4 * hf + i) for i in range(4)], writes=[("yg", t)])

        def stage_M1(half):
            YGB = lambda blk: [("yg", 4 * blk + i) for i in range(4)]
            for c in range(8):
                wsl, wk = wunit((w_st[24 + c], None))
                wg = wsl.rearrange("p (u k n) -> p u k n", u=2, k=8)
                wsl2, wk2 = wunit((w_st[32 + c], None))
                wb = wsl2.rearrange("p (u k n) -> p u k n", u=2, k=8)
                for blk in range(2):
                    s = blk
                    cs = slice(blk * 512, (blk + 1) * 512)
                    b = nb()
                    mm_st(b, wg[:, 0], lambda kc, cs=cs: hT[:, kc, cs], HTB(blk), wk)
                    p.op("act", lambda a, b=b, s=s: a.activation(out=ta[:, s, :], in_=bank(b), func=AF.Tanh, scale=0.5), reads=[PSK(b)], writes=[("ta", s)])
                    b = nb()
                    mm_st(b, wg[:, 1], lambda kc, cs=cs: hT[:, kc, cs], HTB(blk), wk)
                    p.op("act", lambda a, b=b, s=s: a.activation(out=tb[:, s, :], in_=bank(b), func=AF.Tanh, scale=0.5), reads=[PSK(b)], writes=[("tb", s)])
                    b = nb()
                    mm_st(b, wb[:, 0], lambda kc, cs=cs: yg[:, kc, cs], YGB(blk), wk2)
                    p.op("dve", lambda v, b=b, s=s: v.scalar_tensor_tensor(out=m1[:, s, :], in0=ta[:, s, :], scalar=1.0, in1=bank(b), op0=ALU.add, op1=ALU.mult),
                         reads=[("ta", s), PSK(b)], writes=[("m1", s)])
                    b = nb()
                    mm_st(b, wb[:, 1], lambda kc, cs=cs: yr[:, kc, cs], [("yr", i) for i in range(8)], wk2)
                    p.op("dve", lambda v, b=b, s=s: v.scalar_tensor_tensor(out=m2[:, s, :], in0=tb[:, s, :], scalar=1.0, in1=bank(b), op0=ALU.add, op1=ALU.mult),
                         reads=[("tb", s), PSK(b)], writes=[("m2", s)])
                    p.op("dve", lambda v, s=s, c=c, cs=cs: v.tensor_tensor(out=mg[:, c, cs], in0=m1[:, s, :], in1=m2[:, s, :], op=ALU.add),
                         reads=[("m1", s), ("m2", s)], writes=[("mg", c)])

        def stage_M2(half):
            r = half
            for i_ in range(4):
                stage_piece(w_out[i_ * 256:(i_ + 1) * 256, :].rearrange("(k p) n -> p k n", p=128), wo[:, 2 * i_:2 * i_ + 2, :], ("wo", i_))
            for t in range(NT):
                s = t % 2
                p.dma(xt[:, s, :], xin[half * T + t * 128: half * T + (t + 1) * 128, :], writes=[("xt", s)])
                pr = npair()
                for c in range(8):
                    for hf in range(2):
                        p.op("pe", lambda t_, c=c, hf=hf, t=t, pr=pr: t_.matmul(pp[pr][:, hf * 512:(hf + 1) * 512], lhsT=mg[:, c, t * 128:(t + 1) * 128], rhs=wo[:, c, hf * 512:(hf + 1) * 512], start=(c == 0), stop=(c == 7)),
                             reads=[("mg", i) for i in range(8)] + ["wo"], writes=[PSK(2 * pr + hf)])
                cols = []
                for hf in range(2):
                    sc, sk = sscol()
                    cols.append((sc, sk))
                    p.op("act", lambda a, hf=hf, pr=pr, sc=sc: a.activation(out=junk[:, 0:512], in_=pp[pr][:, hf * 512:(hf + 1) * 512], func=AF.Square, accum_out=sc[:, 0:1]),
                         reads=[PSK(2 * pr + hf)], writes=["junk", sk])
                rr, rk = rstd([cols[0][0][:, 0:1], cols[1][0][:, 0:1]], [cols[0][1], cols[1][1]], 128, 1.0 / D, 4.0 * EPS)
                for hf in range(2):
                    cs = slice(hf * 512, (hf + 1) * 512)
                    p.op("dve", lambda v, cs=cs, pr=pr, rr=rr, r=r: v.scalar_tensor_tensor(out=tmp[:, cs], in0=pp[pr][:, cs], scalar=rr, in1=gate[:, 0, r, cs], op0=ALU.mult, op1=ALU.mult),
                         reads=[PSK(2 * pr + hf), rk, ("gate", r)], writes=["tmp"])
                p.op("dve", lambda v, t=t, s=s: v.tensor_tensor(out=x1[:, t, :], in0=tmp, in1=xt[:, s, :], op=ALU.add), reads=["tmp", ("xt", s)], writes=[("x1", t)])
                norm_transpose(x1[:, t, :], ("x1", t), 128, r, 1, h2T, ("h2T", t), t * 128, xn[:, s, :], ("xn", s))

        def stage_F(half):
            r = half
            H2B = lambda blk: [("h2T", 4 * blk + i) for i in range(4)]
            for u in range(16):
                wsl, wk = wunit((w_ff1[u], None))
                w1 = wsl.rearrange("p (u k n) -> p u k n", u=2, k=8)
                for uu in range(2):
                    j = 2 * u + uu
                    for blk in range(2):
                        s = blk
                        b = nb()
                        mm_st(b, w1[:, uu], lambda kc, blk=blk: h2T[:, kc, blk * 512:(blk + 1) * 512], H2B(blk), wk)
                        p.op("act", lambda a, b=b, s=s: a.activation(out=rt[:, s, :], in_=bank(b), func=AF.Relu), reads=[PSK(b)], writes=[("rt", s)])
                        p.op("dve", lambda v, s=s, j=j, blk=blk: v.tensor_tensor(out=f1T[:, j, blk * 512:(blk + 1) * 512], in0=rt[:, s, :], in1=rt[:, s, :], op=ALU.mult),
                             reads=[("rt", s)], writes=[("f1T", j)])
            import os as _os3
            kf = int(_os3.environ.get("KF", "9"))
            if kf < 2:
                return
            sscols = {}
            for ps_ in range(2):
                for jg in range(8):
                    src = w_ff2[jg * 512:(jg + 1) * 512, ps_ * 512:(ps_ + 1) * 512].rearrange("(a p) n -> p a n", p=128)
                    wsl, wk = wunit((src, None))
                    w2 = wsl.rearrange("p (a n) -> p a n", a=4)
                    for jj in range(4):
                        j = 4 * jg + jj
                        for t in range(NT):
                            p.op("pe", lambda t_, j=j, jj=jj, t=t, w2=w2: t_.matmul(bank(t), lhsT=f1T[:, j, t * 128:(t + 1) * 128], rhs=w2[:, jj, :], start=(j == 0), stop=(j == 31)),
                                 reads=[("f1T", j), wk], writes=[PSK(t)])
                for t in range(NT):
                    cs = slice(ps_ * 512, (ps_ + 1) * 512)
                    sc, sk = sscol()
                    sscols[(t, ps_)] = (sc, sk)
                    p.op("act", lambda a, t=t, sc=sc: a.activation(out=junk[:, 0:512], in_=bank(t), func=AF.Square, accum_out=sc[:, 0:1]), reads=[PSK(t)], writes=["junk", sk])
                    p.op("dve", lambda v, t=t, cs=cs: v.tensor_copy(out=f2[:, t, cs], in_=bank(t)), reads=[PSK(t), sk], writes=[("f2", t)])
            if kf < 3:
                return
            for t in range(NT):
                (sa, ka), (sb_, kb) = sscols[(t, 0)], sscols[(t, 1)]
                rr, rk = rstd([sa[:, 0:1], sb_[:, 0:1]], [ka, kb], 128, 1.0 / D, EPS)
                p.op("dve", lambda v, t=t, rr=rr, r=r: v.scalar_tensor_tensor(out=f2[:, t, :], in0=f2[:, t, :], scalar=rr, in1=gate[:, 1, r, :], op0=ALU.mult, op1=ALU.mult),
                     reads=[("f2", t), rk, ("gate", 2 + r)], writes=[("f2", t)])
                p.op("dve", lambda v, t=t: v.tensor_tensor(out=f2[:, t, :], in0=f2[:, t, :], in1=x1[:, t, :], op=ALU.add), reads=[("f2", t), ("x1", t)], writes=[("f2", t)])
                p.dma(y_out[half * T + t * 128: half * T + (t + 1) * 128, :], f2[:, t, :], reads=[("f2", t)], writes=["y_out"])

        import os as _os
        lim = int(_os.environ.get("KLIM", "99"))
        seq = [(stage_consts, ())]
        for c_ in range(4):
            seq += [(stage_A, (1, c_)), (stage_L, (1, True, c_))]
        seq += [(stage_exchange, ())]
        for half in (0, 1):
            seq += [(stage_A, (half,)), (stage_L, (half, False)), (stage_G, (half,)), (stage_M1, (half,)), (stage_M2, (half,)), (stage_F, (half,))]
        for fn_, args_ in seq[:lim]:
            fn_(*args_)
        p.wait_all("sp", ["y_out", "ns_out"])

    dry = Prog(record_only=True)
    dry.wlist = []
    dry.overlaps = ar.ov
    emit(dry, None)
    prog = Prog()
    prog.overlaps = ar.ov
    emit(prog, dry.wlist)
    with ExitStack() as stack:
        prog.build(nc, stack)
    return nc


_NC_CACHE = {}


def _f32(a):
    return np.ascontiguousarray(np.asarray(a, dtype=np.float32))


def _colT(v):
    return np.asarray(v, np.float32).reshape(8, 128).T


def kernel(x_prompt, x_sample, state_lru, c, c_ctx, w_ada, b_ada, g_pre_mix, g_post_mix, g_pre_mlp, g_post_mlp,
           w_in, g_sgu, w_sp, b_sp, conv_w, conv_b, w_ra, b_ra, w_ri, b_ri, lam, w_br_g, w_br_r, w_out, w_ff1, w_ff2):
    f = lambda a: np.asarray(a, dtype=np.float32)
    x_prompt, x_sample, state_lru, c, c_ctx = f(x_prompt), f(x_sample), f(state_lru), f(c), f(c_ctx)
    w_in0 = f(w_in)[0]

    def st_unit(w, cb0, cb1):
        blocks = []
        for cbk in (cb0, cb1):
            mat, col = cbk
            blk = mat[:, col * 128:(col + 1) * 128].reshape(8, 128, 128).transpose(1, 0, 2)
            blocks.append(blk)
        return np.stack(blocks, axis=1).reshape(128, 2048)

    brg, brr = f(w_br_g)[0], f(w_br_r)[0]
    units = []
    for u in range(4):
        units.append(st_unit(None, (w_in0, 2 * u), (w_in0, 2 * u + 1)))
    for u in range(4):
        units.append(np.zeros((128, 2048), np.float32))
    for j in range(8):
        units.append(st_unit(None, (w_in0, 16 + j), (w_in0, 24 + j)))
    for j in range(8):
        units.append(np.zeros((128, 2048), np.float32))
    for cc in range(8):
        units.append(st_unit(None, (w_in0, 32 + cc), (w_in0, 40 + cc)))
    for cc in range(8):
        units.append(st_unit(None, (brg, cc), (brr, cc)))
    for _ in range(4):
        units.append(np.zeros((128, 2048), np.float32))
    w_st = np.ascontiguousarray(np.stack(units, 0))
    ff1 = f(w_ff1)[0]
    w_ff1_u = np.ascontiguousarray(np.stack([st_unit(None, (ff1, 2 * u), (ff1, 2 * u + 1)) for u in range(16)], 0))
    w_v = np.ascontiguousarray(w_in0[:, 1024:2048])

    bdm = np.zeros((128, 4, 8, 128), np.float32)
    for g, (wm, d_) in enumerate(((w_ra, 0), (w_ra, 1), (w_ri, 0), (w_ri, 1))):
        wmat = f(wm)[0, d_]
        for j in range(8):
            for h in range(2):
                bdm[64 * h:64 * h + 64, g, j, 64 * h:64 * h + 64] = wmat[2 * j + h]
    hbT = np.stack([_colT(f(b_ra)[0, 0]), _colT(f(b_ra)[0, 1]), _colT(f(b_ri)[0, 0]), _colT(f(b_ri)[0, 1])], 1).reshape(128, 32)
    cwT = np.stack([_colT(f(conv_w)[0, k]) for k in range(4)], 2).reshape(128, 32)
    cbT = _colT(f(conv_b)[0])
    lamT = np.stack([_colT(f(lam)[0, 0]), _colT(f(lam)[0, 1])], 1).reshape(128, 16)
    wspT = np.ascontiguousarray(f(w_sp)[0].transpose(2, 0, 1)).reshape(128, 1024)
    gvecT = np.stack([_colT(f(g_pre_mix)[0]), _colT(f(g_pre_mlp)[0])], 1).reshape(128, 16)
    gpost_b = np.ascontiguousarray(np.broadcast_to(np.concatenate([f(g_post_mix)[0], f(g_post_mlp)[0]])[None, :], (128, 2048)))
    gsgu_b = np.ascontiguousarray(np.broadcast_to(f(g_sgu)[0][None, :], (128, 1024)))
    b_ada2 = np.ascontiguousarray(np.broadcast_to(f(b_ada)[0][None, :], (2, 6144)))
    kc_sel = np.zeros((2, 256), np.float32)
    kc_sel[0, 0:128] = 1.0
    kc_sel[1, 128:256] = 1.0
    shared = dict(w_ada=_f32(f(w_ada)[0]), b_ada2=b_ada2, gvecT=_f32(gvecT), gpost_b=gpost_b, gsgu_b=gsgu_b, lamT=_f32(lamT),
                  hbT=_f32(hbT), cwT=_f32(cwT), cbT=_f32(cbT), bd_w=_f32(bdm.reshape(128, 4096)), wspT=_f32(wspT),
                  bsp=_f32(f(b_sp)[0].reshape(1, 1024)), w_st=w_st, w_v=w_v, w_out=_f32(f(w_out)[0]), w_ff1=w_ff1_u,
                  w_ff2=_f32(f(w_ff2)[0]), kc_sel=kc_sel)
    in_maps = []
    for r in range(8):
        b, q = r // 4, r % 4
        xp = x_prompt[4 * r:4 * r + 4].reshape(1024, 1024)
        xs = x_sample[b, 1024 * q:1024 * (q + 1)]
        xin = np.ascontiguousarray(np.concatenate([xp, xs], 0))
        xhalo = np.zeros((3, 1024), np.float32)
        hm = np.zeros((128, 3), np.float32)
        for i, tok in enumerate((1024 * q - 2, 1024 * q - 1, 1024 * (q + 1))):
            if 0 <= tok < 4096:
                xhalo[i] = x_sample[b, tok]
                hm[:, i] = 1.0
        cselT = np.stack([_colT(c_ctx), _colT(c[b])], 2).reshape(128, 16)
        st0T = np.stack([_colT(state_lru[b, 0, 0]), _colT(state_lru[b, 0, 1])], 1).reshape(128, 16)
        mf = np.zeros((128, 8, 8), np.float32)
        mb = np.zeros((128, 8, 8), np.float32)
        for rr in range(4):
            if rr < q:
                mf[:, rr, :] = 1.0
            if rr > q:
                mb[:, rr, :] = 1.0
        m = dict(shared)
        m.update(xs_all=np.ascontiguousarray(x_sample[b]), xin=xin, xhalo=xhalo, hmask=hm, cselT=_f32(cselT), st0T=_f32(st0T), mskf=mf.reshape(128, 64), mskb=mb.reshape(128, 64))
        in_maps.append(m)
    if "nc" not in _NC_CACHE:
        _NC_CACHE["nc"] = build_program()
    res = run_bass_kernel_spmd(_NC_CACHE["nc"], in_maps, core_ids=list(range(8)))
    y_prompt = np.zeros((32, 256, 1024), np.float32)
    y_sample = np.zeros((2, 4096, 1024), np.float32)
    new_state = np.zeros((32, 1, 2, 1024), np.float32)
    for r in range(8):
        out = res.results[r]
        yo = np.asarray(out["y_out"], np.float32)
        y_prompt[4 * r:4 * r + 4] = yo[0:1024].reshape(4, 256, 1024)
        y_sample[r // 4, 1024 * (r % 4):1024 * (r % 4 + 1)] = yo[1024:2048]
        new_state[4 * r:4 * r + 4, 0] = np.asarray(out["ns_out"], np.float32).reshape(4, 2, 1024)
    return y_prompt, y_sample, new_state
```

```python
from contextlib import ExitStack
import numpy as np
import concourse.bass as bass
import concourse.mybir as mybir
from concourse.bass_utils import run_bass_kernel_spmd

F32 = mybir.dt.float32
BF16 = mybir.dt.bfloat16
I32 = mybir.dt.int32
AF = mybir.ActivationFunctionType
ALU = mybir.AluOpType
AP = bass.AP

D = 1024
T = 1024
NT = 8
EPS = 1e-6
ENG_NAMES = ("pe", "act", "dve", "pool", "sp")


class Prog:
    def __init__(self, n_dma_sems=28, record_only=False):
        self.streams = {e: [] for e in ENG_NAMES}
        self.count = {e: 0 for e in ENG_NAMES}
        self.n_dma_sems = n_dma_sems
        self.dma_cnt = [0] * n_dma_sems
        self.dma_rr = 0
        self.cc_cnt = 0
        self.state = {}
        self.waited = {e: {} for e in ENG_NAMES}
        self.overlaps = {}
        self.record_only = record_only
        self.sw_ids = {}
        self.sw_gen = {}

    def _split(self, key):
        return key if isinstance(key, tuple) else (key, None)

    def _own_states(self, name, idx, create):
        d = self.state.setdefault(name, {})
        if idx is None:
            if create and None not in d:
                d[None] = [{}, {}]
            return list(d.values())
        if idx not in d and create:
            if None in d:
                d[idx] = [dict(d[None][0]), dict(d[None][1])]
            else:
                d[idx] = [{}, {}]
        out = []
        if idx in d:
            out.append(d[idx])
        if None in d:
            out.append(d[None])
        return out

    def _dep_states(self, key):
        name, idx = self._split(key)
        out = self._own_states(name, idx, False)
        for o in self.overlaps.get(name, ()):
            out.extend(self.state.get(o, {}).values())
        return out

    def _deps(self, reads, writes):
        need = {}

        def add(d):
            for src, seq in d.items():
                if need.get(src, -1) < seq:
                    need[src] = seq
        for k in reads:
            for st in self._dep_states(k):
                add(st[0])
        for k in writes:
            for st in self._dep_states(k):
                add(st[0])
                add(st[1])
        return need

    def _commit(self, me, reads, writes):
        src, seq = me
        for k in reads:
            name, idx = self._split(k)
            sts = self._own_states(name, idx, True)
            if idx is None:
                for st in sts:
                    st[1][src] = seq
            else:
                sts[0][1][src] = seq
        for k in writes:
            name, idx = self._split(k)
            sts = self._own_states(name, idx, True)
            tgt = sts if idx is None else sts[:1]
            for st in tgt:
                st[0] = {src: seq}
                st[1] = {}

    def _emit_waits(self, eng, need):
        w = self.waited[eng]
        for src, seq in need.items():
            if src == eng and eng == "pe":
                continue
            if w.get(src, 0) >= seq:
                continue
            w[src] = seq
            self.streams[eng].append(("wait", src, seq))

    def op(self, eng, fn, reads=(), writes=()):
        need = self._deps(reads, writes)
        self._emit_waits(eng, need)
        self.count[eng] += 1
        me = (eng, self.count[eng])
        self.streams[eng].append(("ins", fn))
        self._commit(me, reads, writes)
        return me

    def dma(self, out, in_, reads=(), writes=(), queue="sp", **kw):
        need = self._deps(reads, writes)
        if queue == "pool":
            dk = writes[0]
            if dk not in self.sw_ids:
                self.sw_ids[dk] = len(self.sw_ids)
                self.sw_gen[dk] = 0
            sid = self.sw_ids[dk]
            self.sw_gen[dk] += 1
            self._emit_waits(queue, need)
            self.count["pool"] += 1
            marker = self.count["pool"]
            self.streams[queue].append(("swdma", out, in_, sid, kw, getattr(self, "last_marker", 0)))
            self.last_marker = marker
            src = ("sw", sid, self.sw_gen[dk])
            for k in reads:
                name, idx = self._split(k)
                for st in (self._own_states(name, idx, True) if idx is None else self._own_states(name, idx, True)[:1]):
                    st[1][src] = 16
            for k in writes:
                name, idx = self._split(k)
                sts = self._own_states(name, idx, True)
                for st in (sts if idx is None else sts[:1]):
                    st[0] = {src: 16, "pool": marker}
                    st[1] = {}
            return (src, 16)
        k = self.dma_rr
        self.dma_rr = (self.dma_rr + 1) % self.n_dma_sems
        src = ("dma", k)
        if self.dma_cnt[k] > 0:
            need[src] = max(need.get(src, 0), self.dma_cnt[k])
        self._emit_waits(queue, need)
        self.dma_cnt[k] += 16
        me = (src, self.dma_cnt[k])
        self.streams[queue].append(("dma", out, in_, k, kw))
        self._commit(me, reads, writes)
        return me

    def cc(self, fn, reads=(), writes=()):
        need = self._deps(reads, writes)
        self._emit_waits("pool", need)
        self.cc_cnt += 1
        me = ("cc", self.cc_cnt)
        self.streams["pool"].append(("cc", fn))
        self._commit(me, reads, writes)
        return me

    def wait_all(self, eng, keys):
        self._emit_waits(eng, self._deps(keys, ()))

    EPOCH = 1024

    def build(self, nc, stack):
        E = self.EPOCH
        sem = {e: [stack.enter_context(nc.semaphore(f"s_{e}{i}")) for i in range(self.count[e] // E + 1)] for e in ENG_NAMES}
        dsem = [stack.enter_context(nc.semaphore(f"s_dma{k}")) for k in range(self.n_dma_sems)]
        cc_sem = stack.enter_context(nc.semaphore("s_cc"))
        block = stack.enter_context(nc.Block())
        deco = {"pe": block.tensor, "act": block.scalar, "dve": block.vector,
                "pool": block.gpsimd, "sp": block.sync}

        def semval(src, seq):
            if src == "cc":
                return cc_sem, seq
            if isinstance(src, tuple):
                return dsem[src[1]], seq
            return sem[src][(seq - 1) // E], (seq - 1) % E + 1

        for e in ENG_NAMES:
            def body(eng, items=self.streams[e], e=e):
                n = 0
                for it in items:
                    if it[0] == "wait":
                        s_, v_ = semval(it[1], it[2])
                        eng.wait_ge(s_, v_)
                    elif it[0] == "ins":
                        it[1](eng).then_inc(sem[e][n // E], 1)
                        n += 1
                    elif it[0] == "cc":
                        it[1](eng).then_inc(cc_sem)
                    else:
                        _, out, in_, k, kw = it
                        eng.dma_start(out=out, in_=in_, **kw).then_inc(dsem[k], 16)
            deco[e](body)


ALL_PH = ("C", "A", "L", "G", "M1", "M2", "F1", "F2")


class Arena:
    def __init__(self):
        self.bufs = []

    def add(self, name, nbytes, phases):
        nbytes = (nbytes + 63) // 64 * 64
        self.bufs.append([name, nbytes, set(ALL_PH if phases == "*" else phases), None])

    def pack(self):
        order = sorted(self.bufs, key=lambda b: (-len(b[2]), -b[1]))
        placed = []
        for b in order:
            cands = sorted([0] + [p[3] + p[1] for p in placed])
            for off in cands:
                ok = True
                for p in placed:
                    if p[2] & b[2] and off < p[3] + p[1] and p[3] < off + b[1]:
                        ok = False
                        break
                if ok:
                    b[3] = off
                    break
            placed.append(b)
        self.total = max(b[3] + b[1] for b in self.bufs)
        self.off = {b[0]: b[3] for b in self.bufs}
        self.size = {b[0]: b[1] for b in self.bufs}
        ov = {}
        for a in self.bufs:
            for b in self.bufs:
                if a is not b and a[3] < b[3] + b[1] and b[3] < a[3] + a[1]:
                    ov.setdefault(a[0], []).append(b[0])
        self.ov = ov


def rev_ap(ap2d):
    apl = [list(d) for d in ap2d.ap]
    n = apl[-1][1]
    st = apl[-1][0]
    apl[-1] = [-st, n]
    return AP(ap2d.tensor, ap2d.offset + st * (n - 1), apl)


def build_program():
    nc = bass.Bass("TRN2", target_bir_lowering=False)

    def din(name, shape, dt=F32):
        return nc.dram_tensor(name, list(shape), dt, kind="ExternalInput").ap()

    def dout(name, shape, dt=F32):
        return nc.dram_tensor(name, list(shape), dt, kind="ExternalOutput").ap()

    xin = din("xin", [2048, D])
    xhalo = din("xhalo", [3, D])
    xs_all = din("xs_all", [4096, D])
    hmask = din("hmask", [128, 3])
    cselT = din("cselT", [128, 16])
    st0T = din("st0T", [128, 16])
    mskf = din("mskf", [128, 64])
    mskb = din("mskb", [128, 64])
    kc_sel = din("kc_sel", [2, 256])
    w_ada = din("w_ada", [D, 6 * D])
    b_ada2 = din("b_ada2", [2, 6 * D])
    gvecT = din("gvecT", [128, 16])
    gpost_b = din("gpost_b", [128, 2 * D])
    gsgu_bd = din("gsgu_b", [128, D])
    lamT = din("lamT", [128, 16])
    hbT = din("hbT", [128, 32])
    cwT = din("cwT", [128, 32])
    cbT = din("cbT", [128, 8])
    bd_w = din("bd_w", [128, 32 * 128])
    wspT = din("wspT", [128, 8 * 128])
    bsp = din("bsp", [1, D])
    w_st = din("w_st", [44, 128, 2048])
    w_v = din("w_v", [D, D])
    w_out = din("w_out", [D, D])
    w_ff1 = din("w_ff1", [16, 128, 2048])
    w_ff2 = din("w_ff2", [4 * D, D])
    y_out = dout("y_out", [2048, D])
    ns_out = dout("ns_out", [8, D])

    ar = Arena()
    A_ = ar.add
    A_("ident_b", 256, "*"); A_("ident_f", 512, "*"); A_("sel", 1024, "*"); A_("ones_b", 256, "*")
    A_("cT", 64, "*"); A_("sT", 64, "*"); A_("modT", 384, "*"); A_("modc", 256, "*")
    A_("lamc", 256, "*"); A_("hb", 128, "*"); A_("cw", 128, "*"); A_("cb", 32, "*")
    A_("bd", 8192, "*"); A_("wsp", 2048, "*"); A_("bspf", 4096, ["C"]); A_("bhi", 2048, "*"); A_("blo", 2048, "*")
    A_("gate", 16384, "*"); A_("gsgu", 4096, ["G"]); A_("hmask", 64, "*")
    A_("h0t", 64, "*"); A_("st0", 64, "*"); A_("msk", 512, "*"); A_("summ", 128, "*"); A_("allg", 1024, "*")
    A_("cand", 1024, "*"); A_("stT", 256, "*"); A_("sto", 4096, ["L"])
    A_("rs", 16 * 32, "*"); A_("ss", 512, "*"); A_("junk", 2048, "*"); A_("gvec", 64, "*")
    A_("ws", 4 * 4096, "*"); A_("wsf", 2 * 8192, "*")
    A_("wa", 2 * 16384, ["C"]); A_("modrow", 2 * 2048, ["C"]); A_("gpost", 8192, ["C"]); A_("bada", 2 * 2048, ["C"])
    A_("xt", 2 * 4096, ["A", "M2"]); A_("xn", 2 * 2048, ["A", "M2"]); A_("xh", 4096, ["A"]); A_("xhn", 2048, ["A"])
    A_("hT", 8 * 1027 * 2, ["A", "L", "G", "M1"])
    for sfx in ("", "_b"):
        A_("xrp" + sfx, 1036 * 4, ["L"]); A_("xc" + sfx, 4096, ["L"]); A_("xcb" + sfx, 2048, ["L"]); A_("gg" + sfx, 4096, ["L"])
        for d_ in range(2):
            A_(f"tr{d_}" + sfx, 4096, ["L"]); A_(f"aa{d_}" + sfx, 4096, ["L"]); A_(f"ti{d_}" + sfx, 4096, ["L"]); A_(f"hh{d_}" + sfx, 4096, ["L"])
    A_("yr", 16384, ["L", "G", "M1"])
    A_("gu", 16384, ["G"]); A_("wv", 16384, ["G"]); A_("gv", 2 * 4096, ["G"]); A_("vn", 2 * 2048, ["G"])
    A_("yg", 16384, ["G", "M1"])
    A_("ta", 2 * 2048, ["M1"]); A_("tb", 2 * 2048, ["M1"]); A_("m1", 2 * 2048, ["M1"]); A_("m2", 2 * 2048, ["M1"])
    A_("mg", 16384, ["M1", "M2"])
    A_("wo", 16384, ["M2"]); A_("tmp", 4096, ["M2"])
    A_("x1", 32768, ["M2", "F1", "F2"]); A_("h2T", 16384, ["M2", "F1"])
    A_("f1T", 65536, ["F1", "F2"]); A_("rt", 2 * 1024, ["F1"]); A_("f2", 32768, ["F2"])
    ar.pack()
    import os as _os2
    if _os2.environ.get("KARENA"):
        for ph in ALL_PH:
            print(ph, sum(b[1] for b in ar.bufs if ph in b[2]))
        print("total", ar.total)
    assert ar.total <= 206 * 1024, ar.total
    arena = nc.alloc_sbuf_tensor("arena", [128, ar.total // 4], F32)

    def V(name, dt=F32, pattern=None, **kw):
        o = ar.off[name] // 4
        n = ar.size[name] // 4
        v = arena[:, o:o + n]
        if dt != F32:
            v = v.bitcast(dt)
        if pattern:
            v = v.rearrange(pattern, **kw)
        return v

    ident_b = V("ident_b", BF16); ident_f = V("ident_f"); sel = V("sel")[0:2, :]; ones_b = V("ones_b", BF16)
    cT = V("cT"); sT = V("sT"); modT = V("modT", F32, "p (c r) -> p c r", r=2)
    modc = V("modc", F32, "p (w r k) -> p w r k", w=4, r=2)
    lamc = V("lamc", F32, "p (w d j) -> p w d j", w=4, d=2)
    hb = V("hb", F32, "p (g j) -> p g j", g=4); cw = V("cw", F32, "p (j k) -> p j k", k=4); cb = V("cb")
    bd = V("bd", BF16, "p (g j m) -> p g j m", g=4, j=8); wsp = V("wsp", BF16, "p (g q) -> p g q", g=8)
    bspf = V("bspf"); bhi = V("bhi", BF16); blo = V("blo", BF16)
    gate = V("gate", F32, "p (w r n) -> p w r n", w=2, r=2); gsgu = V("gsgu"); hmk = V("hmask")
    h0t = V("h0t", F32, "p (d j) -> p d j", d=2); st0 = V("st0", F32, "p (d j) -> p d j", d=2)
    msk = V("msk", F32, "p (w r j) -> p w r j", w=2, r=8)
    summ = V("summ", F32, "p (q j) -> p q j", q=4); allg = V("allg", F32, "p (r q j) -> p r q j", r=8, q=4)
    cand = V("cand", F32, "p (w r j) -> p w r j", w=4, r=8)
    stT = V("stT", F32, "p (j s d) -> p j s d", j=8, s=4); sto = V("sto")
    rs = V("rs", F32, "p (s c) -> p s c", c=8); ssq = V("ss", F32, "p (s c) -> p s c", c=4)
    junk = V("junk", BF16); gvec = V("gvec", F32, "p (w k) -> p w k", w=2)
    ws = V("ws", BF16, "p (s n) -> p s n", s=4)
    wsf = V("wsf", F32, "p (s n) -> p s n", s=2)
    wa = V("wa", F32, "p (s k n) -> p s k n", s=2, k=8); modrow = V("modrow", F32, "p (s n) -> p s n", s=2)
    gpost = V("gpost", F32, "p (w n) -> p w n", w=2); bada = V("bada", F32, "p (s n) -> p s n", s=2)
    xt = V("xt", F32, "p (s n) -> p s n", s=2); xn = V("xn", BF16, "p (s n) -> p s n", s=2)
    xh = V("xh"); xhn = V("xhn", BF16)
    hT = V("hT", BF16, "p (k n) -> p k n", k=8)
    LS = []
    for sfx in ("", "_b"):
        LS.append(dict(xrp=V("xrp" + sfx), xc=V("xc" + sfx), xcb=V("xcb" + sfx, BF16), gg=V("gg" + sfx),
                       trb=[V(f"tr{d_}" + sfx) for d_ in range(2)], aab=[V(f"aa{d_}" + sfx) for d_ in range(2)],
                       tib=[V(f"ti{d_}" + sfx) for d_ in range(2)], hhb=[V(f"hh{d_}" + sfx) for d_ in range(2)], sfx=sfx))
    yr = V("yr", BF16, "p (k n) -> p k n", k=8); yg = V("yg", BF16, "p (k n) -> p k n", k=8)
    gu = V("gu", BF16, "p (k n) -> p k n", k=8); wv = V("wv", BF16, "p (k n) -> p k n", k=8)
    gv = V("gv", F32, "p (s n) -> p s n", s=2); vn = V("vn", BF16, "p (s n) -> p s n", s=2)
    ta = V("ta", F32, "p (s n) -> p s n", s=2); tb = V("tb", F32, "p (s n) -> p s n", s=2)
    m1 = V("m1", F32, "p (s n) -> p s n", s=2); m2 = V("m2", F32, "p (s n) -> p s n", s=2)
    mg = V("mg", BF16, "p (k n) -> p k n", k=8); wo = V("wo", BF16, "p (k n) -> p k n", k=8)
    tmp = V("tmp"); x1 = V("x1", F32, "p (t n) -> p t n", t=8); h2T = V("h2T", BF16, "p (k n) -> p k n", k=8)
    f1T = V("f1T", BF16, "p (k n) -> p k n", k=32); rt = V("rt", BF16, "p (s n) -> p s n", s=2)
    f2 = V("f2", F32, "p (t n) -> p t n", t=8)

    pp = [nc.alloc_psum_tensor(f"pp{i}", [128, 1024], F32) for i in range(4)]

    def bank(b):
        return pp[b // 2][:, (b % 2) * 512:(b % 2) * 512 + 512]

    def emit(p, wplan):
        st = {"bank": 0, "pair": 0, "rs": 0, "ss": 0, "wi": 0, "issued": 0, "alt": 0, "sf": 0}

        def nb():
            b = st["bank"]; st["bank"] = (b + 1) % 8
            return b

        def npair():
            b = st["pair"]; st["pair"] = (b + 1) % 4
            return b

        def alt():
            st["alt"] ^= 1
            return "act" if st["alt"] else "dve"

        PSK = lambda b: ("ps", b)

        def stage_piece(src, dst, dkey):
            k = st["sf"]; st["sf"] = (k + 1) % 2
            n = 1
            for d_ in src.shape[1:]:
                n *= d_
            sfl = wsf[:, k, 0:n]
            sv = sfl
            if len(src.shape) == 3:
                sv = sfl.rearrange("p (a n) -> p a n", a=src.shape[1])
            p.dma(sv, src, writes=[("wsf", k)])
            dfl = dst
            p.op("pool", lambda g_: g_.tensor_copy(out=dfl, in_=(sv if len(dst.shape) == 3 else sfl)), reads=[("wsf", k)], writes=[dkey])

        def wunit(desc):
            i = st["wi"]; st["wi"] += 1
            if wplan is None:
                p.wlist.append(desc)
            else:
                while st["issued"] < min(len(wplan), i + 3):
                    k = st["issued"]; st["issued"] += 1
                    src = wplan[k][0]
                    dst = ws[:, k % 4, :]
                    if len(src.shape) == 3:
                        dst = dst.rearrange("p (a n) -> p a n", a=src.shape[1])
                    stage_piece(src, dst, ("ws", k % 4))
            return ws[:, i % 4, :], ("ws", i % 4)

        def rstd(ss_list, keys, np_, scale, eps):
            s = st["rs"]; st["rs"] = (s + 1) % 16
            R = rs[0:np_, s, :]
            k = ("rs", s)
            xx, tii, yy, hh_, zz = (R[:, c:c + 1] for c in range(5))
            if len(ss_list) == 2:
                p.op("dve", lambda v: v.tensor_tensor(out=xx, in0=ss_list[0], in1=ss_list[1], op=ALU.add), reads=keys, writes=[k])
                src = xx
                rk = [k]
            else:
                src = ss_list[0]
                rk = keys
            p.op("dve", lambda v: v.tensor_scalar(out=xx, in0=src, scalar1=scale, scalar2=eps, op0=ALU.mult, op1=ALU.add), reads=rk, writes=[k])
            p.op("dve", lambda v: v.tensor_scalar(out=tii.bitcast(I32), in0=xx.bitcast(I32), scalar1=1, scalar2=None, op0=ALU.arith_shift_right), reads=[k], writes=[k])
            p.op("dve", lambda v: v.tensor_scalar(out=yy.bitcast(I32), in0=tii.bitcast(I32), scalar1=-1, scalar2=0x5f3759df, op0=ALU.mult, op1=ALU.add), reads=[k], writes=[k])
            p.op("dve", lambda v: v.tensor_scalar(out=hh_, in0=xx, scalar1=0.5, scalar2=None, op0=ALU.mult), reads=[k], writes=[k])
            for _ in range(3):
                p.op("dve", lambda v: v.scalar_tensor_tensor(out=zz, in0=yy, scalar=yy, in1=hh_, op0=ALU.mult, op1=ALU.mult), reads=[k], writes=[k])
                p.op("dve", lambda v: v.tensor_scalar(out=zz, in0=zz, scalar1=-1.0, scalar2=1.5, op0=ALU.mult, op1=ALU.add), reads=[k], writes=[k])
                p.op("dve", lambda v: v.tensor_tensor(out=yy, in0=yy, in1=zz, op=ALU.mult), reads=[k], writes=[k])
            return yy, k

        def sscol():
            s = st["ss"]; st["ss"] = (s + 1) % 32
            return ssq[:, s, :], ("ss", s)

        def stage_consts():
            sp_loads = [(cT, cselT, "cT"), (st0.rearrange("p d j -> p (d j)"), st0T, "st0"),
                        (msk[:, 0].rearrange("p r j -> p (r j)"), mskf, ("msk", 0)), (msk[:, 1].rearrange("p r j -> p (r j)"), mskb, ("msk", 1)),
                        (sel, kc_sel, "sel"), (gvec.rearrange("p w k -> p (w k)"), gvecT, "gvec"),
                        (lamc[:, 3].rearrange("p d j -> p (d j)"), lamT, "lamc"), (hb.rearrange("p g j -> p (g j)"), hbT, "hb"),
                        (cw.rearrange("p j k -> p (j k)"), cwT, "cw"), (cb[:, 0:8], cbT, "cb"), (hmk[:, 0:3], hmask, "hmask"),
                        (bspf[0:1, 0:D], bsp, "bspf"),
                        (gpost.rearrange("p w n -> p (w n)"), gpost_b, "gpost")]
            for dst, src, key in sp_loads:
                p.dma(dst, src, writes=[key])
            bdf = bd.rearrange("p g j m -> p (g j m)")
            for i_ in range(2):
                stage_piece(bd_w[:, i_ * 2048:(i_ + 1) * 2048], bdf[:, i_ * 2048:(i_ + 1) * 2048], ("bd", i_))
            stage_piece(wspT, wsp.rearrange("p g q -> p (g q)"), "wsp")
            p.op("pool", lambda g_: g_.memset(ident_f, 0.0), writes=["ident_f"])
            p.op("pool", lambda g_: g_.affine_select(out=ident_f, in_=ident_f, compare_op=ALU.not_equal, fill=1.0, base=0,
                                                      pattern=[[-1, 128]], channel_multiplier=1), reads=["ident_f"], writes=["ident_f"])
            p.op("dve", lambda v: v.tensor_copy(out=ident_b, in_=ident_f), reads=["ident_f"], writes=["ident_b"])
            p.op("pool", lambda g_: g_.memset(ones_b, 1.0), writes=["ones_b"])
            p.op("pool", lambda g_: g_.memset(allg.rearrange("p r q j -> p (r q j)"), 0.0), writes=["allg"])
            B0 = bspf[0:1, 0:D]; B1 = bspf[0:1, D:2 * D] if False else None
            p.op("dve", lambda v: v.tensor_copy(out=bhi[0:1, 0:D], in_=B0), reads=["bspf"], writes=["bhi"])
            p.op("dve", lambda v: v.tensor_tensor(out=B0, in0=B0, in1=bhi[0:1, 0:D], op=ALU.subtract), reads=["bspf", "bhi"], writes=["bspf"])
            p.op("dve", lambda v: v.tensor_copy(out=blo[0:1, 0:D], in_=B0), reads=["bspf"], writes=["blo"])
            p.op("act", lambda a: a.activation(out=sT, in_=cT, func=AF.Tanh, scale=0.5), reads=["cT"], writes=["sT"])
            p.op("dve", lambda v: v.scalar_tensor_tensor(out=sT, in0=sT, scalar=1.0, in1=cT, op0=ALU.add, op1=ALU.mult), reads=["sT", "cT"], writes=["sT"])
            p.op("dve", lambda v: v.tensor_scalar(out=sT, in0=sT, scalar1=0.5, scalar2=None, op0=ALU.mult), reads=["sT"], writes=["sT"])
            sT3 = sT.rearrange("p (k r) -> p k r", r=2)
            L3 = lamc[:, 3]
            p.op("act", lambda a: a.activation(out=L3, in_=L3, func=AF.Exp, scale=-1.0), reads=["lamc"], writes=["lamc"])
            p.op("act", lambda a: a.activation(out=L3, in_=L3, func=AF.Ln, bias=1.0, scale=1.0), reads=["lamc"], writes=["lamc"])
            p.op("dve", lambda v: v.tensor_scalar(out=lamc[:, 0], in0=L3, scalar1=-8.0, scalar2=None, op0=ALU.mult), reads=["lamc"], writes=["lamc"])
            p.op("dve", lambda v: v.tensor_scalar(out=lamc[:, 1], in0=L3, scalar1=-4.0, scalar2=None, op0=ALU.mult), reads=["lamc"], writes=["lamc"])
            p.op("dve", lambda v: v.tensor_scalar(out=lamc[:, 2], in0=L3, scalar1=-4096.0, scalar2=None, op0=ALU.mult), reads=["lamc"], writes=["lamc"])
            p.op("dve", lambda v: v.tensor_scalar(out=hb.rearrange("p g j -> p (g j)"), in0=hb.rearrange("p g j -> p (g j)"), scalar1=0.5, scalar2=None, op0=ALU.mult), reads=["hb"], writes=["hb"])
            order = [2, 3, 0, 1, 4, 5, 6, 7, 8, 9, 10, 11]
            mt_bank = nb()
            mt_ps = bank(mt_bank)[:, 0:96].rearrange("p (c r) -> p c r", r=2)
            for n_, cbk in enumerate(order):
                s = n_ % 2
                p.dma(wa[:, s], w_ada[:, cbk * 512:(cbk + 1) * 512].rearrange("(k p) n -> p k n", p=128), writes=[("wa", s)])
                p.dma(bada[0:2, s, :], b_ada2[:, cbk * 512:(cbk + 1) * 512], writes=[("bada", s)])
                b = nb()
                if b == mt_bank:
                    b = nb()
                for kc in range(8):
                    p.op("pe", lambda t_, kc=kc, s=s, b=b: t_.matmul(bank(b)[0:2, :], lhsT=sT3[:, kc, :], rhs=wa[:, s, kc, :], start=(kc == 0), stop=(kc == 7)),
                         reads=["sT", ("wa", s)], writes=[PSK(b)])
                p.op("dve", lambda v, s=s, b=b: v.tensor_tensor(out=modrow[0:2, s, :], in0=bank(b)[0:2, :], in1=bada[0:2, s, :], op=ALU.add),
                     reads=[PSK(b), ("bada", s)], writes=[("modrow", s)])
                for i in range(4):
                    ci = cbk * 4 + i
                    p.op("pe", lambda t_, s=s, i=i, ci=ci: t_.matmul(mt_ps[:, ci, :], lhsT=modrow[0:2, s, i * 128:(i + 1) * 128], rhs=ident_f[0:2, 0:2], start=True, stop=True),
                         reads=[("modrow", s), "ident_f"], writes=[PSK(mt_bank)])
                if cbk in (4, 5, 10, 11):
                    w_ = 0 if cbk < 6 else 1
                    hf = cbk % 2
                    for r in range(2):
                        b2 = nb()
                        if b2 == mt_bank:
                            b2 = nb()
                        p.op("pe", lambda t_, s=s, r=r, b2=b2: t_.matmul(bank(b2), lhsT=sel[:, r * 128:(r + 1) * 128], rhs=modrow[0:2, s, :], start=True, stop=True),
                             reads=[("modrow", s), "sel"], writes=[PSK(b2)])
                        p.op("dve", lambda v, w_=w_, r=r, hf=hf, b2=b2: v.tensor_tensor(out=gate[:, w_, r, hf * 512:(hf + 1) * 512], in0=bank(b2), in1=gpost[:, w_, hf * 512:(hf + 1) * 512], op=ALU.mult),
                             reads=[PSK(b2), "gpost"], writes=[("gate", w_ * 2 + r)])
            p.op("dve", lambda v: v.tensor_copy(out=modT, in_=mt_ps), reads=[PSK(mt_bank)], writes=["modT"])
            for r in range(2):
                for w_, (sc_i, sh_i) in enumerate(((8, 0), (32, 24))):
                    p.op("dve", lambda v, r=r, w_=w_, sc_i=sc_i: v.scalar_tensor_tensor(out=modc[:, 2 * w_, r, :], in0=modT[:, sc_i:sc_i + 8, r], scalar=1.0, in1=gvec[:, w_, :], op0=ALU.add, op1=ALU.mult),
                         reads=["modT", "gvec"], writes=["modc"])
                    p.op("dve", lambda v, r=r, w_=w_, sh_i=sh_i: v.tensor_copy(out=modc[:, 2 * w_ + 1, r, :], in_=modT[:, sh_i:sh_i + 8, r]),
                         reads=["modT"], writes=["modc"])

        def norm_transpose(src_tile, src_key, np_, r, widx, dstT, dst_key, col0, xn_tile, xn_key):
            sc, sk = sscol()
            p.op("act", lambda a: a.activation(out=junk[0:np_, :], in_=src_tile, func=AF.Square, accum_out=sc[0:np_, 0:1]),
                 reads=[src_key], writes=["junk", sk])
            rr, rk = rstd([sc[0:np_, 0:1]], [sk], np_, 1.0 / D, EPS)
            p.op("act", lambda a: a.activation(out=xn_tile, in_=src_tile, func=AF.Identity, scale=rr), reads=[src_key, rk], writes=[xn_key])
            b = nb()
            pt = bank(b).bitcast(BF16).rearrange("p (k n) -> p k n", k=8)
            for kc in range(8):
                p.op("pe", lambda t_, kc=kc: t_.transpose(pt[:, kc, 0:np_], xn_tile[:, kc * 128:(kc + 1) * 128], ident_b[0:np_, 0:np_]),
                     reads=[xn_key, "ident_b"], writes=[PSK(b)])
            use_act = alt() == "act"
            for kc in range(8):
                scl = modc[:, 2 * widx, r, kc:kc + 1]; shf = modc[:, 2 * widx + 1, r, kc:kc + 1]
                o = dstT[:, kc, col0:col0 + np_]
                if kc % 2 == 0:
                    p.op("act", lambda a, o=o, kc=kc, scl=scl, shf=shf: a.activation(out=o, in_=pt[:, kc, 0:np_], func=AF.Identity, scale=scl, bias=shf),
                         reads=[PSK(b), "modc"], writes=[dst_key])
                else:
                    p.op("dve", lambda v, o=o, kc=kc, scl=scl, shf=shf: v.tensor_scalar(out=o, in0=pt[:, kc, 0:np_], scalar1=scl, scalar2=shf, op0=ALU.mult, op1=ALU.add),
                         reads=[PSK(b), "modc"], writes=[dst_key])

        def stage_A(half, chunk=None):
            r = half
            src, base = (xin, half * T) if chunk is None else (xs_all, chunk * T)
            for t in range(NT):
                s = t % 2
                p.dma(xt[:, s, :], src[base + t * 128: base + (t + 1) * 128, :], writes=[("xt", s)])
                norm_transpose(xt[:, s, :], ("xt", s), 128, r, 0, hT, ("hT", t), t * 128, xn[:, s, :], ("xn", s))
            if half == 1:
                if chunk is None:
                    p.dma(xh[0:3, 0:D], xhalo, writes=["xh"])
                else:
                    p.op("dve", lambda v: v.memset(xh[0:3, 0:D], 0.0), writes=["xh"])
                    if chunk > 0:
                        p.dma(xh[0:2, 0:D], xs_all[base - 2:base, :], writes=["xh"])
                    if chunk < 3:
                        p.dma(xh[2:3, 0:D], xs_all[base + T:base + T + 1, :], writes=["xh"])
                norm_transpose(xh[0:3, 0:D], "xh", 3, r, 0, hT, ("hT", 8), T, xhn[0:3, 0:D], "xhn")

        HTB = lambda blk: [("hT", 4 * blk + i) for i in range(4)]

        def mm_st(b, wview, rhs_fn, rkeys, wkey, ncols=512):
            for kc in range(8):
                p.op("pe", lambda t_, kc=kc: t_.matmul(bank(b)[:, 0:ncols], lhsT=wview[:, kc, :], rhs=rhs_fn(kc), start=(kc == 0), stop=(kc == 7)),
                     reads=[wkey] + rkeys, writes=[PSK(b)])

        def stage_L(half, prepass, chunk=None):
            nseq, L = (4, 256) if half == 0 else (1, 1024)
            XW = L + 3
            v3 = lambda tl: tl.rearrange("p (s n) -> p s n", s=nseq)
            for S_ in LS:
                p.op("dve", lambda v, S_=S_: v.memset(S_["xrp"], 0.0), writes=["xrp" + S_["sfx"]])
            dirs = [d for d in (0, 1) if not (prepass and ((chunk == 0 and d == 1) or (chunk == 3 and d == 0)))]

            def do_chunk(j, S_):
                xrp, xc, xcb, gg = S_["xrp"], S_["xc"], S_["xcb"], S_["gg"]
                trb, aab, tib, hhb = S_["trb"], S_["aab"], S_["tib"], S_["hhb"]
                KS = lambda n: n + S_["sfx"]
                xrp3 = xrp[:, 0:nseq * XW].rearrange("p (s n) -> p s n", s=nseq)
                wsl, wk = wunit((w_st[8 + j], None))
                wx = wsl.rearrange("p (u k n) -> p u k n", u=2, k=8)
                for blk in range(2):
                    b = nb()
                    mm_st(b, wx[:, 0], lambda kc, blk=blk: hT[:, kc, blk * 512:(blk + 1) * 512], HTB(blk), wk)
                    if half == 0:
                        dst = xrp3[:, 2 * blk:2 * blk + 2, 2:2 + L]
                        src = bank(b).rearrange("p (s n) -> p s n", s=2)
                    else:
                        dst = xrp[:, 2 + blk * 512:2 + (blk + 1) * 512]
                        src = bank(b)
                    p.op("dve", lambda v, dst=dst, src=src: v.tensor_copy(out=dst, in_=src), reads=[PSK(b)], writes=[KS("xrp")])
                if half == 1:
                    b = nb()
                    mm_st(b, wx[:, 0], lambda kc: hT[:, kc, T:T + 3], [("hT", 8)], wk, ncols=3)
                    if prepass:
                        if chunk > 0:
                            p.op("dve", lambda v, b=b: v.tensor_copy(out=xrp[:, 0:2], in_=bank(b)[:, 0:2]), reads=[PSK(b)], writes=[KS("xrp")])
                        if chunk < 3:
                            p.op("dve", lambda v, b=b: v.tensor_copy(out=xrp[:, 2 + T:3 + T], in_=bank(b)[:, 2:3]), reads=[PSK(b)], writes=[KS("xrp")])
                    else:
                        p.op("dve", lambda v, b=b: v.tensor_tensor(out=xrp[:, 0:2], in0=bank(b)[:, 0:2], in1=hmk[:, 0:2], op=ALU.mult), reads=[PSK(b), "hmask"], writes=[KS("xrp")])
                        p.op("dve", lambda v, b=b: v.tensor_tensor(out=xrp[:, 2 + T:3 + T], in0=bank(b)[:, 2:3], in1=hmk[:, 2:3], op=ALU.mult), reads=[PSK(b), "hmask"], writes=[KS("xrp")])
                if not prepass:
                    for blk in range(2):
                        b = nb()
                        mm_st(b, wx[:, 1], lambda kc, blk=blk: hT[:, kc, blk * 512:(blk + 1) * 512], HTB(blk), wk)
                        p.op("act", lambda a, b=b, blk=blk: a.activation(out=gg[:, blk * 512:(blk + 1) * 512], in_=bank(b), func=AF.Gelu_apprx_tanh),
                             reads=[PSK(b)], writes=[KS("gg")])
                xc3 = v3(xc)
                p.op("dve", lambda v, j=j: v.tensor_scalar(out=xc3, in0=xrp3[:, :, 0:L], scalar1=cw[:, j, 0:1], scalar2=cb[:, j:j + 1], op0=ALU.mult, op1=ALU.add),
                     reads=[KS("xrp"), "cw", "cb"], writes=[KS("xc")])
                for k in range(1, 4):
                    p.op("dve", lambda v, j=j, k=k: v.scalar_tensor_tensor(out=xc3, in0=xrp3[:, :, k:k + L], scalar=cw[:, j, k:k + 1], in1=xc3, op0=ALU.mult, op1=ALU.add),
                         reads=[KS("xrp"), "cw", KS("xc")], writes=[KS("xc")])
                p.op("act", lambda a: a.copy(out=xcb, in_=xc), reads=[KS("xc")], writes=[KS("xcb")])
                tsum = []
                for g in range(4):
                    d_ = g % 2
                    if d_ not in dirs:
                        continue
                    dstb = trb[d_] if g < 2 else tib[d_]
                    dkey = KS(f"tr{d_}") if g < 2 else KS(f"ti{d_}")
                    for blk in range(2):
                        b = nb()
                        p.op("pe", lambda t_, g=g, j=j, blk=blk, b=b: t_.matmul(bank(b), lhsT=bd[:, g, j, :], rhs=xcb[:, blk * 512:(blk + 1) * 512], start=True, stop=True),
                             reads=["bd", KS("xcb")], writes=[PSK(b)])
                        if prepass and g < 2:
                            sc, sk = sscol()
                            tsum.append((d_, sc, sk))
                            p.op("act", lambda a, g=g, j=j, blk=blk, b=b, dstb=dstb, sc=sc: a.activation(out=dstb[:, blk * 512:(blk + 1) * 512], in_=bank(b), func=AF.Tanh, scale=0.5, bias=hb[:, g, j:j + 1], accum_out=sc[:, 0:1]),
                                 reads=[PSK(b), "hb"], writes=[dkey, sk])
                        else:
                            p.op("act", lambda a, g=g, j=j, blk=blk, b=b, dstb=dstb: a.activation(out=dstb[:, blk * 512:(blk + 1) * 512], in_=bank(b), func=AF.Tanh, scale=0.5, bias=hb[:, g, j:j + 1]),
                                 reads=[PSK(b), "hb"], writes=[dkey])
                for d_ in dirs:
                    p.op("act", lambda a, d_=d_, j=j: a.activation(out=aab[d_], in_=trb[d_], func=AF.Exp, scale=lamc[:, 1, d_, j:j + 1], bias=lamc[:, 1, d_, j:j + 1]),
                         reads=[KS(f"tr{d_}"), "lamc"], writes=[KS(f"aa{d_}")])
                    p.op("act", lambda a, d_=d_, j=j: a.activation(out=trb[d_], in_=trb[d_], func=AF.Exp, scale=lamc[:, 0, d_, j:j + 1], bias=lamc[:, 0, d_, j:j + 1]),
                         reads=[KS(f"tr{d_}"), "lamc"], writes=[KS(f"tr{d_}")])
                for d_ in dirs:
                    p.op("act", lambda a, d_=d_: a.activation(out=trb[d_], in_=trb[d_], func=AF.Sqrt, scale=-0.25, bias=0.25),
                         reads=[KS(f"tr{d_}")], writes=[KS(f"tr{d_}")])
                for d_ in dirs:
                    p.op("dve", lambda v, d_=d_: v.scalar_tensor_tensor(out=tib[d_], in0=tib[d_], scalar=1.0, in1=xc, op0=ALU.add, op1=ALU.mult),
                         reads=[KS(f"ti{d_}"), KS("xc")], writes=[KS(f"ti{d_}")])
                    p.op("dve", lambda v, d_=d_: v.tensor_tensor(out=tib[d_], in0=tib[d_], in1=trb[d_], op=ALU.mult),
                         reads=[KS(f"ti{d_}"), KS(f"tr{d_}")], writes=[KS(f"ti{d_}")])
                for d_ in dirs:
                    a3 = v3(aab[d_]); b3 = v3(tib[d_]); h3 = v3(hhb[d_])
                    for s in range(nseq):
                        init = 0.0
                        ik = []
                        if not (prepass or half == 0):
                            ix = 0 if d_ == 0 else T - 1
                            p.op("dve", lambda v, d_=d_, j=j, ix=ix: v.scalar_tensor_tensor(out=tib[d_][:, ix:ix + 1], in0=aab[d_][:, ix:ix + 1], scalar=h0t[:, d_, j:j + 1], in1=tib[d_][:, ix:ix + 1], op0=ALU.mult, op1=ALU.add),
                                 reads=[KS(f"aa{d_}"), KS(f"ti{d_}"), "h0t"], writes=[KS(f"ti{d_}")])
                        if d_ == 0:
                            oa, da, db = h3[:, s, :], a3[:, s, :], b3[:, s, :]
                        else:
                            oa, da, db = rev_ap(h3[:, s, :]), rev_ap(a3[:, s, :]), rev_ap(b3[:, s, :])
                        p.op("dve", lambda v, oa=oa, da=da, db=db, init=init: v.tensor_tensor_scan(out=oa, data0=da, data1=db, initial=init, op0=ALU.mult, op1=ALU.add),
                             reads=[KS(f"aa{d_}"), KS(f"ti{d_}")] + ik, writes=[KS(f"hh{d_}")])
                hf3 = v3(hhb[0]); hb3 = v3(hhb[1])
                if prepass:
                    for d_ in dirs:
                        cols = [(sc, sk) for (dd, sc, sk) in tsum if dd == d_]
                        tcol, tk = sscol()
                        p.op("dve", lambda v, cols=cols, tcol=tcol: v.tensor_tensor(out=tcol[:, 0:1], in0=cols[0][0][:, 0:1], in1=cols[1][0][:, 0:1], op=ALU.add),
                             reads=[cols[0][1], cols[1][1]], writes=[tk])
                        p.op("act", lambda a, d_=d_, j=j, tcol=tcol: a.activation(out=allg[:, chunk, 2 * d_, j:j + 1], in_=tcol[:, 0:1], func=AF.Exp, scale=lamc[:, 1, d_, j:j + 1], bias=lamc[:, 2, d_, j:j + 1]),
                             reads=[tk, "lamc"], writes=[("allg", chunk * 32 + 16 * d_ + j)])
                    if 0 in dirs:
                        p.op("dve", lambda v, j=j: v.tensor_copy(out=allg[:, chunk, 1, j:j + 1], in_=hhb[0][:, T - 1:T]), reads=[KS("hh0")], writes=[("allg", chunk * 32 + 8 + j)])
                    if 1 in dirs:
                        p.op("dve", lambda v, j=j: v.tensor_copy(out=allg[:, chunk, 3, j:j + 1], in_=hhb[1][:, 0:1]), reads=[KS("hh1")], writes=[("allg", chunk * 32 + 24 + j)])
                else:
                    if half == 0:
                        p.op("dve", lambda v, j=j: v.tensor_copy(out=stT[:, j, :, 0], in_=hf3[:, :, L - 1]), reads=[KS("hh0")], writes=["stT"])
                        p.op("dve", lambda v, j=j: v.tensor_copy(out=stT[:, j, :, 1], in_=hb3[:, :, 0]), reads=[KS("hh1")], writes=["stT"])
                    p.op("dve", lambda v: v.tensor_tensor(out=hhb[0], in0=hhb[0], in1=hhb[1], op=ALU.add), reads=[KS("hh0"), KS("hh1")], writes=[KS("hh0")])
                    p.op("dve", lambda v, j=j: v.tensor_tensor(out=yr[:, j, :], in0=hhb[0], in1=gg, op=ALU.mult), reads=[KS("hh0"), KS("gg")], writes=[("yr", j)])
            for j in range(8):
                do_chunk(j, LS[j % 2])
            import os as _os5
            if (not prepass) and half == 0 and not _os5.environ.get("KNONS"):
                pr = npair()
                for j in range(8):
                    p.op("pe", lambda t_, j=j, pr=pr: t_.matmul(pp[pr][0:8, j * 128:(j + 1) * 128], lhsT=stT[:, j].rearrange("p s d -> p (s d)"), rhs=ident_f, start=True, stop=True),
                         reads=["stT", "ident_f"], writes=[PSK(2 * pr + j // 4)])
                p.op("dve", lambda v, pr=pr: v.tensor_copy(out=sto[0:8, 0:D], in_=pp[pr][0:8, :]), reads=[PSK(2 * pr), PSK(2 * pr + 1)], writes=["sto"])
                p.dma(ns_out, sto[0:8, 0:D], reads=["sto"], writes=["ns_out"])

        def sto_ps(b, j):
            pr = pp[b // 2]
            return pr[0:8, j * 128:(j + 1) * 128]

        def stage_exchange():
            for d_ in range(2):
                Ad = allg[:, 0:4, 2 * d_, :]; Hd = allg[:, 0:4, 2 * d_ + 1, :]
                Ap = cand[:, 2 * d_, 0:4]; Hp = cand[:, 2 * d_ + 1, 0:4]
                p.op("dve", lambda v, Ad=Ad, Ap=Ap, d_=d_: v.scalar_tensor_tensor(out=Ap, in0=Ad, scalar=-1.0, in1=msk[:, d_, 0:4], op0=ALU.add, op1=ALU.mult), reads=["allg", "msk"], writes=["cand"])
                p.op("dve", lambda v, Ap=Ap: v.tensor_scalar(out=Ap, in0=Ap, scalar1=1.0, scalar2=None, op0=ALU.add), reads=["cand"], writes=["cand"])
                p.op("dve", lambda v, Hd=Hd, Hp=Hp, d_=d_: v.tensor_tensor(out=Hp, in0=Hd, in1=msk[:, d_, 0:4], op=ALU.mult), reads=["allg", "msk"], writes=["cand"])
                p.op("dve", lambda v, d_=d_: v.tensor_copy(out=h0t[:, d_, :], in_=st0[:, d_, :]), reads=["st0"], writes=["h0t"])
                for r in (range(4) if d_ == 0 else range(3, -1, -1)):
                    p.op("dve", lambda v, d_=d_, r=r, Ap=Ap: v.tensor_tensor(out=h0t[:, d_, :], in0=h0t[:, d_, :], in1=Ap[:, r, :], op=ALU.mult), reads=["h0t", "cand"], writes=["h0t"])
                    p.op("dve", lambda v, d_=d_, r=r, Hp=Hp: v.tensor_tensor(out=h0t[:, d_, :], in0=h0t[:, d_, :], in1=Hp[:, r, :], op=ALU.add), reads=["h0t", "cand"], writes=["h0t"])

        def stage_G(half):
            p.dma(gsgu, gsgu_bd, writes=["gsgu"])
            for u in range(4):
                wsl, wk = wunit((w_st[u], None))
                wu = wsl.rearrange("p (u k n) -> p u k n", u=2, k=8)
                for uu in range(2):
                    j = 2 * u + uu
                    for blk in range(2):
                        b = nb()
                        mm_st(b, wu[:, uu], lambda kc, blk=blk: hT[:, kc, blk * 512:(blk + 1) * 512], HTB(blk), wk)
                        p.op("act", lambda a, b=b, j=j, blk=blk: a.activation(out=gu[:, j, blk * 512:(blk + 1) * 512], in_=bank(b), func=AF.Gelu_apprx_tanh),
                             reads=[PSK(b)], writes=[("gu", j)])
            for i_ in range(4):
                stage_piece(w_v[i_ * 256:(i_ + 1) * 256, :].rearrange("(k p) n -> p k n", p=128), wv[:, 2 * i_:2 * i_ + 2, :], ("wv", i_))
            for t in range(NT):
                s = t % 2
                pr = npair()
                for kc in range(8):
                    for hf in range(2):
                        p.op("pe", lambda t_, kc=kc, hf=hf, t=t, pr=pr: t_.matmul(pp[pr][:, hf * 512:(hf + 1) * 512], lhsT=hT[:, kc, t * 128:(t + 1) * 128], rhs=wv[:, kc, hf * 512:(hf + 1) * 512], start=(kc == 0), stop=(kc == 7)),
                             reads=[("hT", t), "wv"], writes=[PSK(2 * pr + hf)])
                scs = []
                for hf in range(2):
                    p.op("act", lambda a, hf=hf, s=s, pr=pr: a.activation(out=gv[:, s, hf * 512:(hf + 1) * 512], in_=pp[pr][:, hf * 512:(hf + 1) * 512], func=AF.Gelu_apprx_tanh),
                         reads=[PSK(2 * pr + hf)], writes=[("gv", s)])
                sc, sk = sscol()
                p.op("act", lambda a, s=s, sc=sc: a.activation(out=junk, in_=gv[:, s, :], func=AF.Square, accum_out=sc[:, 0:1]), reads=[("gv", s)], writes=["junk", sk])
                rr, rk = rstd([sc[:, 0:1]], [sk], 128, 1.0 / D, EPS)
                p.op("dve", lambda v, s=s, rr=rr: v.scalar_tensor_tensor(out=vn[:, s, :], in0=gv[:, s, :], scalar=rr, in1=gsgu, op0=ALU.mult, op1=ALU.mult),
                     reads=[("gv", s), rk, "gsgu"], writes=[("vn", s)])
                pr2 = npair()
                ps3 = pp[pr2].rearrange("p (g q) -> p g q", g=8)
                for g in range(8):
                    kk = [PSK(2 * pr2 + g // 4)]
                    p.op("pe", lambda t_, g=g, s=s: t_.matmul(ps3[:, g, :], lhsT=vn[:, s, g * 128:(g + 1) * 128], rhs=wsp[:, g, :], start=True, stop=False),
                         reads=[("vn", s), "wsp"], writes=kk)
                    p.op("pe", lambda t_, g=g: t_.matmul(ps3[:, g, :], lhsT=ones_b[0:1, 0:128], rhs=bhi[0:1, g * 128:(g + 1) * 128], start=False, stop=False),
                         reads=["ones_b", "bhi"], writes=kk)
                    p.op("pe", lambda t_, g=g: t_.matmul(ps3[:, g, :], lhsT=ones_b[0:1, 0:128], rhs=blo[0:1, g * 128:(g + 1) * 128], start=False, stop=True),
                         reads=["ones_b", "blo"], writes=kk)
                for hf in range(2):
                    p.op("dve", lambda v, hf=hf, t=t: v.tensor_tensor(out=yg[:, 4 * hf:4 * hf + 4, t * 128:(t + 1) * 128], in0=ps3[:, 4 * hf:4 * hf + 4, :], in1=gu[:, 4 * hf:4 * hf + 4, t * 128:(t + 1) * 128], op=ALU.mult),
                         reads=[PSK(2 * pr2 + hf)] + [("gu", 4 * hf + i) for i in range(4)], writes=[("yg", t)])

        def stage_M1(half):
            YGB = lambda blk: [("yg", 4 * blk + i) for i in range(4)]
            for c in range(8):
                wsl, wk = wunit((w_st[24 + c], None))
                wg = wsl.rearrange("p (u k n) -> p u k n", u=2, k=8)
                wsl2, wk2 = wunit((w_st[32 + c], None))
                wb = wsl2.rearrange("p (u k n) -> p u k n", u=2, k=8)
                for blk in range(2):
                    s = blk
                    cs = slice(blk * 512, (blk + 1) * 512)
                    b = nb()
                    mm_st(b, wg[:, 0], lambda kc, cs=cs: hT[:, kc, cs], HTB(blk), wk)
                    p.op("act", lambda a, b=b, s=s: a.activation(out=ta[:, s, :], in_=bank(b), func=AF.Tanh, scale=0.5), reads=[PSK(b)], writes=[("ta", s)])
                    b = nb()
                    mm_st(b, wg[:, 1], lambda kc, cs=cs: hT[:, kc, cs], HTB(blk), wk)
                    p.op("act", lambda a, b=b, s=s: a.activation(out=tb[:, s, :], in_=bank(b), func=AF.Tanh, scale=0.5), reads=[PSK(b)], writes=[("tb", s)])
                    b = nb()
                    mm_st(b, wb[:, 0], lambda kc, cs=cs: yg[:, kc, cs], YGB(blk), wk2)
                    p.op("dve", lambda v, b=b, s=s: v.scalar_tensor_tensor(out=m1[:, s, :], in0=ta[:, s, :], scalar=1.0, in1=bank(b), op0=ALU.add, op1=ALU.mult),
                         reads=[("ta", s), PSK(b)], writes=[("m1", s)])
                    b = nb()
                    mm_st(b, wb[:, 1], lambda kc, cs=cs: yr[:, kc, cs], [("yr", i) for i in range(8)], wk2)
                    p.op("dve", lambda v, b=b, s=s: v.scalar_tensor_tensor(out=m2[:, s, :], in0=tb[:, s, :], scalar=1.0, in1=bank(b), op0=ALU.add, op1=ALU.mult),
                         reads=[("tb", s), PSK(b)], writes=[("m2", s)])
                    p.op("dve", lambda v, s=s, c=c, cs=cs: v.tensor_tensor(out=mg[:, c, cs], in0=m1[:, s, :], in1=m2[:, s, :], op=ALU.add),
                         reads=[("m1", s), ("m2", s)], writes=[("mg", c)])

        def stage_M2(half):
            r = half
            for i_ in range(4):
                stage_piece(w_out[i_ * 256:(i_ + 1) * 256, :].rearrange("(k p) n -> p k n", p=128), wo[:, 2 * i_:2 * i_ + 2, :], ("wo", i_))
            for t in range(NT):
                s = t % 2
                p.dma(xt[:, s, :], xin[half * T + t * 128: half * T + (t + 1) * 128, :], writes=[("xt", s)])
                pr = npair()
                for c in range(8):
                    for hf in range(2):
                        p.op("pe", lambda t_, c=c, hf=hf, t=t, pr=pr: t_.matmul(pp[pr][:, hf * 512:(hf + 1) * 512], lhsT=mg[:, c, t * 128:(t + 1) * 128], rhs=wo[:, c, hf * 512:(hf + 1) * 512], start=(c == 0), stop=(c == 7)),
                             reads=[("mg", i) for i in range(8)] + ["wo"], writes=[PSK(2 * pr + hf)])
                cols = []
                for hf in range(2):
                    sc, sk = sscol()
                    cols.append((sc, sk))
                    p.op("act", lambda a, hf=hf, pr=pr, sc=sc: a.activation(out=junk[:, 0:512], in_=pp[pr][:, hf * 512:(hf + 1) * 512], func=AF.Square, accum_out=sc[:, 0:1]),
                         reads=[PSK(2 * pr + hf)], writes=["junk", sk])
                rr, rk = rstd([cols[0][0][:, 0:1], cols[1][0][:, 0:1]], [cols[0][1], cols[1][1]], 128, 1.0 / D, 4.0 * EPS)
                for hf in range(2):
                    cs = slice(hf * 512, (hf + 1) * 512)
                    p.op("dve", lambda v, cs=cs, pr=pr, rr=rr, r=r: v.scalar_tensor_tensor(out=tmp[:, cs], in0=pp[pr][:, cs], scalar=rr, in1=gate[:, 0, r, cs], op0=ALU.mult, op1=ALU.mult),
                         reads=[PSK(2 * pr + hf), rk, ("gate", r)], writes=["tmp"])
                p.op("dve", lambda v, t=t, s=s: v.tensor_tensor(out=x1[:, t, :], in0=tmp, in1=xt[:, s, :], op=ALU.add), reads=["tmp", ("xt", s)], writes=[("x1", t)])
                norm_transpose(x1[:, t, :], ("x1", t), 128, r, 1, h2T, ("h2T", t), t * 128, xn[:, s, :], ("xn", s))

        def stage_F(half):
            r = half
            H2B = lambda blk: [("h2T", 4 * blk + i) for i in range(4)]
            for u in range(16):
                wsl, wk = wunit((w_ff1[u], None))
                w1 = wsl.rearrange("p (u k n) -> p u k n", u=2, k=8)
                for uu in range(2):
                    j = 2 * u + uu
                    for blk in range(2):
                        s = blk
                        b = nb()
                        mm_st(b, w1[:, uu], lambda kc, blk=blk: h2T[:, kc, blk * 512:(blk + 1) * 512], H2B(blk), wk)
                        p.op("act", lambda a, b=b, s=s: a.activation(out=rt[:, s, :], in_=bank(b), func=AF.Relu), reads=[PSK(b)], writes=[("rt", s)])
                        p.op("dve", lambda v, s=s, j=j, blk=blk: v.tensor_tensor(out=f1T[:, j, blk * 512:(blk + 1) * 512], in0=rt[:, s, :], in1=rt[:, s, :], op=ALU.mult),
                             reads=[("rt", s)], writes=[("f1T", j)])
            import os as _os3
            kf = int(_os3.environ.get("KF", "9"))
            if kf < 2:
                return
            sscols = {}
            for ps_ in range(2):
                for jg in range(8):
                    src = w_ff2[jg * 512:(jg + 1) * 512, ps_ * 512:(ps_ + 1) * 512].rearrange("(a p) n -> p a n", p=128)
                    wsl, wk = wunit((src, None))
                    w2 = wsl.rearrange("p (a n) -> p a n", a=4)
                    for jj in range(4):
                        j = 4 * jg + jj
                        for t in range(NT):
                            p.op("pe", lambda t_, j=j, jj=jj, t=t, w2=w2: t_.matmul(bank(t), lhsT=f1T[:, j, t * 128:(t + 1) * 128], rhs=w2[:, jj, :], start=(j == 0), stop=(j == 31)),
                                 reads=[("f1T", j), wk], writes=[PSK(t)])
                for t in range(NT):
                    cs = slice(ps_ * 512, (ps_ + 1) * 512)
                    sc, sk = sscol()
                    sscols[(t, ps_)] = (sc, sk)
                    p.op("act", lambda a, t=t, sc=sc: a.activation(out=junk[:, 0:512], in_=bank(t), func=AF.Square, accum_out=sc[:, 0:1]), reads=[PSK(t)], writes=["junk", sk])
                    p.op("dve", lambda v, t=t, cs=cs: v.tensor_copy(out=f2[:, t, cs], in_=bank(t)), reads=[PSK(t), sk], writes=[("f2", t)])
            if kf < 3:
                return
            for t in range(NT):
                (sa, ka), (sb_, kb) = sscols[(t, 0)], sscols[(t, 1)]
                rr, rk = rstd([sa[:, 0:1], sb_[:, 0:1]], [ka, kb], 128, 1.0 / D, EPS)
                p.op("dve", lambda v, t=t, rr=rr, r=r: v.scalar_tensor_tensor(out=f2[:, t, :], in0=f2[:, t, :], scalar=rr, in1=gate[:, 1, r, :], op0=ALU.mult, op1=ALU.mult),
                     reads=[("f2", t), rk, ("gate", 2 + r)], writes=[("f2", t)])
                p.op("dve", lambda v, t=t: v.tensor_tensor(out=f2[:, t, :], in0=f2[:, t, :], in1=x1[:, t, :], op=ALU.add), reads=[("f2", t), ("x1", t)], writes=[("f2", t)])
                p.dma(y_out[half * T + t * 128: half * T + (t + 1) * 128, :], f2[:, t, :], reads=[("f2", t)], writes=["y_out"])

        import os as _os
        lim = int(_os.environ.get("KLIM", "99"))
        seq = [(stage_consts, ())]
        for c_ in range(4):
            seq += [(stage_A, (1, c_)), (stage_L, (1, True, c_))]
        seq += [(stage_exchange, ())]
        for half in (0, 1):
            seq += [(stage_A, (half,)), (stage_L, (half, False)), (stage_G, (half,)), (stage_M1, (half,)), (stage_M2, (half,)), (stage_F, (half,))]
        for fn_, args_ in seq[:lim]:
            fn_(*args_)
        p.wait_all("sp", ["y_out", "ns_out"])

    dry = Prog(record_only=True)
    dry.wlist = []
    dry.overlaps = ar.ov
    emit(dry, None)
    prog = Prog()
    prog.overlaps = ar.ov
    emit(prog, dry.wlist)
    with ExitStack() as stack:
        prog.build(nc, stack)
    return nc


_NC_CACHE = {}


def _f32(a):
    return np.ascontiguousarray(np.asarray(a, dtype=np.float32))


def _colT(v):
    return np.asarray(v, np.float32).reshape(8, 128).T


def kernel(x_prompt, x_sample, state_lru, c, c_ctx, w_ada, b_ada, g_pre_mix, g_post_mix, g_pre_mlp, g_post_mlp,
           w_in, g_sgu, w_sp, b_sp, conv_w, conv_b, w_ra, b_ra, w_ri, b_ri, lam, w_br_g, w_br_r, w_out, w_ff1, w_ff2):
    f = lambda a: np.asarray(a, dtype=np.float32)
    x_prompt, x_sample, state_lru, c, c_ctx = f(x_prompt), f(x_sample), f(state_lru), f(c), f(c_ctx)
    w_in0 = f(w_in)[0]

    def st_unit(w, cb0, cb1):
        blocks = []
        for cbk in (cb0, cb1):
            mat, col = cbk
            blk = mat[:, col * 128:(col + 1) * 128].reshape(8, 128, 128).transpose(1, 0, 2)
            blocks.append(blk)
        return np.stack(blocks, axis=1).reshape(128, 2048)

    brg, brr = f(w_br_g)[0], f(w_br_r)[0]
    units = []
    for u in range(4):
        units.append(st_unit(None, (w_in0, 2 * u), (w_in0, 2 * u + 1)))
    for u in range(4):
        units.append(np.zeros((128, 2048), np.float32))
    for j in range(8):
        units.append(st_unit(None, (w_in0, 16 + j), (w_in0, 24 + j)))
    for j in range(8):
        units.append(np.zeros((128, 2048), np.float32))
    for cc in range(8):
        units.append(st_unit(None, (w_in0, 32 + cc), (w_in0, 40 + cc)))
    for cc in range(8):
        units.append(st_unit(None, (brg, cc), (brr, cc)))
    for _ in range(4):
        units.append(np.zeros((128, 2048), np.float32))
    w_st = np.ascontiguousarray(np.stack(units, 0))
    ff1 = f(w_ff1)[0]
    w_ff1_u = np.ascontiguousarray(np.stack([st_unit(None, (ff1, 2 * u), (ff1, 2 * u + 1)) for u in range(16)], 0))
    w_v = np.ascontiguousarray(w_in0[:, 1024:2048])

    bdm = np.zeros((128, 4, 8, 128), np.float32)
    for g, (wm, d_) in enumerate(((w_ra, 0), (w_ra, 1), (w_ri, 0), (w_ri, 1))):
        wmat = f(wm)[0, d_]
        for j in range(8):
            for h in range(2):
                bdm[64 * h:64 * h + 64, g, j, 64 * h:64 * h + 64] = wmat[2 * j + h]
    hbT = np.stack([_colT(f(b_ra)[0, 0]), _colT(f(b_ra)[0, 1]), _colT(f(b_ri)[0, 0]), _colT(f(b_ri)[0, 1])], 1).reshape(128, 32)
    cwT = np.stack([_colT(f(conv_w)[0, k]) for k in range(4)], 2).reshape(128, 32)
    cbT = _colT(f(conv_b)[0])
    lamT = np.stack([_colT(f(lam)[0, 0]), _colT(f(lam)[0, 1])], 1).reshape(128, 16)
    wspT = np.ascontiguousarray(f(w_sp)[0].transpose(2, 0, 1)).reshape(128, 1024)
    gvecT = np.stack([_colT(f(g_pre_mix)[0]), _colT(f(g_pre_mlp)[0])], 1).reshape(128, 16)
    gpost_b = np.ascontiguousarray(np.broadcast_to(np.concatenate([f(g_post_mix)[0], f(g_post_mlp)[0]])[None, :], (128, 2048)))
    gsgu_b = np.ascontiguousarray(np.broadcast_to(f(g_sgu)[0][None, :], (128, 1024)))
    b_ada2 = np.ascontiguousarray(np.broadcast_to(f(b_ada)[0][None, :], (2, 6144)))
    kc_sel = np.zeros((2, 256), np.float32)
    kc_sel[0, 0:128] = 1.0
    kc_sel[1, 128:256] = 1.0
    shared = dict(w_ada=_f32(f(w_ada)[0]), b_ada2=b_ada2, gvecT=_f32(gvecT), gpost_b=gpost_b, gsgu_b=gsgu_b, lamT=_f32(lamT),
                  hbT=_f32(hbT), cwT=_f32(cwT), cbT=_f32(cbT), bd_w=_f32(bdm.reshape(128, 4096)), wspT=_f32(wspT),
                  bsp=_f32(f(b_sp)[0].reshape(1, 1024)), w_st=w_st, w_v=w_v, w_out=_f32(f(w_out)[0]), w_ff1=w_ff1_u,
                  w_ff2=_f32(f(w_ff2)[0]), kc_sel=kc_sel)
    in_maps = []
    for r in range(8):
        b, q = r // 4, r % 4
        xp = x_prompt[4 * r:4 * r + 4].reshape(1024, 1024)
        xs = x_sample[b, 1024 * q:1024 * (q + 1)]
        xin = np.ascontiguousarray(np.concatenate([xp, xs], 0))
        xhalo = np.zeros((3, 1024), np.float32)
        hm = np.zeros((128, 3), np.float32)
        for i, tok in enumerate((1024 * q - 2, 1024 * q - 1, 1024 * (q + 1))):
            if 0 <= tok < 4096:
                xhalo[i] = x_sample[b, tok]
                hm[:, i] = 1.0
        cselT = np.stack([_colT(c_ctx), _colT(c[b])], 2).reshape(128, 16)
        st0T = np.stack([_colT(state_lru[b, 0, 0]), _colT(state_lru[b, 0, 1])], 1).reshape(128, 16)
        mf = np.zeros((128, 8, 8), np.float32)
        mb = np.zeros((128, 8, 8), np.float32)
        for rr in range(4):
            if rr < q:
                mf[:, rr, :] = 1.0
            if rr > q:
                mb[:, rr, :] = 1.0
        m = dict(shared)
        m.update(xs_all=np.ascontiguousarray(x_sample[b]), xin=xin, xhalo=xhalo, hmask=hm, cselT=_f32(cselT), st0T=_f32(st0T), mskf=mf.reshape(128, 64), mskb=mb.reshape(128, 64))
        in_maps.append(m)
    if "nc" not in _NC_CACHE:
        _NC_CACHE["nc"] = build_program()
    res = run_bass_kernel_spmd(_NC_CACHE["nc"], in_maps, core_ids=list(range(8)))
    y_prompt = np.zeros((32, 256, 1024), np.float32)
    y_sample = np.zeros((2, 4096, 1024), np.float32)
    new_state = np.zeros((32, 1, 2, 1024), np.float32)
    for r in range(8):
        out = res.results[r]
        yo = np.asarray(out["y_out"], np.float32)
        y_prompt[4 * r:4 * r + 4] = yo[0:1024].reshape(4, 256, 1024)
        y_sample[r // 4, 1024 * (r % 4):1024 * (r % 4 + 1)] = yo[1024:2048]
        new_state[4 * r:4 * r + 4, 0] = np.asarray(out["ns_out"], np.float32).reshape(4, 2, 1024)
    return y_prompt, y_sample, new_state
```

```python
from contextlib import ExitStack
import numpy as np
import concourse.bass as bass
import concourse.mybir as mybir
from concourse.bass_utils import run_bass_kernel_spmd

F32 = mybir.dt.float32
BF16 = mybir.dt.bfloat16
I32 = mybir.dt.int32
AF = mybir.ActivationFunctionType
ALU = mybir.AluOpType
AP = bass.AP

D = 1024
T = 1024
NT = 8
EPS = 1e-6
ENG_NAMES = ("pe", "act", "dve", "pool", "sp")


class Prog:
    def __init__(self, n_dma_sems=28, record_only=False):
        self.streams = {e: [] for e in ENG_NAMES}
        self.count = {e: 0 for e in ENG_NAMES}
        self.n_dma_sems = n_dma_sems
        self.dma_cnt = [0] * n_dma_sems
        self.dma_rr = 0
        self.cc_cnt = 0
        self.state = {}
        self.waited = {e: {} for e in ENG_NAMES}
        self.overlaps = {}
        self.record_only = record_only
        self.sw_ids = {}
        self.sw_gen = {}

    def _split(self, key):
        return key if isinstance(key, tuple) else (key, None)

    def _own_states(self, name, idx, create):
        d = self.state.setdefault(name, {})
        if idx is None:
            if create and None not in d:
                d[None] = [{}, {}]
            return list(d.values())
        if idx not in d and create:
            if None in d:
                d[idx] = [dict(d[None][0]), dict(d[None][1])]
            else:
                d[idx] = [{}, {}]
        out = []
        if idx in d:
            out.append(d[idx])
        if None in d:
            out.append(d[None])
        return out

    def _dep_states(self, key):
        name, idx = self._split(key)
        out = self._own_states(name, idx, False)
        for o in self.overlaps.get(name, ()):
            out.extend(self.state.get(o, {}).values())
        return out

    def _deps(self, reads, writes):
        need = {}

        def add(d):
            for src, seq in d.items():
                if need.get(src, -1) < seq:
                    need[src] = seq
        for k in reads:
            for st in self._dep_states(k):
                add(st[0])
        for k in writes:
            for st in self._dep_states(k):
                add(st[0])
                add(st[1])
        return need

    def _commit(self, me, reads, writes):
        src, seq = me
        for k in reads:
            name, idx = self._split(k)
            sts = self._own_states(name, idx, True)
            if idx is None:
                for st in sts:
                    st[1][src] = seq
            else:
                sts[0][1][src] = seq
        for k in writes:
            name, idx = self._split(k)
            sts = self._own_states(name, idx, True)
            tgt = sts if idx is None else sts[:1]
            for st in tgt:
                st[0] = {src: seq}
                st[1] = {}

    def _emit_waits(self, eng, need):
        w = self.waited[eng]
        for src, seq in need.items():
            if src == eng and eng == "pe":
                continue
            if w.get(src, 0) >= seq:
                continue
            w[src] = seq
            self.streams[eng].append(("wait", src, seq))

    def op(self, eng, fn, reads=(), writes=()):
        need = self._deps(reads, writes)
        self._emit_waits(eng, need)
        self.count[eng] += 1
        me = (eng, self.count[eng])
        self.streams[eng].append(("ins", fn))
        self._commit(me, reads, writes)
        return me

    def dma(self, out, in_, reads=(), writes=(), queue="sp", **kw):
        need = self._deps(reads, writes)
        if queue == "pool":
            dk = writes[0]
            if dk not in self.sw_ids:
                self.sw_ids[dk] = len(self.sw_ids)
                self.sw_gen[dk] = 0
            sid = self.sw_ids[dk]
            self.sw_gen[dk] += 1
            self._emit_waits(queue, need)
            self.count["pool"] += 1
            marker = self.count["pool"]
            self.streams[queue].append(("swdma", out, in_, sid, kw, getattr(self, "last_marker", 0)))
            self.last_marker = marker
            src = ("sw", sid, self.sw_gen[dk])
            for k in reads:
                name, idx = self._split(k)
                for st in (self._own_states(name, idx, True) if idx is None else self._own_states(name, idx, True)[:1]):
                    st[1][src] = 16
            for k in writes:
                name, idx = self._split(k)
                sts = self._own_states(name, idx, True)
                for st in (sts if idx is None else sts[:1]):
                    st[0] = {src: 16, "pool": marker}
                    st[1] = {}
            return (src, 16)
        k = self.dma_rr
        self.dma_rr = (self.dma_rr + 1) % self.n_dma_sems
        src = ("dma", k)
        if self.dma_cnt[k] > 0:
            need[src] = max(need.get(src, 0), self.dma_cnt[k])
        self._emit_waits(queue, need)
        self.dma_cnt[k] += 16
        me = (src, self.dma_cnt[k])
        self.streams[queue].append(("dma", out, in_, k, kw))
        self._commit(me, reads, writes)
        return me

    def cc(self, fn, reads=(), writes=()):
        need = self._deps(reads, writes)
        self._emit_waits("pool", need)
        self.cc_cnt += 1
        me = ("cc", self.cc_cnt)
        self.streams["pool"].append(("cc", fn))
        self._commit(me, reads, writes)
        return me

    def wait_all(self, eng, keys):
        self._emit_waits(eng, self._deps(keys, ()))

    EPOCH = 1024

    def build(self, nc, stack):
        E = self.EPOCH
        sem = {e: [stack.enter_context(nc.semaphore(f"s_{e}{i}")) for i in range(self.count[e] // E + 1)] for e in ENG_NAMES}
        dsem = [stack.enter_context(nc.semaphore(f"s_dma{k}")) for k in range(self.n_dma_sems)]
        cc_sem = stack.enter_context(nc.semaphore("s_cc"))
        block = stack.enter_context(nc.Block())
        deco = {"pe": block.tensor, "act": block.scalar, "dve": block.vector,
                "pool": block.gpsimd, "sp": block.sync}

        def semval(src, seq):
            if src == "cc":
                return cc_sem, seq
            if isinstance(src, tuple):
                return dsem[src[1]], seq
            return sem[src][(seq - 1) // E], (seq - 1) % E + 1

        for e in ENG_NAMES:
            def body(eng, items=self.streams[e], e=e):
                n = 0
                for it in items:
                    if it[0] == "wait":
                        s_, v_ = semval(it[1], it[2])
                        eng.wait_ge(s_, v_)
                    elif it[0] == "ins":
                        it[1](eng).then_inc(sem[e][n // E], 1)
                        n += 1
                    elif it[0] == "cc":
                        it[1](eng).then_inc(cc_sem)
                    else:
                        _, out, in_, k, kw = it
                        eng.dma_start(out=out, in_=in_, **kw).then_inc(dsem[k], 16)
            deco[e](body)


ALL_PH = ("C", "A", "L", "G", "M1", "M2", "F1", "F2")


class Arena:
    def __init__(self):
        self.bufs = []

    def add(self, name, nbytes, phases):
        nbytes = (nbytes + 63) // 64 * 64
        self.bufs.append([name, nbytes, set(ALL_PH if phases == "*" else phases), None])

    def pack(self):
        order = sorted(self.bufs, key=lambda b: (-len(b[2]), -b[1]))
        placed = []
        for b in order:
            cands = sorted([0] + [p[3] + p[1] for p in placed])
            for off in cands:
                ok = True
                for p in placed:
                    if p[2] & b[2] and off < p[3] + p[1] and p[3] < off + b[1]:
                        ok = False
                        break
                if ok:
                    b[3] = off
                    break
            placed.append(b)
        self.total = max(b[3] + b[1] for b in self.bufs)
        self.off = {b[0]: b[3] for b in self.bufs}
        self.size = {b[0]: b[1] for b in self.bufs}
        ov = {}
        for a in self.bufs:
            for b in self.bufs:
                if a is not b and a[3] < b[3] + b[1] and b[3] < a[3] + a[1]:
                    ov.setdefault(a[0], []).append(b[0])
        self.ov = ov


def rev_ap(ap2d):
    apl = [list(d) for d in ap2d.ap]
    n = apl[-1][1]
    st = apl[-1][0]
    apl[-1] = [-st, n]
    return AP(ap2d.tensor, ap2d.offset + st * (n - 1), apl)


def build_program():
    nc = bass.Bass("TRN2", target_bir_lowering=False)

    def din(name, shape, dt=F32):
        return nc.dram_tensor(name, list(shape), dt, kind="ExternalInput").ap()

    def dout(name, shape, dt=F32):
        return nc.dram_tensor(name, list(shape), dt, kind="ExternalOutput").ap()

    xin = din("xin", [2048, D])
    xhalo = din("xhalo", [3, D])
    xs_all = din("xs_all", [4096, D])
    hmask = din("hmask", [128, 3])
    cselT = din("cselT", [128, 16])
    st0T = din("st0T", [128, 16])
    mskf = din("mskf", [128, 64])
    mskb = din("mskb", [128, 64])
    kc_sel = din("kc_sel", [2, 256])
    w_ada = din("w_ada", [D, 6 * D])
    b_ada2 = din("b_ada2", [2, 6 * D])
    gvecT = din("gvecT", [128, 16])
    gpost_b = din("gpost_b", [128, 2 * D])
    gsgu_bd = din("gsgu_b", [128, D])
    lamT = din("lamT", [128, 16])
    hbT = din("hbT", [128, 32])
    cwT = din("cwT", [128, 32])
    cbT = din("cbT", [128, 8])
    bd_w = din("bd_w", [128, 32 * 128])
    wspT = din("wspT", [128, 8 * 128])
    bsp = din("bsp", [1, D])
    w_st = din("w_st", [44, 128, 2048])
    w_v = din("w_v", [D, D])
    w_out = din("w_out", [D, D])
    w_ff1 = din("w_ff1", [16, 128, 2048])
    w_ff2 = din("w_ff2", [4 * D, D])
    y_out = dout("y_out", [2048, D])
    ns_out = dout("ns_out", [8, D])

    ar = Arena()
    A_ = ar.add
    A_("ident_b", 256, "*"); A_("ident_f", 512, "*"); A_("sel", 1024, "*"); A_("ones_b", 256, "*")
    A_("cT", 64, "*"); A_("sT", 64, "*"); A_("modT", 384, "*"); A_("modc", 256, "*")
    A_("lamc", 256, "*"); A_("hb", 128, "*"); A_("cw", 128, "*"); A_("cb", 32, "*")
    A_("bd", 8192, "*"); A_("wsp", 2048, "*"); A_("bspf", 4096, ["C"]); A_("bhi", 2048, "*"); A_("blo", 2048, "*")
    A_("gate", 16384, "*"); A_("gsgu", 4096, ["G"]); A_("hmask", 64, "*")
    A_("h0t", 64, "*"); A_("st0", 64, "*"); A_("msk", 512, "*"); A_("summ", 128, "*"); A_("allg", 1024, "*")
    A_("cand", 1024, "*"); A_("stT", 256, "*"); A_("sto", 4096, ["L"])
    A_("rs", 16 * 32, "*"); A_("ss", 512, "*"); A_("junk", 2048, "*"); A_("gvec", 64, "*")
    A_("ws", 4 * 4096, "*"); A_("wsf", 2 * 8192, "*")
    A_("wa", 2 * 16384, ["C"]); A_("modrow", 2 * 2048, ["C"]); A_("gpost", 8192, ["C"]); A_("bada", 2 * 2048, ["C"])
    A_("xa", 8 * 4096, ["A"]); A_("xna", 8 * 2048, ["A"]); A_("ssa", 64, "*"); A_("rsa", 5 * 64, "*")
    A_("xt", 2 * 4096, ["A", "M2"]); A_("xn", 2 * 2048, ["A", "M2"]); A_("xh", 4096, ["A"]); A_("xhn", 2048, ["A"])
    A_("hT", 8 * 1027 * 2, ["A", "L", "G", "M1"])
    for sfx in ("", "_b"):
        A_("xrp" + sfx, 1036 * 4, ["L"]); A_("xc" + sfx, 4096, ["L"]); A_("xcb" + sfx, 2048, ["L"]); A_("gg" + sfx, 4096, ["L"])
        for d_ in range(2):
            A_(f"tr{d_}" + sfx, 4096, ["L"]); A_(f"aa{d_}" + sfx, 4096, ["L"]); A_(f"ti{d_}" + sfx, 4096, ["L"]); A_(f"hh{d_}" + sfx, 4096, ["L"])
    A_("yr", 16384, ["L", "G", "M1"])
    A_("gu", 16384, ["G"]); A_("wv", 16384, ["G"]); A_("gv", 2 * 4096, ["G"]); A_("vn", 2 * 2048, ["G"])
    A_("yg", 16384, ["G", "M1"])
    A_("ta", 2 * 2048, ["M1"]); A_("tb", 2 * 2048, ["M1"]); A_("m1", 2 * 2048, ["M1"]); A_("m2", 2 * 2048, ["M1"])
    A_("mg", 16384, ["M1", "M2"])
    A_("wo", 16384, ["M2"]); A_("tmp", 4096, ["M2"])
    A_("x1", 32768, ["M2", "F1", "F2"]); A_("h2T", 16384, ["M2", "F1"])
    A_("f1T", 65536, ["F1", "F2"]); A_("rt", 2 * 1024, ["F1"]); A_("f2", 32768, ["F2"])
    ar.pack()
    import os as _os2
    if _os2.environ.get("KARENA"):
        for ph in ALL_PH:
            print(ph, sum(b[1] for b in ar.bufs if ph in b[2]))
        print("total", ar.total)
    assert ar.total <= 206 * 1024, ar.total
    arena = nc.alloc_sbuf_tensor("arena", [128, ar.total // 4], F32)

    def V(name, dt=F32, pattern=None, **kw):
        o = ar.off[name] // 4
        n = ar.size[name] // 4
        v = arena[:, o:o + n]
        if dt != F32:
            v = v.bitcast(dt)
        if pattern:
            v = v.rearrange(pattern, **kw)
        return v

    ident_b = V("ident_b", BF16); ident_f = V("ident_f"); sel = V("sel")[0:2, :]; ones_b = V("ones_b", BF16)
    cT = V("cT"); sT = V("sT"); modT = V("modT", F32, "p (c r) -> p c r", r=2)
    modc = V("modc", F32, "p (w r k) -> p w r k", w=4, r=2)
    lamc = V("lamc", F32, "p (w d j) -> p w d j", w=4, d=2)
    hb = V("hb", F32, "p (g j) -> p g j", g=4); cw = V("cw", F32, "p (j k) -> p j k", k=4); cb = V("cb")
    bd = V("bd", BF16, "p (g j m) -> p g j m", g=4, j=8); wsp = V("wsp", BF16, "p (g q) -> p g q", g=8)
    bspf = V("bspf"); bhi = V("bhi", BF16); blo = V("blo", BF16)
    gate = V("gate", F32, "p (w r n) -> p w r n", w=2, r=2); gsgu = V("gsgu"); hmk = V("hmask")
    h0t = V("h0t", F32, "p (d j) -> p d j", d=2); st0 = V("st0", F32, "p (d j) -> p d j", d=2)
    msk = V("msk", F32, "p (w r j) -> p w r j", w=2, r=8)
    summ = V("summ", F32, "p (q j) -> p q j", q=4); allg = V("allg", F32, "p (r q j) -> p r q j", r=8, q=4)
    cand = V("cand", F32, "p (w r j) -> p w r j", w=4, r=8)
    stT = V("stT", F32, "p (j s d) -> p j s d", j=8, s=4); sto = V("sto")
    rs = V("rs", F32, "p (s c) -> p s c", c=8); ssq = V("ss", F32, "p (s c) -> p s c", c=4)
    junk = V("junk", BF16); gvec = V("gvec", F32, "p (w k) -> p w k", w=2)
    ws = V("ws", BF16, "p (s n) -> p s n", s=4)
    wsf = V("wsf", F32, "p (s n) -> p s n", s=2)
    wa = V("wa", F32, "p (s k n) -> p s k n", s=2, k=8); modrow = V("modrow", F32, "p (s n) -> p s n", s=2)
    gpost = V("gpost", F32, "p (w n) -> p w n", w=2); bada = V("bada", F32, "p (s n) -> p s n", s=2)
    xt = V("xt", F32, "p (s n) -> p s n", s=2); xn = V("xn", BF16, "p (s n) -> p s n", s=2)
    xh = V("xh"); xhn = V("xhn", BF16)
    xa = V("xa", F32, "p (t n) -> p t n", t=8); xna = V("xna", BF16, "p (t n) -> p t n", t=8)
    ssa = V("ssa"); rsa = V("rsa", F32, "p (c n) -> p c n", c=5)
    hT = V("hT", BF16, "p (k n) -> p k n", k=8)
    LS = []
    for sfx in ("", "_b"):
        LS.append(dict(xrp=V("xrp" + sfx), xc=V("xc" + sfx), xcb=V("xcb" + sfx, BF16), gg=V("gg" + sfx),
                       trb=[V(f"tr{d_}" + sfx) for d_ in range(2)], aab=[V(f"aa{d_}" + sfx) for d_ in range(2)],
                       tib=[V(f"ti{d_}" + sfx) for d_ in range(2)], hhb=[V(f"hh{d_}" + sfx) for d_ in range(2)], sfx=sfx))
    yr = V("yr", BF16, "p (k n) -> p k n", k=8); yg = V("yg", BF16, "p (k n) -> p k n", k=8)
    gu = V("gu", BF16, "p (k n) -> p k n", k=8); wv = V("wv", BF16, "p (k n) -> p k n", k=8)
    gv = V("gv", F32, "p (s n) -> p s n", s=2); vn = V("vn", BF16, "p (s n) -> p s n", s=2)
    ta = V("ta", F32, "p (s n) -> p s n", s=2); tb = V("tb", F32, "p (s n) -> p s n", s=2)
    m1 = V("m1", F32, "p (s n) -> p s n", s=2); m2 = V("m2", F32, "p (s n) -> p s n", s=2)
    mg = V("mg", BF16, "p (k n) -> p k n", k=8); wo = V("wo", BF16, "p (k n) -> p k n", k=8)
    tmp = V("tmp"); x1 = V("x1", F32, "p (t n) -> p t n", t=8); h2T = V("h2T", BF16, "p (k n) -> p k n", k=8)
    f1T = V("f1T", BF16, "p (k n) -> p k n", k=32); rt = V("rt", BF16, "p (s n) -> p s n", s=2)
    f2 = V("f2", F32, "p (t n) -> p t n", t=8)

    pp = [nc.alloc_psum_tensor(f"pp{i}", [128, 1024], F32) for i in range(4)]

    def bank(b):
        return pp[b // 2][:, (b % 2) * 512:(b % 2) * 512 + 512]

    def emit(p, wplan):
        st = {"bank": 0, "pair": 0, "rs": 0, "ss": 0, "wi": 0, "issued": 0, "alt": 0, "sf": 0}

        def nb():
            b = st["bank"]; st["bank"] = (b + 1) % 8
            return b

        def npair():
            b = st["pair"]; st["pair"] = (b + 1) % 4
            return b

        def alt():
            st["alt"] ^= 1
            return "act" if st["alt"] else "dve"

        PSK = lambda b: ("ps", b)

        def stage_piece(src, dst, dkey):
            k = st["sf"]; st["sf"] = (k + 1) % 2
            n = 1
            for d_ in src.shape[1:]:
                n *= d_
            sfl = wsf[:, k, 0:n]
            sv = sfl
            if len(src.shape) == 3:
                sv = sfl.rearrange("p (a n) -> p a n", a=src.shape[1])
            p.dma(sv, src, writes=[("wsf", k)])
            dfl = dst
            p.op("pool", lambda g_: g_.tensor_copy(out=dfl, in_=(sv if len(dst.shape) == 3 else sfl)), reads=[("wsf", k)], writes=[dkey])

        def wunit(desc):
            i = st["wi"]; st["wi"] += 1
            if wplan is None:
                p.wlist.append(desc)
            else:
                while st["issued"] < min(len(wplan), i + 3):
                    k = st["issued"]; st["issued"] += 1
                    src = wplan[k][0]
                    dst = ws[:, k % 4, :]
                    if len(src.shape) == 3:
                        dst = dst.rearrange("p (a n) -> p a n", a=src.shape[1])
                    stage_piece(src, dst, ("ws", k % 4))
            return ws[:, i % 4, :], ("ws", i % 4)

        def rstd(ss_list, keys, np_, scale, eps):
            s = st["rs"]; st["rs"] = (s + 1) % 16
            R = rs[0:np_, s, :]
            k = ("rs", s)
            xx, tii, yy, hh_, zz = (R[:, c:c + 1] for c in range(5))
            if len(ss_list) == 2:
                p.op("dve", lambda v: v.tensor_tensor(out=xx, in0=ss_list[0], in1=ss_list[1], op=ALU.add), reads=keys, writes=[k])
                src = xx
                rk = [k]
            else:
                src = ss_list[0]
                rk = keys
            p.op("dve", lambda v: v.tensor_scalar(out=xx, in0=src, scalar1=scale, scalar2=eps, op0=ALU.mult, op1=ALU.add), reads=rk, writes=[k])
            p.op("dve", lambda v: v.tensor_scalar(out=tii.bitcast(I32), in0=xx.bitcast(I32), scalar1=1, scalar2=None, op0=ALU.arith_shift_right), reads=[k], writes=[k])
            p.op("dve", lambda v: v.tensor_scalar(out=yy.bitcast(I32), in0=tii.bitcast(I32), scalar1=-1, scalar2=0x5f3759df, op0=ALU.mult, op1=ALU.add), reads=[k], writes=[k])
            p.op("dve", lambda v: v.tensor_scalar(out=hh_, in0=xx, scalar1=0.5, scalar2=None, op0=ALU.mult), reads=[k], writes=[k])
            for _ in range(3):
                p.op("dve", lambda v: v.scalar_tensor_tensor(out=zz, in0=yy, scalar=yy, in1=hh_, op0=ALU.mult, op1=ALU.mult), reads=[k], writes=[k])
                p.op("dve", lambda v: v.tensor_scalar(out=zz, in0=zz, scalar1=-1.0, scalar2=1.5, op0=ALU.mult, op1=ALU.add), reads=[k], writes=[k])
                p.op("dve", lambda v: v.tensor_tensor(out=yy, in0=yy, in1=zz, op=ALU.mult), reads=[k], writes=[k])
            return yy, k

        def sscol():
            s = st["ss"]; st["ss"] = (s + 1) % 32
            return ssq[:, s, :], ("ss", s)

        def stage_consts():
            sp_loads = [(cT, cselT, "cT"), (st0.rearrange("p d j -> p (d j)"), st0T, "st0"),
                        (msk[:, 0].rearrange("p r j -> p (r j)"), mskf, ("msk", 0)), (msk[:, 1].rearrange("p r j -> p (r j)"), mskb, ("msk", 1)),
                        (sel, kc_sel, "sel"), (gvec.rearrange("p w k -> p (w k)"), gvecT, "gvec"),
                        (lamc[:, 3].rearrange("p d j -> p (d j)"), lamT, "lamc"), (hb.rearrange("p g j -> p (g j)"), hbT, "hb"),
                        (cw.rearrange("p j k -> p (j k)"), cwT, "cw"), (cb[:, 0:8], cbT, "cb"), (hmk[:, 0:3], hmask, "hmask"),
                        (bspf[0:1, 0:D], bsp, "bspf"),
                        (gpost.rearrange("p w n -> p (w n)"), gpost_b, "gpost")]
            for dst, src, key in sp_loads:
                p.dma(dst, src, writes=[key])
            bdf = bd.rearrange("p g j m -> p (g j m)")
            for i_ in range(2):
                stage_piece(bd_w[:, i_ * 2048:(i_ + 1) * 2048], bdf[:, i_ * 2048:(i_ + 1) * 2048], ("bd", i_))
            stage_piece(wspT, wsp.rearrange("p g q -> p (g q)"), "wsp")
            p.op("pool", lambda g_: g_.memset(ident_f, 0.0), writes=["ident_f"])
            p.op("pool", lambda g_: g_.affine_select(out=ident_f, in_=ident_f, compare_op=ALU.not_equal, fill=1.0, base=0,
                                                      pattern=[[-1, 128]], channel_multiplier=1), reads=["ident_f"], writes=["ident_f"])
            p.op("dve", lambda v: v.tensor_copy(out=ident_b, in_=ident_f), reads=["ident_f"], writes=["ident_b"])
            p.op("pool", lambda g_: g_.memset(ones_b, 1.0), writes=["ones_b"])
            p.op("pool", lambda g_: g_.memset(allg.rearrange("p r q j -> p (r q j)"), 0.0), writes=["allg"])
            B0 = bspf[0:1, 0:D]; B1 = bspf[0:1, D:2 * D] if False else None
            p.op("dve", lambda v: v.tensor_copy(out=bhi[0:1, 0:D], in_=B0), reads=["bspf"], writes=["bhi"])
            p.op("dve", lambda v: v.tensor_tensor(out=B0, in0=B0, in1=bhi[0:1, 0:D], op=ALU.subtract), reads=["bspf", "bhi"], writes=["bspf"])
            p.op("dve", lambda v: v.tensor_copy(out=blo[0:1, 0:D], in_=B0), reads=["bspf"], writes=["blo"])
            p.op("act", lambda a: a.activation(out=sT, in_=cT, func=AF.Tanh, scale=0.5), reads=["cT"], writes=["sT"])
            p.op("dve", lambda v: v.scalar_tensor_tensor(out=sT, in0=sT, scalar=1.0, in1=cT, op0=ALU.add, op1=ALU.mult), reads=["sT", "cT"], writes=["sT"])
            p.op("dve", lambda v: v.tensor_scalar(out=sT, in0=sT, scalar1=0.5, scalar2=None, op0=ALU.mult), reads=["sT"], writes=["sT"])
            sT3 = sT.rearrange("p (k r) -> p k r", r=2)
            L3 = lamc[:, 3]
            p.op("act", lambda a: a.activation(out=L3, in_=L3, func=AF.Exp, scale=-1.0), reads=["lamc"], writes=["lamc"])
            p.op("act", lambda a: a.activation(out=L3, in_=L3, func=AF.Ln, bias=1.0, scale=1.0), reads=["lamc"], writes=["lamc"])
            p.op("dve", lambda v: v.tensor_scalar(out=lamc[:, 0], in0=L3, scalar1=-8.0, scalar2=None, op0=ALU.mult), reads=["lamc"], writes=["lamc"])
            p.op("dve", lambda v: v.tensor_scalar(out=lamc[:, 1], in0=L3, scalar1=-4.0, scalar2=None, op0=ALU.mult), reads=["lamc"], writes=["lamc"])
            p.op("dve", lambda v: v.tensor_scalar(out=lamc[:, 2], in0=L3, scalar1=-4096.0, scalar2=None, op0=ALU.mult), reads=["lamc"], writes=["lamc"])
            p.op("dve", lambda v: v.tensor_scalar(out=hb.rearrange("p g j -> p (g j)"), in0=hb.rearrange("p g j -> p (g j)"), scalar1=0.5, scalar2=None, op0=ALU.mult), reads=["hb"], writes=["hb"])
            order = [2, 3, 0, 1, 4, 5, 6, 7, 8, 9, 10, 11]
            mt_bank = nb()
            mt_ps = bank(mt_bank)[:, 0:96].rearrange("p (c r) -> p c r", r=2)
            for n_, cbk in enumerate(order):
                s = n_ % 2
                p.dma(wa[:, s], w_ada[:, cbk * 512:(cbk + 1) * 512].rearrange("(k p) n -> p k n", p=128), writes=[("wa", s)])
                p.dma(bada[0:2, s, :], b_ada2[:, cbk * 512:(cbk + 1) * 512], writes=[("bada", s)])
                b = nb()
                if b == mt_bank:
                    b = nb()
                for kc in range(8):
                    p.op("pe", lambda t_, kc=kc, s=s, b=b: t_.matmul(bank(b)[0:2, :], lhsT=sT3[:, kc, :], rhs=wa[:, s, kc, :], start=(kc == 0), stop=(kc == 7)),
                         reads=["sT", ("wa", s)], writes=[PSK(b)])
                p.op("dve", lambda v, s=s, b=b: v.tensor_tensor(out=modrow[0:2, s, :], in0=bank(b)[0:2, :], in1=bada[0:2, s, :], op=ALU.add),
                     reads=[PSK(b), ("bada", s)], writes=[("modrow", s)])
                for i in range(4):
                    ci = cbk * 4 + i
                    p.op("pe", lambda t_, s=s, i=i, ci=ci: t_.matmul(mt_ps[:, ci, :], lhsT=modrow[0:2, s, i * 128:(i + 1) * 128], rhs=ident_f[0:2, 0:2], start=True, stop=True),
                         reads=[("modrow", s), "ident_f"], writes=[PSK(mt_bank)])
                if cbk in (4, 5, 10, 11):
                    w_ = 0 if cbk < 6 else 1
                    hf = cbk % 2
                    for r in range(2):
                        b2 = nb()
                        if b2 == mt_bank:
                            b2 = nb()
                        p.op("pe", lambda t_, s=s, r=r, b2=b2: t_.matmul(bank(b2), lhsT=sel[:, r * 128:(r + 1) * 128], rhs=modrow[0:2, s, :], start=True, stop=True),
                             reads=[("modrow", s), "sel"], writes=[PSK(b2)])
                        p.op("dve", lambda v, w_=w_, r=r, hf=hf, b2=b2: v.tensor_tensor(out=gate[:, w_, r, hf * 512:(hf + 1) * 512], in0=bank(b2), in1=gpost[:, w_, hf * 512:(hf + 1) * 512], op=ALU.mult),
                             reads=[PSK(b2), "gpost"], writes=[("gate", w_ * 2 + r)])
            p.op("dve", lambda v: v.tensor_copy(out=modT, in_=mt_ps), reads=[PSK(mt_bank)], writes=["modT"])
            for r in range(2):
                for w_, (sc_i, sh_i) in enumerate(((8, 0), (32, 24))):
                    p.op("dve", lambda v, r=r, w_=w_, sc_i=sc_i: v.scalar_tensor_tensor(out=modc[:, 2 * w_, r, :], in0=modT[:, sc_i:sc_i + 8, r], scalar=1.0, in1=gvec[:, w_, :], op0=ALU.add, op1=ALU.mult),
                         reads=["modT", "gvec"], writes=["modc"])
                    p.op("dve", lambda v, r=r, w_=w_, sh_i=sh_i: v.tensor_copy(out=modc[:, 2 * w_ + 1, r, :], in_=modT[:, sh_i:sh_i + 8, r]),
                         reads=["modT"], writes=["modc"])

        def norm_transpose(src_tile, src_key, np_, r, widx, dstT, dst_key, col0, xn_tile, xn_key):
            sc, sk = sscol()
            p.op("act", lambda a: a.activation(out=junk[0:np_, :], in_=src_tile, func=AF.Square, accum_out=sc[0:np_, 0:1]),
                 reads=[src_key], writes=["junk", sk])
            rr, rk = rstd([sc[0:np_, 0:1]], [sk], np_, 1.0 / D, EPS)
            p.op("act", lambda a: a.activation(out=xn_tile, in_=src_tile, func=AF.Identity, scale=rr), reads=[src_key, rk], writes=[xn_key])
            transpose_evac(xn_tile, xn_key, np_, r, widx, dstT, dst_key, col0)

        def transpose_evac(xn_tile, xn_key, np_, r, widx, dstT, dst_key, col0):
            b = nb()
            pt = bank(b).bitcast(BF16).rearrange("p (k n) -> p k n", k=8)
            for kc in range(8):
                p.op("pe", lambda t_, kc=kc: t_.transpose(pt[:, kc, 0:np_], xn_tile[:, kc * 128:(kc + 1) * 128], ident_b[0:np_, 0:np_]),
                     reads=[xn_key, "ident_b"], writes=[PSK(b)])
            use_act = alt() == "act"
            for kc in range(8):
                scl = modc[:, 2 * widx, r, kc:kc + 1]; shf = modc[:, 2 * widx + 1, r, kc:kc + 1]
                o = dstT[:, kc, col0:col0 + np_]
                if kc % 2 == 0:
                    p.op("act", lambda a, o=o, kc=kc, scl=scl, shf=shf: a.activation(out=o, in_=pt[:, kc, 0:np_], func=AF.Identity, scale=scl, bias=shf),
                         reads=[PSK(b), "modc"], writes=[dst_key])
                else:
                    p.op("dve", lambda v, o=o, kc=kc, scl=scl, shf=shf: v.tensor_scalar(out=o, in0=pt[:, kc, 0:np_], scalar1=scl, scalar2=shf, op0=ALU.mult, op1=ALU.add),
                         reads=[PSK(b), "modc"], writes=[dst_key])

        def stage_A(half, chunk=None):
            r = half
            src, base = (xin, half * T) if chunk is None else (xs_all, chunk * T)
            for t in range(NT):
                p.dma(xa[:, t, :], src[base + t * 128: base + (t + 1) * 128, :], writes=[("xa", t)])
            for t in range(NT):
                p.op("act", lambda a, t=t: a.activation(out=junk, in_=xa[:, t, :], func=AF.Square, accum_out=ssa[:, t:t + 1]),
                     reads=[("xa", t)], writes=["junk", ("ssa", t)])
            xx, tii, yy, hh_, zz = (rsa[:, c, 0:8] for c in range(5))
            k = "rsa"
            p.op("dve", lambda v: v.tensor_scalar(out=xx, in0=ssa[:, 0:8], scalar1=1.0 / D, scalar2=EPS, op0=ALU.mult, op1=ALU.add), reads=[("ssa", t) for t in range(NT)], writes=[k])
            p.op("dve", lambda v: v.tensor_scalar(out=tii.bitcast(I32), in0=xx.bitcast(I32), scalar1=1, scalar2=None, op0=ALU.arith_shift_right), reads=[k], writes=[k])
            p.op("dve", lambda v: v.tensor_scalar(out=yy.bitcast(I32), in0=tii.bitcast(I32), scalar1=-1, scalar2=0x5f3759df, op0=ALU.mult, op1=ALU.add), reads=[k], writes=[k])
            p.op("dve", lambda v: v.tensor_scalar(out=hh_, in0=xx, scalar1=0.5, scalar2=None, op0=ALU.mult), reads=[k], writes=[k])
            for _ in range(3):
                p.op("dve", lambda v: v.tensor_tensor(out=zz, in0=yy, in1=yy, op=ALU.mult), reads=[k], writes=[k])
                p.op("dve", lambda v: v.tensor_tensor(out=zz, in0=zz, in1=hh_, op=ALU.mult), reads=[k], writes=[k])
                p.op("dve", lambda v: v.tensor_scalar(out=zz, in0=zz, scalar1=-1.0, scalar2=1.5, op0=ALU.mult, op1=ALU.add), reads=[k], writes=[k])
                p.op("dve", lambda v: v.tensor_tensor(out=yy, in0=yy, in1=zz, op=ALU.mult), reads=[k], writes=[k])
            for t in range(NT):
                p.op("act", lambda a, t=t: a.activation(out=xna[:, t, :], in_=xa[:, t, :], func=AF.Identity, scale=yy[:, t:t + 1]), reads=[("xa", t), k], writes=[("xna", t)])
                transpose_evac(xna[:, t, :], ("xna", t), 128, r, 0, hT, ("hT", t), t * 128)
            if half == 1:
                if chunk is None:
                    p.dma(xh[0:3, 0:D], xhalo, writes=["xh"])
                else:
                    p.op("dve", lambda v: v.memset(xh[0:3, 0:D], 0.0), writes=["xh"])
                    if chunk > 0:
                        p.dma(xh[0:2, 0:D], xs_all[base - 2:base, :], writes=["xh"])
                    if chunk < 3:
                        p.dma(xh[2:3, 0:D], xs_all[base + T:base + T + 1, :], writes=["xh"])
                norm_transpose(xh[0:3, 0:D], "xh", 3, r, 0, hT, ("hT", 8), T, xhn[0:3, 0:D], "xhn")

        HTB = lambda blk: [("hT", 4 * blk + i) for i in range(4)]

        def mm_st(b, wview, rhs_fn, rkeys, wkey, ncols=512):
            for kc in range(8):
                p.op("pe", lambda t_, kc=kc: t_.matmul(bank(b)[:, 0:ncols], lhsT=wview[:, kc, :], rhs=rhs_fn(kc), start=(kc == 0), stop=(kc == 7)),
                     reads=[wkey] + rkeys, writes=[PSK(b)])

        def stage_L(half, prepass, chunk=None):
            nseq, L = (4, 256) if half == 0 else (1, 1024)
            XW = L + 3
            v3 = lambda tl: tl.rearrange("p (s n) -> p s n", s=nseq)
            for S_ in LS:
                p.op("dve", lambda v, S_=S_: v.memset(S_["xrp"], 0.0), writes=["xrp" + S_["sfx"]])
            dirs = [d for d in (0, 1) if not (prepass and ((chunk == 0 and d == 1) or (chunk == 3 and d == 0)))]

            def do_chunk(j, S_):
                xrp, xc, xcb, gg = S_["xrp"], S_["xc"], S_["xcb"], S_["gg"]
                trb, aab, tib, hhb = S_["trb"], S_["aab"], S_["tib"], S_["hhb"]
                KS = lambda n: n + S_["sfx"]
                xrp3 = xrp[:, 0:nseq * XW].rearrange("p (s n) -> p s n", s=nseq)
                wsl, wk = wunit((w_st[8 + j], None))
                wx = wsl.rearrange("p (u k n) -> p u k n", u=2, k=8)
                for blk in range(2):
                    b = nb()
                    mm_st(b, wx[:, 0], lambda kc, blk=blk: hT[:, kc, blk * 512:(blk + 1) * 512], HTB(blk), wk)
                    if half == 0:
                        dst = xrp3[:, 2 * blk:2 * blk + 2, 2:2 + L]
                        src = bank(b).rearrange("p (s n) -> p s n", s=2)
                    else:
                        dst = xrp[:, 2 + blk * 512:2 + (blk + 1) * 512]
                        src = bank(b)
                    p.op("dve", lambda v, dst=dst, src=src: v.tensor_copy(out=dst, in_=src), reads=[PSK(b)], writes=[KS("xrp")])
                if half == 1:
                    b = nb()
                    mm_st(b, wx[:, 0], lambda kc: hT[:, kc, T:T + 3], [("hT", 8)], wk, ncols=3)
                    if prepass:
                        if chunk > 0:
                            p.op("dve", lambda v, b=b: v.tensor_copy(out=xrp[:, 0:2], in_=bank(b)[:, 0:2]), reads=[PSK(b)], writes=[KS("xrp")])
                        if chunk < 3:
                            p.op("dve", lambda v, b=b: v.tensor_copy(out=xrp[:, 2 + T:3 + T], in_=bank(b)[:, 2:3]), reads=[PSK(b)], writes=[KS("xrp")])
                    else:
                        p.op("dve", lambda v, b=b: v.tensor_tensor(out=xrp[:, 0:2], in0=bank(b)[:, 0:2], in1=hmk[:, 0:2], op=ALU.mult), reads=[PSK(b), "hmask"], writes=[KS("xrp")])
                        p.op("dve", lambda v, b=b: v.tensor_tensor(out=xrp[:, 2 + T:3 + T], in0=bank(b)[:, 2:3], in1=hmk[:, 2:3], op=ALU.mult), reads=[PSK(b), "hmask"], writes=[KS("xrp")])
                if not prepass:
                    for blk in range(2):
                        b = nb()
                        mm_st(b, wx[:, 1], lambda kc, blk=blk: hT[:, kc, blk * 512:(blk + 1) * 512], HTB(blk), wk)
                        p.op("act", lambda a, b=b, blk=blk: a.activation(out=gg[:, blk * 512:(blk + 1) * 512], in_=bank(b), func=AF.Gelu_apprx_tanh),
                             reads=[PSK(b)], writes=[KS("gg")])
                xc3 = v3(xc)
                p.op("dve", lambda v, j=j: v.tensor_scalar(out=xc3, in0=xrp3[:, :, 0:L], scalar1=cw[:, j, 0:1], scalar2=cb[:, j:j + 1], op0=ALU.mult, op1=ALU.add),
                     reads=[KS("xrp"), "cw", "cb"], writes=[KS("xc")])
                for k in range(1, 4):
                    p.op("dve", lambda v, j=j, k=k: v.scalar_tensor_tensor(out=xc3, in0=xrp3[:, :, k:k + L], scalar=cw[:, j, k:k + 1], in1=xc3, op0=ALU.mult, op1=ALU.add),
                         reads=[KS("xrp"), "cw", KS("xc")], writes=[KS("xc")])
                p.op("act", lambda a: a.copy(out=xcb, in_=xc), reads=[KS("xc")], writes=[KS("xcb")])
                tsum = []
                for g in range(4):
                    d_ = g % 2
                    if d_ not in dirs:
                        continue
                    dstb = trb[d_] if g < 2 else tib[d_]
                    dkey = KS(f"tr{d_}") if g < 2 else KS(f"ti{d_}")
                    for blk in range(2):
                        b = nb()
                        p.op("pe", lambda t_, g=g, j=j, blk=blk, b=b: t_.matmul(bank(b), lhsT=bd[:, g, j, :], rhs=xcb[:, blk * 512:(blk + 1) * 512], start=True, stop=True),
                             reads=["bd", KS("xcb")], writes=[PSK(b)])
                        if prepass and g < 2:
                            sc, sk = sscol()
                            tsum.append((d_, sc, sk))
                            p.op("act", lambda a, g=g, j=j, blk=blk, b=b, dstb=dstb, sc=sc: a.activation(out=dstb[:, blk * 512:(blk + 1) * 512], in_=bank(b), func=AF.Tanh, scale=0.5, bias=hb[:, g, j:j + 1], accum_out=sc[:, 0:1]),
                                 reads=[PSK(b), "hb"], writes=[dkey, sk])
                        else:
                            p.op("act", lambda a, g=g, j=j, blk=blk, b=b, dstb=dstb: a.activation(out=dstb[:, blk * 512:(blk + 1) * 512], in_=bank(b), func=AF.Tanh, scale=0.5, bias=hb[:, g, j:j + 1]),
                                 reads=[PSK(b), "hb"], writes=[dkey])
                for d_ in dirs:
                    p.op("act", lambda a, d_=d_, j=j: a.activation(out=aab[d_], in_=trb[d_], func=AF.Exp, scale=lamc[:, 1, d_, j:j + 1], bias=lamc[:, 1, d_, j:j + 1]),
                         reads=[KS(f"tr{d_}"), "lamc"], writes=[KS(f"aa{d_}")])
                    p.op("act", lambda a, d_=d_, j=j: a.activation(out=trb[d_], in_=trb[d_], func=AF.Exp, scale=lamc[:, 0, d_, j:j + 1], bias=lamc[:, 0, d_, j:j + 1]),
                         reads=[KS(f"tr{d_}"), "lamc"], writes=[KS(f"tr{d_}")])
                for d_ in dirs:
                    p.op("act", lambda a, d_=d_: a.activation(out=trb[d_], in_=trb[d_], func=AF.Sqrt, scale=-0.25, bias=0.25),
                         reads=[KS(f"tr{d_}")], writes=[KS(f"tr{d_}")])
                for d_ in dirs:
                    p.op("dve", lambda v, d_=d_: v.scalar_tensor_tensor(out=tib[d_], in0=tib[d_], scalar=1.0, in1=xc, op0=ALU.add, op1=ALU.mult),
                         reads=[KS(f"ti{d_}"), KS("xc")], writes=[KS(f"ti{d_}")])
                    p.op("dve", lambda v, d_=d_: v.tensor_tensor(out=tib[d_], in0=tib[d_], in1=trb[d_], op=ALU.mult),
                         reads=[KS(f"ti{d_}"), KS(f"tr{d_}")], writes=[KS(f"ti{d_}")])
                for d_ in dirs:
                    a3 = v3(aab[d_]); b3 = v3(tib[d_]); h3 = v3(hhb[d_])
                    for s in range(nseq):
                        init = 0.0
                        ik = []
                        if not (prepass or half == 0):
                            ix = 0 if d_ == 0 else T - 1
                            p.op("dve", lambda v, d_=d_, j=j, ix=ix: v.scalar_tensor_tensor(out=tib[d_][:, ix:ix + 1], in0=aab[d_][:, ix:ix + 1], scalar=h0t[:, d_, j:j + 1], in1=tib[d_][:, ix:ix + 1], op0=ALU.mult, op1=ALU.add),
                                 reads=[KS(f"aa{d_}"), KS(f"ti{d_}"), "h0t"], writes=[KS(f"ti{d_}")])
                        if d_ == 0:
                            oa, da, db = h3[:, s, :], a3[:, s, :], b3[:, s, :]
                        else:
                            oa, da, db = rev_ap(h3[:, s, :]), rev_ap(a3[:, s, :]), rev_ap(b3[:, s, :])
                        p.op("dve", lambda v, oa=oa, da=da, db=db, init=init: v.tensor_tensor_scan(out=oa, data0=da, data1=db, initial=init, op0=ALU.mult, op1=ALU.add),
                             reads=[KS(f"aa{d_}"), KS(f"ti{d_}")] + ik, writes=[KS(f"hh{d_}")])
                hf3 = v3(hhb[0]); hb3 = v3(hhb[1])
                if prepass:
                    for d_ in dirs:
                        cols = [(sc, sk) for (dd, sc, sk) in tsum if dd == d_]
                        tcol, tk = sscol()
                        p.op("dve", lambda v, cols=cols, tcol=tcol: v.tensor_tensor(out=tcol[:, 0:1], in0=cols[0][0][:, 0:1], in1=cols[1][0][:, 0:1], op=ALU.add),
                             reads=[cols[0][1], cols[1][1]], writes=[tk])
                        p.op("act", lambda a, d_=d_, j=j, tcol=tcol: a.activation(out=allg[:, chunk, 2 * d_, j:j + 1], in_=tcol[:, 0:1], func=AF.Exp, scale=lamc[:, 1, d_, j:j + 1], bias=lamc[:, 2, d_, j:j + 1]),
                             reads=[tk, "lamc"], writes=[("allg", chunk * 32 + 16 * d_ + j)])
                    if 0 in dirs:
                        p.op("dve", lambda v, j=j: v.tensor_copy(out=allg[:, chunk, 1, j:j + 1], in_=hhb[0][:, T - 1:T]), reads=[KS("hh0")], writes=[("allg", chunk * 32 + 8 + j)])
                    if 1 in dirs:
                        p.op("dve", lambda v, j=j: v.tensor_copy(out=allg[:, chunk, 3, j:j + 1], in_=hhb[1][:, 0:1]), reads=[KS("hh1")], writes=[("allg", chunk * 32 + 24 + j)])
                else:
                    if half == 0:
                        p.op("dve", lambda v, j=j: v.tensor_copy(out=stT[:, j, :, 0], in_=hf3[:, :, L - 1]), reads=[KS("hh0")], writes=["stT"])
                        p.op("dve", lambda v, j=j: v.tensor_copy(out=stT[:, j, :, 1], in_=hb3[:, :, 0]), reads=[KS("hh1")], writes=["stT"])
                    p.op("dve", lambda v: v.tensor_tensor(out=hhb[0], in0=hhb[0], in1=hhb[1], op=ALU.add), reads=[KS("hh0"), KS("hh1")], writes=[KS("hh0")])
                    p.op("dve", lambda v, j=j: v.tensor_tensor(out=yr[:, j, :], in0=hhb[0], in1=gg, op=ALU.mult), reads=[KS("hh0"), KS("gg")], writes=[("yr", j)])
            for j in range(8):
                do_chunk(j, LS[j % 2])
            import os as _os5
            if (not prepass) and half == 0 and not _os5.environ.get("KNONS"):
                pr = npair()
                for j in range(8):
                    p.op("pe", lambda t_, j=j, pr=pr: t_.matmul(pp[pr][0:8, j * 128:(j + 1) * 128], lhsT=stT[:, j].rearrange("p s d -> p (s d)"), rhs=ident_f, start=True, stop=True),
                         reads=["stT", "ident_f"], writes=[PSK(2 * pr + j // 4)])
                p.op("dve", lambda v, pr=pr: v.tensor_copy(out=sto[0:8, 0:D], in_=pp[pr][0:8, :]), reads=[PSK(2 * pr), PSK(2 * pr + 1)], writes=["sto"])
                p.dma(ns_out, sto[0:8, 0:D], reads=["sto"], writes=["ns_out"])

        def sto_ps(b, j):
            pr = pp[b // 2]
            return pr[0:8, j * 128:(j + 1) * 128]

        def stage_exchange():
            for d_ in range(2):
                Ad = allg[:, 0:4, 2 * d_, :]; Hd = allg[:, 0:4, 2 * d_ + 1, :]
                Ap = cand[:, 2 * d_, 0:4]; Hp = cand[:, 2 * d_ + 1, 0:4]
                p.op("dve", lambda v, Ad=Ad, Ap=Ap, d_=d_: v.scalar_tensor_tensor(out=Ap, in0=Ad, scalar=-1.0, in1=msk[:, d_, 0:4], op0=ALU.add, op1=ALU.mult), reads=["allg", "msk"], writes=["cand"])
                p.op("dve", lambda v, Ap=Ap: v.tensor_scalar(out=Ap, in0=Ap, scalar1=1.0, scalar2=None, op0=ALU.add), reads=["cand"], writes=["cand"])
                p.op("dve", lambda v, Hd=Hd, Hp=Hp, d_=d_: v.tensor_tensor(out=Hp, in0=Hd, in1=msk[:, d_, 0:4], op=ALU.mult), reads=["allg", "msk"], writes=["cand"])
                p.op("dve", lambda v, d_=d_: v.tensor_copy(out=h0t[:, d_, :], in_=st0[:, d_, :]), reads=["st0"], writes=["h0t"])
                for r in (range(4) if d_ == 0 else range(3, -1, -1)):
                    p.op("dve", lambda v, d_=d_, r=r, Ap=Ap: v.tensor_tensor(out=h0t[:, d_, :], in0=h0t[:, d_, :], in1=Ap[:, r, :], op=ALU.mult), reads=["h0t", "cand"], writes=["h0t"])
                    p.op("dve", lambda v, d_=d_, r=r, Hp=Hp: v.tensor_tensor(out=h0t[:, d_, :], in0=h0t[:, d_, :], in1=Hp[:, r, :], op=ALU.add), reads=["h0t", "cand"], writes=["h0t"])

        def stage_G(half):
            p.dma(gsgu, gsgu_bd, writes=["gsgu"])
            for u in range(4):
                wsl, wk = wunit((w_st[u], None))
                wu = wsl.rearrange("p (u k n) -> p u k n", u=2, k=8)
                for uu in range(2):
                    j = 2 * u + uu
                    for blk in range(2):
                        b = nb()
                        mm_st(b, wu[:, uu], lambda kc, blk=blk: hT[:, kc, blk * 512:(blk + 1) * 512], HTB(blk), wk)
                        p.op("act", lambda a, b=b, j=j, blk=blk: a.activation(out=gu[:, j, blk * 512:(blk + 1) * 512], in_=bank(b), func=AF.Gelu_apprx_tanh),
                             reads=[PSK(b)], writes=[("gu", j)])
            for i_ in range(4):
                stage_piece(w_v[i_ * 256:(i_ + 1) * 256, :].rearrange("(k p) n -> p k n", p=128), wv[:, 2 * i_:2 * i_ + 2, :], ("wv", i_))
            for t in range(NT):
                s = t % 2
                pr = npair()
                for kc in range(8):
                    for hf in range(2):
                        p.op("pe", lambda t_, kc=kc, hf=hf, t=t, pr=pr: t_.matmul(pp[pr][:, hf * 512:(hf + 1) * 512], lhsT=hT[:, kc, t * 128:(t + 1) * 128], rhs=wv[:, kc, hf * 512:(hf + 1) * 512], start=(kc == 0), stop=(kc == 7)),
                             reads=[("hT", t), "wv"], writes=[PSK(2 * pr + hf)])
                scs = []
                for hf in range(2):
                    p.op("act", lambda a, hf=hf, s=s, pr=pr: a.activation(out=gv[:, s, hf * 512:(hf + 1) * 512], in_=pp[pr][:, hf * 512:(hf + 1) * 512], func=AF.Gelu_apprx_tanh),
                         reads=[PSK(2 * pr + hf)], writes=[("gv", s)])
                sc, sk = sscol()
                p.op("act", lambda a, s=s, sc=sc: a.activation(out=junk, in_=gv[:, s, :], func=AF.Square, accum_out=sc[:, 0:1]), reads=[("gv", s)], writes=["junk", sk])
                rr, rk = rstd([sc[:, 0:1]], [sk], 128, 1.0 / D, EPS)
                p.op("dve", lambda v, s=s, rr=rr: v.scalar_tensor_tensor(out=vn[:, s, :], in0=gv[:, s, :], scalar=rr, in1=gsgu, op0=ALU.mult, op1=ALU.mult),
                     reads=[("gv", s), rk, "gsgu"], writes=[("vn", s)])
                pr2 = npair()
                ps3 = pp[pr2].rearrange("p (g q) -> p g q", g=8)
                for g in range(8):
                    kk = [PSK(2 * pr2 + g // 4)]
                    p.op("pe", lambda t_, g=g, s=s: t_.matmul(ps3[:, g, :], lhsT=vn[:, s, g * 128:(g + 1) * 128], rhs=wsp[:, g, :], start=True, stop=False),
                         reads=[("vn", s), "wsp"], writes=kk)
                    p.op("pe", lambda t_, g=g: t_.matmul(ps3[:, g, :], lhsT=ones_b[0:1, 0:128], rhs=bhi[0:1, g * 128:(g + 1) * 128], start=False, stop=False),
                         reads=["ones_b", "bhi"], writes=kk)
                    p.op("pe", lambda t_, g=g: t_.matmul(ps3[:, g, :], lhsT=ones_b[0:1, 0:128], rhs=blo[0:1, g * 128:(g + 1) * 128], start=False, stop=True),
                         reads=["ones_b", "blo"], writes=kk)
                for hf in range(2):
                    p.op("dve", lambda v, hf=hf, t=t: v.tensor_tensor(out=yg[:, 4 * hf:4 * hf + 4, t * 128:(t + 1) * 128], in0=ps3[:, 4 * hf:4 * hf + 4, :], in1=gu[:, 4 * hf:4 * hf + 4, t * 128:(t + 1) * 128], op=ALU.mult),
                         reads=[PSK(2 * pr2 + hf)] + [("gu", 4 * hf + i) for i in range(4)], writes=[("yg", t)])

        def stage_M1(half):
            YGB = lambda blk: [("yg", 4 * blk + i) for i in range(4)]
            for c in range(8):
                wsl, wk = wunit((w_st[24 + c], None))
                wg = wsl.rearrange("p (u k n) -> p u k n", u=2, k=8)
                wsl2, wk2 = wunit((w_st[32 + c], None))
                wb = wsl2.rearrange("p (u k n) -> p u k n", u=2, k=8)
                for blk in range(2):
                    s = blk
                    cs = slice(blk * 512, (blk + 1) * 512)
                    b = nb()
                    mm_st(b, wg[:, 0], lambda kc, cs=cs: hT[:, kc, cs], HTB(blk), wk)
                    p.op("act", lambda a, b=b, s=s: a.activation(out=ta[:, s, :], in_=bank(b), func=AF.Tanh, scale=0.5), reads=[PSK(b)], writes=[("ta", s)])
                    b = nb()
                    mm_st(b, wg[:, 1], lambda kc, cs=cs: hT[:, kc, cs], HTB(blk), wk)
                    p.op("act", lambda a, b=b, s=s: a.activation(out=tb[:, s, :], in_=bank(b), func=AF.Tanh, scale=0.5), reads=[PSK(b)], writes=[("tb", s)])
                    b = nb()
                    mm_st(b, wb[:, 0], lambda kc, cs=cs: yg[:, kc, cs], YGB(blk), wk2)
                    p.op("dve", lambda v, b=b, s=s: v.scalar_tensor_tensor(out=m1[:, s, :], in0=ta[:, s, :], scalar=1.0, in1=bank(b), op0=ALU.add, op1=ALU.mult),
                         reads=[("ta", s), PSK(b)], writes=[("m1", s)])
                    b = nb()
                    mm_st(b, wb[:, 1], lambda kc, cs=cs: yr[:, kc, cs], [("yr", i) for i in range(8)], wk2)
                    p.op("dve", lambda v, b=b, s=s: v.scalar_tensor_tensor(out=m2[:, s, :], in0=tb[:, s, :], scalar=1.0, in1=bank(b), op0=ALU.add, op1=ALU.mult),
                         reads=[("tb", s), PSK(b)], writes=[("m2", s)])
                    p.op("dve", lambda v, s=s, c=c, cs=cs: v.tensor_tensor(out=mg[:, c, cs], in0=m1[:, s, :], in1=m2[:, s, :], op=ALU.add),
                         reads=[("m1", s), ("m2", s)], writes=[("mg", c)])

        def stage_M2(half):
            r = half
            for i_ in range(4):
                stage_piece(w_out[i_ * 256:(i_ + 1) * 256, :].rearrange("(k p) n -> p k n", p=128), wo[:, 2 * i_:2 * i_ + 2, :], ("wo", i_))
            for t in range(NT):
                s = t % 2
                p.dma(xt[:, s, :], xin[half * T + t * 128: half * T + (t + 1) * 128, :], writes=[("xt", s)])
                pr = npair()
                for c in range(8):
                    for hf in range(2):
                        p.op("pe", lambda t_, c=c, hf=hf, t=t, pr=pr: t_.matmul(pp[pr][:, hf * 512:(hf + 1) * 512], lhsT=mg[:, c, t * 128:(t + 1) * 128], rhs=wo[:, c, hf * 512:(hf + 1) * 512], start=(c == 0), stop=(c == 7)),
                             reads=[("mg", i) for i in range(8)] + ["wo"], writes=[PSK(2 * pr + hf)])
                cols = []
                for hf in range(2):
                    sc, sk = sscol()
                    cols.append((sc, sk))
                    p.op("act", lambda a, hf=hf, pr=pr, sc=sc: a.activation(out=junk[:, 0:512], in_=pp[pr][:, hf * 512:(hf + 1) * 512], func=AF.Square, accum_out=sc[:, 0:1]),
                         reads=[PSK(2 * pr + hf)], writes=["junk", sk])
                rr, rk = rstd([cols[0][0][:, 0:1], cols[1][0][:, 0:1]], [cols[0][1], cols[1][1]], 128, 1.0 / D, 4.0 * EPS)
                for hf in range(2):
                    cs = slice(hf * 512, (hf + 1) * 512)
                    p.op("dve", lambda v, cs=cs, pr=pr, rr=rr, r=r: v.scalar_tensor_tensor(out=tmp[:, cs], in0=pp[pr][:, cs], scalar=rr, in1=gate[:, 0, r, cs], op0=ALU.mult, op1=ALU.mult),
                         reads=[PSK(2 * pr + hf), rk, ("gate", r)], writes=["tmp"])
                p.op("dve", lambda v, t=t, s=s: v.tensor_tensor(out=x1[:, t, :], in0=tmp, in1=xt[:, s, :], op=ALU.add), reads=["tmp", ("xt", s)], writes=[("x1", t)])
                norm_transpose(x1[:, t, :], ("x1", t), 128, r, 1, h2T, ("h2T", t), t * 128, xn[:, s, :], ("xn", s))

        def stage_F(half):
            r = half
            H2B = lambda blk: [("h2T", 4 * blk + i) for i in range(4)]
            for u in range(16):
                wsl, wk = wunit((w_ff1[u], None))
                w1 = wsl.rearrange("p (u k n) -> p u k n", u=2, k=8)
                for uu in range(2):
                    j = 2 * u + uu
                    for blk in range(2):
                        s = blk
                        b = nb()
                        mm_st(b, w1[:, uu], lambda kc, blk=blk: h2T[:, kc, blk * 512:(blk + 1) * 512], H2B(blk), wk)
                        p.op("act", lambda a, b=b, s=s: a.activation(out=rt[:, s, :], in_=bank(b), func=AF.Relu), reads=[PSK(b)], writes=[("rt", s)])
                        p.op("dve", lambda v, s=s, j=j, blk=blk: v.tensor_tensor(out=f1T[:, j, blk * 512:(blk + 1) * 512], in0=rt[:, s, :], in1=rt[:, s, :], op=ALU.mult),
                             reads=[("rt", s)], writes=[("f1T", j)])
            import os as _os3
            kf = int(_os3.environ.get("KF", "9"))
            if kf < 2:
                return
            sscols = {}
            for ps_ in range(2):
                for jg in range(8):
                    src = w_ff2[jg * 512:(jg + 1) * 512, ps_ * 512:(ps_ + 1) * 512].rearrange("(a p) n -> p a n", p=128)
                    wsl, wk = wunit((src, None))
                    w2 = wsl.rearrange("p (a n) -> p a n", a=4)
                    for jj in range(4):
                        j = 4 * jg + jj
                        for t in range(NT):
                            p.op("pe", lambda t_, j=j, jj=jj, t=t, w2=w2: t_.matmul(bank(t), lhsT=f1T[:, j, t * 128:(t + 1) * 128], rhs=w2[:, jj, :], start=(j == 0), stop=(j == 31)),
                                 reads=[("f1T", j), wk], writes=[PSK(t)])
                for t in range(NT):
                    cs = slice(ps_ * 512, (ps_ + 1) * 512)
                    sc, sk = sscol()
                    sscols[(t, ps_)] = (sc, sk)
                    p.op("act", lambda a, t=t, sc=sc: a.activation(out=junk[:, 0:512], in_=bank(t), func=AF.Square, accum_out=sc[:, 0:1]), reads=[PSK(t)], writes=["junk", sk])
                    p.op("dve", lambda v, t=t, cs=cs: v.tensor_copy(out=f2[:, t, cs], in_=bank(t)), reads=[PSK(t), sk], writes=[("f2", t)])
            if kf < 3:
                return
            for t in range(NT):
                (sa, ka), (sb_, kb) = sscols[(t, 0)], sscols[(t, 1)]
                rr, rk = rstd([sa[:, 0:1], sb_[:, 0:1]], [ka, kb], 128, 1.0 / D, EPS)
                p.op("dve", lambda v, t=t, rr=rr, r=r: v.scalar_tensor_tensor(out=f2[:, t, :], in0=f2[:, t, :], scalar=rr, in1=gate[:, 1, r, :], op0=ALU.mult, op1=ALU.mult),
                     reads=[("f2", t), rk, ("gate", 2 + r)], writes=[("f2", t)])
                p.op("dve", lambda v, t=t: v.tensor_tensor(out=f2[:, t, :], in0=f2[:, t, :], in1=x1[:, t, :], op=ALU.add), reads=[("f2", t), ("x1", t)], writes=[("f2", t)])
                p.dma(y_out[half * T + t * 128: half * T + (t + 1) * 128, :], f2[:, t, :], reads=[("f2", t)], writes=["y_out"])

        import os as _os
        lim = int(_os.environ.get("KLIM", "99"))
        seq = [(stage_consts, ())]
        for c_ in range(4):
            seq += [(stage_A, (1, c_)), (stage_L, (1, True, c_))]
        seq += [(stage_exchange, ())]
        for half in (0, 1):
            seq += [(stage_A, (half,)), (stage_L, (half, False)), (stage_G, (half,)), (stage_M1, (half,)), (stage_M2, (half,)), (stage_F, (half,))]
        for fn_, args_ in seq[:lim]:
            fn_(*args_)
        p.wait_all("sp", ["y_out", "ns_out"])

    dry = Prog(record_only=True)
    dry.wlist = []
    dry.overlaps = ar.ov
    emit(dry, None)
    prog = Prog()
    prog.overlaps = ar.ov
    emit(prog, dry.wlist)
    with ExitStack() as stack:
        prog.build(nc, stack)
    return nc


_NC_CACHE = {}


def _f32(a):
    return np.ascontiguousarray(np.asarray(a, dtype=np.float32))


def _colT(v):
    return np.asarray(v, np.float32).reshape(8, 128).T


def kernel(x_prompt, x_sample, state_lru, c, c_ctx, w_ada, b_ada, g_pre_mix, g_post_mix, g_pre_mlp, g_post_mlp,
           w_in, g_sgu, w_sp, b_sp, conv_w, conv_b, w_ra, b_ra, w_ri, b_ri, lam, w_br_g, w_br_r, w_out, w_ff1, w_ff2):
    f = lambda a: np.asarray(a, dtype=np.float32)
    x_prompt, x_sample, state_lru, c, c_ctx = f(x_prompt), f(x_sample), f(state_lru), f(c), f(c_ctx)
    w_in0 = f(w_in)[0]

    def st_unit(w, cb0, cb1):
        blocks = []
        for cbk in (cb0, cb1):
            mat, col = cbk
            blk = mat[:, col * 128:(col + 1) * 128].reshape(8, 128, 128).transpose(1, 0, 2)
            blocks.append(blk)
        return np.stack(blocks, axis=1).reshape(128, 2048)

    brg, brr = f(w_br_g)[0], f(w_br_r)[0]
    units = []
    for u in range(4):
        units.append(st_unit(None, (w_in0, 2 * u), (w_in0, 2 * u + 1)))
    for u in range(4):
        units.append(np.zeros((128, 2048), np.float32))
    for j in range(8):
        units.append(st_unit(None, (w_in0, 16 + j), (w_in0, 24 + j)))
    for j in range(8):
        units.append(np.zeros((128, 2048), np.float32))
    for cc in range(8):
        units.append(st_unit(None, (w_in0, 32 + cc), (w_in0, 40 + cc)))
    for cc in range(8):
        units.append(st_unit(None, (brg, cc), (brr, cc)))
    for _ in range(4):
        units.append(np.zeros((128, 2048), np.float32))
    w_st = np.ascontiguousarray(np.stack(units, 0))
    ff1 = f(w_ff1)[0]
    w_ff1_u = np.ascontiguousarray(np.stack([st_unit(None, (ff1, 2 * u), (ff1, 2 * u + 1)) for u in range(16)], 0))
    w_v = np.ascontiguousarray(w_in0[:, 1024:2048])

    bdm = np.zeros((128, 4, 8, 128), np.float32)
    for g, (wm, d_) in enumerate(((w_ra, 0), (w_ra, 1), (w_ri, 0), (w_ri, 1))):
        wmat = f(wm)[0, d_]
        for j in range(8):
            for h in range(2):
                bdm[64 * h:64 * h + 64, g, j, 64 * h:64 * h + 64] = wmat[2 * j + h]
    hbT = np.stack([_colT(f(b_ra)[0, 0]), _colT(f(b_ra)[0, 1]), _colT(f(b_ri)[0, 0]), _colT(f(b_ri)[0, 1])], 1).reshape(128, 32)
    cwT = np.stack([_colT(f(conv_w)[0, k]) for k in range(4)], 2).reshape(128, 32)
    cbT = _colT(f(conv_b)[0])
    lamT = np.stack([_colT(f(lam)[0, 0]), _colT(f(lam)[0, 1])], 1).reshape(128, 16)
    wspT = np.ascontiguousarray(f(w_sp)[0].transpose(2, 0, 1)).reshape(128, 1024)
    gvecT = np.stack([_colT(f(g_pre_mix)[0]), _colT(f(g_pre_mlp)[0])], 1).reshape(128, 16)
    gpost_b = np.ascontiguousarray(np.broadcast_to(np.concatenate([f(g_post_mix)[0], f(g_post_mlp)[0]])[None, :], (128, 2048)))
    gsgu_b = np.ascontiguousarray(np.broadcast_to(f(g_sgu)[0][None, :], (128, 1024)))
    b_ada2 = np.ascontiguousarray(np.broadcast_to(f(b_ada)[0][None, :], (2, 6144)))
    kc_sel = np.zeros((2, 256), np.float32)
    kc_sel[0, 0:128] = 1.0
    kc_sel[1, 128:256] = 1.0
    shared = dict(w_ada=_f32(f(w_ada)[0]), b_ada2=b_ada2, gvecT=_f32(gvecT), gpost_b=gpost_b, gsgu_b=gsgu_b, lamT=_f32(lamT),
                  hbT=_f32(hbT), cwT=_f32(cwT), cbT=_f32(cbT), bd_w=_f32(bdm.reshape(128, 4096)), wspT=_f32(wspT),
                  bsp=_f32(f(b_sp)[0].reshape(1, 1024)), w_st=w_st, w_v=w_v, w_out=_f32(f(w_out)[0]), w_ff1=w_ff1_u,
                  w_ff2=_f32(f(w_ff2)[0]), kc_sel=kc_sel)
    in_maps = []
    for r in range(8):
        b, q = r // 4, r % 4
        xp = x_prompt[4 * r:4 * r + 4].reshape(1024, 1024)
        xs = x_sample[b, 1024 * q:1024 * (q + 1)]
        xin = np.ascontiguousarray(np.concatenate([xp, xs], 0))
        xhalo = np.zeros((3, 1024), np.float32)
        hm = np.zeros((128, 3), np.float32)
        for i, tok in enumerate((1024 * q - 2, 1024 * q - 1, 1024 * (q + 1))):
            if 0 <= tok < 4096:
                xhalo[i] = x_sample[b, tok]
                hm[:, i] = 1.0
        cselT = np.stack([_colT(c_ctx), _colT(c[b])], 2).reshape(128, 16)
        st0T = np.stack([_colT(state_lru[b, 0, 0]), _colT(state_lru[b, 0, 1])], 1).reshape(128, 16)
        mf = np.zeros((128, 8, 8), np.float32)
        mb = np.zeros((128, 8, 8), np.float32)
        for rr in range(4):
            if rr < q:
                mf[:, rr, :] = 1.0
            if rr > q:
                mb[:, rr, :] = 1.0
        m = dict(shared)
        m.update(xs_all=np.ascontiguousarray(x_sample[b]), xin=xin, xhalo=xhalo, hmask=hm, cselT=_f32(cselT), st0T=_f32(st0T), mskf=mf.reshape(128, 64), mskb=mb.reshape(128, 64))
        in_maps.append(m)
    if "nc" not in _NC_CACHE:
        _NC_CACHE["nc"] = build_program()
    res = run_bass_kernel_spmd(_NC_CACHE["nc"], in_maps, core_ids=list(range(8)))
    y_prompt = np.zeros((32, 256, 1024), np.float32)
    y_sample = np.zeros((2, 4096, 1024), np.float32)
    new_state = np.zeros((32, 1, 2, 1024), np.float32)
    for r in range(8):
        out = res.results[r]
        yo = np.asarray(out["y_out"], np.float32)
        y_prompt[4 * r:4 * r + 4] = yo[0:1024].reshape(4, 256, 1024)
        y_sample[r // 4, 1024 * (r % 4):1024 * (r % 4 + 1)] = yo[1024:2048]
        new_state[4 * r:4 * r + 4, 0] = np.asarray(out["ns_out"], np.float32).reshape(4, 2, 1024)
    return y_prompt, y_sample, new_state
```

```python
from contextlib import ExitStack
import numpy as np
import concourse.bass as bass
import concourse.mybir as mybir
from concourse.bass_utils import run_bass_kernel_spmd

F32 = mybir.dt.float32
BF16 = mybir.dt.bfloat16
I32 = mybir.dt.int32
AF = mybir.ActivationFunctionType
ALU = mybir.AluOpType
AP = bass.AP

D = 1024
T = 1024
NT = 8
EPS = 1e-6
ENG_NAMES = ("pe", "act", "dve", "pool", "sp")


class Prog:
    def __init__(self, n_dma_sems=28, record_only=False):
        self.streams = {e: [] for e in ENG_NAMES}
        self.count = {e: 0 for e in ENG_NAMES}
        self.n_dma_sems = n_dma_sems
        self.dma_cnt = [0] * n_dma_sems
        self.dma_rr = 0
        self.cc_cnt = 0
        self.state = {}
        self.waited = {e: {} for e in ENG_NAMES}
        self.overlaps = {}
        self.record_only = record_only
        self.sw_ids = {}
        self.sw_gen = {}

    def _split(self, key):
        return key if isinstance(key, tuple) else (key, None)

    def _own_states(self, name, idx, create):
        d = self.state.setdefault(name, {})
        if idx is None:
            if create and None not in d:
                d[None] = [{}, {}]
            return list(d.values())
        if idx not in d and create:
            if None in d:
                d[idx] = [dict(d[None][0]), dict(d[None][1])]
            else:
                d[idx] = [{}, {}]
        out = []
        if idx in d:
            out.append(d[idx])
        if None in d:
            out.append(d[None])
        return out

    def _dep_states(self, key):
        name, idx = self._split(key)
        out = self._own_states(name, idx, False)
        for o in self.overlaps.get(name, ()):
            out.extend(self.state.get(o, {}).values())
        return out

    def _deps(self, reads, writes):
        need = {}

        def add(d):
            for src, seq in d.items():
                if need.get(src, -1) < seq:
                    need[src] = seq
        for k in reads:
            for st in self._dep_states(k):
                add(st[0])
        for k in writes:
            for st in self._dep_states(k):
                add(st[0])
                add(st[1])
        return need

    def _commit(self, me, reads, writes):
        src, seq = me
        for k in reads:
            name, idx = self._split(k)
            sts = self._own_states(name, idx, True)
            if idx is None:
                for st in sts:
                    st[1][src] = seq
            else:
                sts[0][1][src] = seq
        for k in writes:
            name, idx = self._split(k)
            sts = self._own_states(name, idx, True)
            tgt = sts if idx is None else sts[:1]
            for st in tgt:
                st[0] = {src: seq}
                st[1] = {}

    def _emit_waits(self, eng, need):
        w = self.waited[eng]
        for src, seq in need.items():
            if src == eng and eng == "pe":
                continue
            if w.get(src, 0) >= seq:
                continue
            w[src] = seq
            self.streams[eng].append(("wait", src, seq))

    def op(self, eng, fn, reads=(), writes=()):
        need = self._deps(reads, writes)
        self._emit_waits(eng, need)
        self.count[eng] += 1
        me = (eng, self.count[eng])
        self.streams[eng].append(("ins", fn))
        self._commit(me, reads, writes)
        return me

    def dma(self, out, in_, reads=(), writes=(), queue="sp", **kw):
        need = self._deps(reads, writes)
        if queue == "pool":
            dk = writes[0]
            if dk not in self.sw_ids:
                self.sw_ids[dk] = len(self.sw_ids)
                self.sw_gen[dk] = 0
            sid = self.sw_ids[dk]
            self.sw_gen[dk] += 1
            self._emit_waits(queue, need)
            self.count["pool"] += 1
            marker = self.count["pool"]
            self.streams[queue].append(("swdma", out, in_, sid, kw, getattr(self, "last_marker", 0)))
            self.last_marker = marker
            src = ("sw", sid, self.sw_gen[dk])
            for k in reads:
                name, idx = self._split(k)
                for st in (self._own_states(name, idx, True) if idx is None else self._own_states(name, idx, True)[:1]):
                    st[1][src] = 16
            for k in writes:
                name, idx = self._split(k)
                sts = self._own_states(name, idx, True)
                for st in (sts if idx is None else sts[:1]):
                    st[0] = {src: 16, "pool": marker}
                    st[1] = {}
            return (src, 16)
        k = self.dma_rr
        self.dma_rr = (self.dma_rr + 1) % self.n_dma_sems
        src = ("dma", k)
        if self.dma_cnt[k] > 0:
            need[src] = max(need.get(src, 0), self.dma_cnt[k])
        self._emit_waits(queue, need)
        self.dma_cnt[k] += 16
        me = (src, self.dma_cnt[k])
        self.streams[queue].append(("dma", out, in_, k, kw))
        self._commit(me, reads, writes)
        return me

    def cc(self, fn, reads=(), writes=()):
        need = self._deps(reads, writes)
        self._emit_waits("pool", need)
        self.cc_cnt += 1
        me = ("cc", self.cc_cnt)
        self.streams["pool"].append(("cc", fn))
        self._commit(me, reads, writes)
        return me

    def wait_all(self, eng, keys):
        self._emit_waits(eng, self._deps(keys, ()))

    EPOCH = 1024

    def build(self, nc, stack):
        E = self.EPOCH
        sem = {e: [stack.enter_context(nc.semaphore(f"s_{e}{i}")) for i in range(self.count[e] // E + 1)] for e in ENG_NAMES}
        dsem = [stack.enter_context(nc.semaphore(f"s_dma{k}")) for k in range(self.n_dma_sems)]
        cc_sem = stack.enter_context(nc.semaphore("s_cc"))
        block = stack.enter_context(nc.Block())
        deco = {"pe": block.tensor, "act": block.scalar, "dve": block.vector,
                "pool": block.gpsimd, "sp": block.sync}

        def semval(src, seq):
            if src == "cc":
                return cc_sem, seq
            if isinstance(src, tuple):
                return dsem[src[1]], seq
            return sem[src][(seq - 1) // E], (seq - 1) % E + 1

        for e in ENG_NAMES:
            def body(eng, items=self.streams[e], e=e):
                n = 0
                for it in items:
                    if it[0] == "wait":
                        s_, v_ = semval(it[1], it[2])
                        eng.wait_ge(s_, v_)
                    elif it[0] == "ins":
                        it[1](eng).then_inc(sem[e][n // E], 1)
                        n += 1
                    elif it[0] == "cc":
                        it[1](eng).then_inc(cc_sem)
                    else:
                        _, out, in_, k, kw = it
                        eng.dma_start(out=out, in_=in_, **kw).then_inc(dsem[k], 16)
            deco[e](body)


ALL_PH = ("C", "A", "L", "G", "M1", "M2", "F1", "F2")


class Arena:
    def __init__(self):
        self.bufs = []

    def add(self, name, nbytes, phases):
        nbytes = (nbytes + 63) // 64 * 64
        self.bufs.append([name, nbytes, set(ALL_PH if phases == "*" else phases), None])

    def pack(self):
        order = sorted(self.bufs, key=lambda b: (-len(b[2]), -b[1]))
        placed = []
        for b in order:
            cands = sorted([0] + [p[3] + p[1] for p in placed])
            for off in cands:
                ok = True
                for p in placed:
                    if p[2] & b[2] and off < p[3] + p[1] and p[3] < off + b[1]:
                        ok = False
                        break
                if ok:
                    b[3] = off
                    break
            placed.append(b)
        self.total = max(b[3] + b[1] for b in self.bufs)
        self.off = {b[0]: b[3] for b in self.bufs}
        self.size = {b[0]: b[1] for b in self.bufs}
        ov = {}
        for a in self.bufs:
            for b in self.bufs:
                if a is not b and a[3] < b[3] + b[1] and b[3] < a[3] + a[1]:
                    ov.setdefault(a[0], []).append(b[0])
        self.ov = ov


def rev_ap(ap2d):
    apl = [list(d) for d in ap2d.ap]
    n = apl[-1][1]
    st = apl[-1][0]
    apl[-1] = [-st, n]
    return AP(ap2d.tensor, ap2d.offset + st * (n - 1), apl)


def build_program():
    nc = bass.Bass("TRN2", target_bir_lowering=False)

    def din(name, shape, dt=F32):
        return nc.dram_tensor(name, list(shape), dt, kind="ExternalInput").ap()

    def dout(name, shape, dt=F32):
        return nc.dram_tensor(name, list(shape), dt, kind="ExternalOutput").ap()

    xin = din("xin", [2048, D])
    xhalo = din("xhalo", [3, D])
    xs_all = din("xs_all", [4096, D])
    hmask = din("hmask", [128, 3])
    cselT = din("cselT", [128, 16])
    st0T = din("st0T", [128, 16])
    mskf = din("mskf", [128, 64])
    mskb = din("mskb", [128, 64])
    kc_sel = din("kc_sel", [2, 256])
    w_ada = din("w_ada", [D, 6 * D])
    b_ada2 = din("b_ada2", [2, 6 * D])
    gvecT = din("gvecT", [128, 16])
    gpost_b = din("gpost_b", [128, 2 * D])
    gsgu_bd = din("gsgu_b", [128, D])
    lamT = din("lamT", [128, 16])
    hbT = din("hbT", [128, 32])
    cwT = din("cwT", [128, 32])
    cbT = din("cbT", [128, 8])
    bd_w = din("bd_w", [128, 32 * 128])
    wspT = din("wspT", [128, 8 * 128])
    bsp = din("bsp", [1, D])
    w_st = din("w_st", [44, 128, 2048])
    w_v = din("w_v", [D, D])
    w_out = din("w_out", [D, D])
    w_ff1 = din("w_ff1", [16, 128, 2048])
    w_ff2 = din("w_ff2", [4 * D, D])
    y_out = dout("y_out", [2048, D])
    ns_out = dout("ns_out", [8, D])

    ar = Arena()
    A_ = ar.add
    A_("ident_b", 256, "*"); A_("ident_f", 512, "*"); A_("sel", 1024, "*"); A_("ones_b", 256, "*")
    A_("cT", 64, "*"); A_("sT", 64, "*"); A_("modT", 384, "*"); A_("modc", 256, "*")
    A_("lamc", 256, "*"); A_("hb", 128, "*"); A_("cw", 128, "*"); A_("cb", 32, "*")
    A_("bd", 8192, "*"); A_("wsp", 2048, "*"); A_("bspf", 4096, ["C"]); A_("bhi", 2048, "*"); A_("blo", 2048, "*")
    A_("gate", 16384, "*"); A_("gsgu", 4096, ["G"]); A_("hmask", 64, "*")
    A_("h0t", 64, "*"); A_("st0", 64, "*"); A_("msk", 512, "*"); A_("summ", 128, "*"); A_("allg", 1024, "*")
    A_("cand", 1024, "*"); A_("stT", 256, "*"); A_("sto", 4096, ["L"])
    A_("rs", 16 * 32, "*"); A_("ss", 512, "*"); A_("junk", 2048, "*"); A_("gvec", 64, "*")
    A_("ws", 4 * 4096, "*"); A_("wsf", 2 * 8192, "*")
    A_("wa", 2 * 16384, ["C"]); A_("modrow", 2 * 2048, ["C"]); A_("gpost", 8192, ["C"]); A_("bada", 2 * 2048, ["C"])
    A_("xa", 8 * 4096, ["A"]); A_("xna", 8 * 2048, ["A"]); A_("ssa", 64, "*"); A_("rsa", 5 * 64, "*")
    A_("xt", 2 * 4096, ["A", "M2"]); A_("xn", 2 * 2048, ["A", "M2"]); A_("xh", 4096, ["A"]); A_("xhn", 2048, ["A"])
    A_("hT", 8 * 1027 * 2, ["A", "L", "G", "M1"])
    for sfx in ("", "_b"):
        A_("xrp" + sfx, 1036 * 4, ["L"]); A_("xc" + sfx, 4096, ["L"]); A_("xcb" + sfx, 2048, ["L"]); A_("gg" + sfx, 4096, ["L"])
        for d_ in range(2):
            A_(f"tr{d_}" + sfx, 4096, ["L"]); A_(f"aa{d_}" + sfx, 4096, ["L"]); A_(f"ti{d_}" + sfx, 4096, ["L"]); A_(f"hh{d_}" + sfx, 4096, ["L"])
    A_("yr", 16384, ["L", "G", "M1"])
    A_("gu", 16384, ["G"]); A_("wv", 16384, ["G"]); A_("gv", 2 * 4096, ["G"]); A_("vn", 2 * 2048, ["G"])
    A_("yg", 16384, ["G", "M1"])
    A_("ta", 2 * 2048, ["M1"]); A_("tb", 2 * 2048, ["M1"]); A_("m1", 2 * 2048, ["M1"]); A_("m2", 2 * 2048, ["M1"])
    A_("mg", 16384, ["M1", "M2"])
    A_("wo", 16384, ["M2"]); A_("tmp", 4096, ["M2"])
    A_("x1", 32768, ["M2", "F1", "F2"]); A_("h2T", 16384, ["M2", "F1"])
    A_("f1T", 65536, ["F1", "F2"]); A_("rt", 2 * 1024, ["F1"]); A_("f2", 32768, ["F2"])
    ar.pack()
    import os as _os2
    if _os2.environ.get("KARENA"):
        for ph in ALL_PH:
            print(ph, sum(b[1] for b in ar.bufs if ph in b[2]))
        print("total", ar.total)
    assert ar.total <= 206 * 1024, ar.total
    arena = nc.alloc_sbuf_tensor("arena", [128, ar.total // 4], F32)

    def V(name, dt=F32, pattern=None, **kw):
        o = ar.off[name] // 4
        n = ar.size[name] // 4
        v = arena[:, o:o + n]
        if dt != F32:
            v = v.bitcast(dt)
        if pattern:
            v = v.rearrange(pattern, **kw)
        return v

    ident_b = V("ident_b", BF16); ident_f = V("ident_f"); sel = V("sel")[0:2, :]; ones_b = V("ones_b", BF16)
    cT = V("cT"); sT = V("sT"); modT = V("modT", F32, "p (c r) -> p c r", r=2)
    modc = V("modc", F32, "p (w r k) -> p w r k", w=4, r=2)
    lamc = V("lamc", F32, "p (w d j) -> p w d j", w=4, d=2)
    hb = V("hb", F32, "p (g j) -> p g j", g=4); cw = V("cw", F32, "p (j k) -> p j k", k=4); cb = V("cb")
    bd = V("bd", BF16, "p (g j m) -> p g j m", g=4, j=8); wsp = V("wsp", BF16, "p (g q) -> p g q", g=8)
    bspf = V("bspf"); bhi = V("bhi", BF16); blo = V("blo", BF16)
    gate = V("gate", F32, "p (w r n) -> p w r n", w=2, r=2); gsgu = V("gsgu"); hmk = V("hmask")
    h0t = V("h0t", F32, "p (d j) -> p d j", d=2); st0 = V("st0", F32, "p (d j) -> p d j", d=2)
    msk = V("msk", F32, "p (w r j) -> p w r j", w=2, r=8)
    summ = V("summ", F32, "p (q j) -> p q j", q=4); allg = V("allg", F32, "p (r q j) -> p r q j", r=8, q=4)
    cand = V("cand", F32, "p (w r j) -> p w r j", w=4, r=8)
    stT = V("stT", F32, "p (j s d) -> p j s d", j=8, s=4); sto = V("sto")
    rs = V("rs", F32, "p (s c) -> p s c", c=8); ssq = V("ss", F32, "p (s c) -> p s c", c=4)
    junk = V("junk", BF16); gvec = V("gvec", F32, "p (w k) -> p w k", w=2)
    ws = V("ws", BF16, "p (s n) -> p s n", s=4)
    wsf = V("wsf", F32, "p (s n) -> p s n", s=2)
    wa = V("wa", F32, "p (s k n) -> p s k n", s=2, k=8); modrow = V("modrow", F32, "p (s n) -> p s n", s=2)
    gpost = V("gpost", F32, "p (w n) -> p w n", w=2); bada = V("bada", F32, "p (s n) -> p s n", s=2)
    xt = V("xt", F32, "p (s n) -> p s n", s=2); xn = V("xn", BF16, "p (s n) -> p s n", s=2)
    xh = V("xh"); xhn = V("xhn", BF16)
    xa = V("xa", F32, "p (t n) -> p t n", t=8); xna = V("xna", BF16, "p (t n) -> p t n", t=8)
    ssa = V("ssa"); rsa = V("rsa", F32, "p (c n) -> p c n", c=5)
    hT = V("hT", BF16, "p (k n) -> p k n", k=8)
    LS = []
    for sfx in ("", "_b"):
        LS.append(dict(xrp=V("xrp" + sfx), xc=V("xc" + sfx), xcb=V("xcb" + sfx, BF16), gg=V("gg" + sfx),
                       trb=[V(f"tr{d_}" + sfx) for d_ in range(2)], aab=[V(f"aa{d_}" + sfx) for d_ in range(2)],
                       tib=[V(f"ti{d_}" + sfx) for d_ in range(2)], hhb=[V(f"hh{d_}" + sfx) for d_ in range(2)], sfx=sfx))
    yr = V("yr", BF16, "p (k n) -> p k n", k=8); yg = V("yg", BF16, "p (k n) -> p k n", k=8)
    gu = V("gu", BF16, "p (k n) -> p k n", k=8); wv = V("wv", BF16, "p (k n) -> p k n", k=8)
    gv = V("gv", F32, "p (s n) -> p s n", s=2); vn = V("vn", BF16, "p (s n) -> p s n", s=2)
    ta = V("ta", F32, "p (s n) -> p s n", s=2); tb = V("tb", F32, "p (s n) -> p s n", s=2)
    m1 = V("m1", F32, "p (s n) -> p s n", s=2); m2 = V("m2", F32, "p (s n) -> p s n", s=2)
    mg = V("mg", BF16, "p (k n) -> p k n", k=8); wo = V("wo", BF16, "p (k n) -> p k n", k=8)
    tmp = V("tmp"); x1 = V("x1", F32, "p (t n) -> p t n", t=8); h2T = V("h2T", BF16, "p (k n) -> p k n", k=8)
    f1T = V("f1T", BF16, "p (k n) -> p k n", k=32); rt = V("rt", BF16, "p (s n) -> p s n", s=2)
    f2 = V("f2", F32, "p (t n) -> p t n", t=8)

    pp = [nc.alloc_psum_tensor(f"pp{i}", [128, 1024], F32) for i in range(4)]

    def bank(b):
        return pp[b // 2][:, (b % 2) * 512:(b % 2) * 512 + 512]

    def emit(p, wplan):
        st = {"bank": 0, "pair": 0, "rs": 0, "ss": 0, "wi": 0, "issued": 0, "alt": 0, "sf": 0}

        def nb():
            b = st["bank"]; st["bank"] = (b + 1) % 8
            return b

        def npair():
            b = st["pair"]; st["pair"] = (b + 1) % 4
            return b

        def alt():
            st["alt"] ^= 1
            return "act" if st["alt"] else "dve"

        PSK = lambda b: ("ps", b)

        def stage_piece(src, dst, dkey):
            k = st["sf"]; st["sf"] = (k + 1) % 2
            n = 1
            for d_ in src.shape[1:]:
                n *= d_
            sfl = wsf[:, k, 0:n]
            sv = sfl
            if len(src.shape) == 3:
                sv = sfl.rearrange("p (a n) -> p a n", a=src.shape[1])
            p.dma(sv, src, writes=[("wsf", k)])
            dfl = dst
            p.op("pool", lambda g_: g_.tensor_copy(out=dfl, in_=(sv if len(dst.shape) == 3 else sfl)), reads=[("wsf", k)], writes=[dkey])

        def wunit(desc):
            i = st["wi"]; st["wi"] += 1
            if wplan is None:
                p.wlist.append(desc)
            else:
                while st["issued"] < min(len(wplan), i + 3):
                    k = st["issued"]; st["issued"] += 1
                    src = wplan[k][0]
                    dst = ws[:, k % 4, :]
                    if len(src.shape) == 3:
                        dst = dst.rearrange("p (a n) -> p a n", a=src.shape[1])
                    stage_piece(src, dst, ("ws", k % 4))
            return ws[:, i % 4, :], ("ws", i % 4)

        def rstd(ss_list, keys, np_, scale, eps):
            s = st["rs"]; st["rs"] = (s + 1) % 16
            R = rs[0:np_, s, :]
            k = ("rs", s)
            xx, tii, yy, hh_, zz = (R[:, c:c + 1] for c in range(5))
            if len(ss_list) == 2:
                p.op("dve", lambda v: v.tensor_tensor(out=xx, in0=ss_list[0], in1=ss_list[1], op=ALU.add), reads=keys, writes=[k])
                src = xx
                rk = [k]
            else:
                src = ss_list[0]
                rk = keys
            p.op("dve", lambda v: v.tensor_scalar(out=xx, in0=src, scalar1=scale, scalar2=eps, op0=ALU.mult, op1=ALU.add), reads=rk, writes=[k])
            p.op("dve", lambda v: v.tensor_scalar(out=tii.bitcast(I32), in0=xx.bitcast(I32), scalar1=1, scalar2=None, op0=ALU.arith_shift_right), reads=[k], writes=[k])
            p.op("dve", lambda v: v.tensor_scalar(out=yy.bitcast(I32), in0=tii.bitcast(I32), scalar1=-1, scalar2=0x5f3759df, op0=ALU.mult, op1=ALU.add), reads=[k], writes=[k])
            p.op("dve", lambda v: v.tensor_scalar(out=hh_, in0=xx, scalar1=0.5, scalar2=None, op0=ALU.mult), reads=[k], writes=[k])
            for _ in range(3):
                p.op("dve", lambda v: v.scalar_tensor_tensor(out=zz, in0=yy, scalar=yy, in1=hh_, op0=ALU.mult, op1=ALU.mult), reads=[k], writes=[k])
                p.op("dve", lambda v: v.tensor_scalar(out=zz, in0=zz, scalar1=-1.0, scalar2=1.5, op0=ALU.mult, op1=ALU.add), reads=[k], writes=[k])
                p.op("dve", lambda v: v.tensor_tensor(out=yy, in0=yy, in1=zz, op=ALU.mult), reads=[k], writes=[k])
            return yy, k

        def sscol():
            s = st["ss"]; st["ss"] = (s + 1) % 32
            return ssq[:, s, :], ("ss", s)

        def stage_consts():
            sp_loads = [(cT, cselT, "cT"), (st0.rearrange("p d j -> p (d j)"), st0T, "st0"),
                        (msk[:, 0].rearrange("p r j -> p (r j)"), mskf, ("msk", 0)), (msk[:, 1].rearrange("p r j -> p (r j)"), mskb, ("msk", 1)),
                        (sel, kc_sel, "sel"), (gvec.rearrange("p w k -> p (w k)"), gvecT, "gvec"),
                        (lamc[:, 3].rearrange("p d j -> p (d j)"), lamT, "lamc"), (hb.rearrange("p g j -> p (g j)"), hbT, "hb"),
                        (cw.rearrange("p j k -> p (j k)"), cwT, "cw"), (cb[:, 0:8], cbT, "cb"), (hmk[:, 0:3], hmask, "hmask"),
                        (bspf[0:1, 0:D], bsp, "bspf"),
                        (gpost.rearrange("p w n -> p (w n)"), gpost_b, "gpost")]
            for dst, src, key in sp_loads:
                p.dma(dst, src, writes=[key])
            bdf = bd.rearrange("p g j m -> p (g j m)")
            for i_ in range(2):
                stage_piece(bd_w[:, i_ * 2048:(i_ + 1) * 2048], bdf[:, i_ * 2048:(i_ + 1) * 2048], ("bd", i_))
            stage_piece(wspT, wsp.rearrange("p g q -> p (g q)"), "wsp")
            p.op("pool", lambda g_: g_.memset(ident_f, 0.0), writes=["ident_f"])
            p.op("pool", lambda g_: g_.affine_select(out=ident_f, in_=ident_f, compare_op=ALU.not_equal, fill=1.0, base=0,
                                                      pattern=[[-1, 128]], channel_multiplier=1), reads=["ident_f"], writes=["ident_f"])
            p.op("dve", lambda v: v.tensor_copy(out=ident_b, in_=ident_f), reads=["ident_f"], writes=["ident_b"])
            p.op("pool", lambda g_: g_.memset(ones_b, 1.0), writes=["ones_b"])
            p.op("pool", lambda g_: g_.memset(allg.rearrange("p r q j -> p (r q j)"), 0.0), writes=["allg"])
            B0 = bspf[0:1, 0:D]; B1 = bspf[0:1, D:2 * D] if False else None
            p.op("dve", lambda v: v.tensor_copy(out=bhi[0:1, 0:D], in_=B0), reads=["bspf"], writes=["bhi"])
            p.op("dve", lambda v: v.tensor_tensor(out=B0, in0=B0, in1=bhi[0:1, 0:D], op=ALU.subtract), reads=["bspf", "bhi"], writes=["bspf"])
            p.op("dve", lambda v: v.tensor_copy(out=blo[0:1, 0:D], in_=B0), reads=["bspf"], writes=["blo"])
            p.op("act", lambda a: a.activation(out=sT, in_=cT, func=AF.Tanh, scale=0.5), reads=["cT"], writes=["sT"])
            p.op("dve", lambda v: v.scalar_tensor_tensor(out=sT, in0=sT, scalar=1.0, in1=cT, op0=ALU.add, op1=ALU.mult), reads=["sT", "cT"], writes=["sT"])
            p.op("dve", lambda v: v.tensor_scalar(out=sT, in0=sT, scalar1=0.5, scalar2=None, op0=ALU.mult), reads=["sT"], writes=["sT"])
            sT3 = sT.rearrange("p (k r) -> p k r", r=2)
            L3 = lamc[:, 3]
            p.op("act", lambda a: a.activation(out=L3, in_=L3, func=AF.Exp, scale=-1.0), reads=["lamc"], writes=["lamc"])
            p.op("act", lambda a: a.activation(out=L3, in_=L3, func=AF.Ln, bias=1.0, scale=1.0), reads=["lamc"], writes=["lamc"])
            p.op("dve", lambda v: v.tensor_scalar(out=lamc[:, 0], in0=L3, scalar1=-8.0, scalar2=None, op0=ALU.mult), reads=["lamc"], writes=["lamc"])
            p.op("dve", lambda v: v.tensor_scalar(out=lamc[:, 1], in0=L3, scalar1=-4.0, scalar2=None, op0=ALU.mult), reads=["lamc"], writes=["lamc"])
            p.op("dve", lambda v: v.tensor_scalar(out=lamc[:, 2], in0=L3, scalar1=-4096.0, scalar2=None, op0=ALU.mult), reads=["lamc"], writes=["lamc"])
            p.op("dve", lambda v: v.tensor_scalar(out=hb.rearrange("p g j -> p (g j)"), in0=hb.rearrange("p g j -> p (g j)"), scalar1=0.5, scalar2=None, op0=ALU.mult), reads=["hb"], writes=["hb"])
            order = [2, 3, 0, 1, 4, 5, 6, 7, 8, 9, 10, 11]
            mt_bank = nb()
            mt_ps = bank(mt_bank)[:, 0:96].rearrange("p (c r) -> p c r", r=2)
            for n_, cbk in enumerate(order):
                s = n_ % 2
                p.dma(wa[:, s], w_ada[:, cbk * 512:(cbk + 1) * 512].rearrange("(k p) n -> p k n", p=128), writes=[("wa", s)])
                p.dma(bada[0:2, s, :], b_ada2[:, cbk * 512:(cbk + 1) * 512], writes=[("bada", s)])
                b = nb()
                if b == mt_bank:
                    b = nb()
                for kc in range(8):
                    p.op("pe", lambda t_, kc=kc, s=s, b=b: t_.matmul(bank(b)[0:2, :], lhsT=sT3[:, kc, :], rhs=wa[:, s, kc, :], start=(kc == 0), stop=(kc == 7)),
                         reads=["sT", ("wa", s)], writes=[PSK(b)])
                p.op("dve", lambda v, s=s, b=b: v.tensor_tensor(out=modrow[0:2, s, :], in0=bank(b)[0:2, :], in1=bada[0:2, s, :], op=ALU.add),
                     reads=[PSK(b), ("bada", s)], writes=[("modrow", s)])
                for i in range(4):
                    ci = cbk * 4 + i
                    p.op("pe", lambda t_, s=s, i=i, ci=ci: t_.matmul(mt_ps[:, ci, :], lhsT=modrow[0:2, s, i * 128:(i + 1) * 128], rhs=ident_f[0:2, 0:2], start=True, stop=True),
                         reads=[("modrow", s), "ident_f"], writes=[PSK(mt_bank)])
                if cbk in (4, 5, 10, 11):
                    w_ = 0 if cbk < 6 else 1
                    hf = cbk % 2
                    for r in range(2):
                        b2 = nb()
                        if b2 == mt_bank:
                            b2 = nb()
                        p.op("pe", lambda t_, s=s, r=r, b2=b2: t_.matmul(bank(b2), lhsT=sel[:, r * 128:(r + 1) * 128], rhs=modrow[0:2, s, :], start=True, stop=True),
                             reads=[("modrow", s), "sel"], writes=[PSK(b2)])
                        p.op("dve", lambda v, w_=w_, r=r, hf=hf, b2=b2: v.tensor_tensor(out=gate[:, w_, r, hf * 512:(hf + 1) * 512], in0=bank(b2), in1=gpost[:, w_, hf * 512:(hf + 1) * 512], op=ALU.mult),
                             reads=[PSK(b2), "gpost"], writes=[("gate", w_ * 2 + r)])
            p.op("dve", lambda v: v.tensor_copy(out=modT, in_=mt_ps), reads=[PSK(mt_bank)], writes=["modT"])
            for r in range(2):
                for w_, (sc_i, sh_i) in enumerate(((8, 0), (32, 24))):
                    p.op("dve", lambda v, r=r, w_=w_, sc_i=sc_i: v.scalar_tensor_tensor(out=modc[:, 2 * w_, r, :], in0=modT[:, sc_i:sc_i + 8, r], scalar=1.0, in1=gvec[:, w_, :], op0=ALU.add, op1=ALU.mult),
                         reads=["modT", "gvec"], writes=["modc"])
                    p.op("dve", lambda v, r=r, w_=w_, sh_i=sh_i: v.tensor_copy(out=modc[:, 2 * w_ + 1, r, :], in_=modT[:, sh_i:sh_i + 8, r]),
                         reads=["modT"], writes=["modc"])

        def norm_transpose(src_tile, src_key, np_, r, widx, dstT, dst_key, col0, xn_tile, xn_key):
            sc, sk = sscol()
            p.op("act", lambda a: a.activation(out=junk[0:np_, :], in_=src_tile, func=AF.Square, accum_out=sc[0:np_, 0:1]),
                 reads=[src_key], writes=["junk", sk])
            rr, rk = rstd([sc[0:np_, 0:1]], [sk], np_, 1.0 / D, EPS)
            p.op("act", lambda a: a.activation(out=xn_tile, in_=src_tile, func=AF.Identity, scale=rr), reads=[src_key, rk], writes=[xn_key])
            transpose_evac(xn_tile, xn_key, np_, r, widx, dstT, dst_key, col0)

        def transpose_evac(xn_tile, xn_key, np_, r, widx, dstT, dst_key, col0):
            b = nb()
            pt = bank(b).bitcast(BF16).rearrange("p (k n) -> p k n", k=8)
            for kc in range(8):
                p.op("pe", lambda t_, kc=kc: t_.transpose(pt[:, kc, 0:np_], xn_tile[:, kc * 128:(kc + 1) * 128], ident_b[0:np_, 0:np_]),
                     reads=[xn_key, "ident_b"], writes=[PSK(b)])
            use_act = alt() == "act"
            for kc in range(8):
                scl = modc[:, 2 * widx, r, kc:kc + 1]; shf = modc[:, 2 * widx + 1, r, kc:kc + 1]
                o = dstT[:, kc, col0:col0 + np_]
                if kc % 2 == 0:
                    p.op("act", lambda a, o=o, kc=kc, scl=scl, shf=shf: a.activation(out=o, in_=pt[:, kc, 0:np_], func=AF.Identity, scale=scl, bias=shf),
                         reads=[PSK(b), "modc"], writes=[dst_key])
                else:
                    p.op("dve", lambda v, o=o, kc=kc, scl=scl, shf=shf: v.tensor_scalar(out=o, in0=pt[:, kc, 0:np_], scalar1=scl, scalar2=shf, op0=ALU.mult, op1=ALU.add),
                         reads=[PSK(b), "modc"], writes=[dst_key])

        def stage_A(half, chunk=None):
            r = half
            src, base = (xin, half * T) if chunk is None else (xs_all, chunk * T)
            for t in range(NT):
                p.dma(xa[:, t, :], src[base + t * 128: base + (t + 1) * 128, :], writes=[("xa", t)])
            for t in range(NT):
                p.op("act", lambda a, t=t: a.activation(out=junk, in_=xa[:, t, :], func=AF.Square, accum_out=ssa[:, t:t + 1]),
                     reads=[("xa", t)], writes=["junk", ("ssa", t)])
            xx, tii, yy, hh_, zz = (rsa[:, c, 0:8] for c in range(5))
            k = "rsa"
            p.op("dve", lambda v: v.tensor_scalar(out=xx, in0=ssa[:, 0:8], scalar1=1.0 / D, scalar2=EPS, op0=ALU.mult, op1=ALU.add), reads=[("ssa", t) for t in range(NT)], writes=[k])
            p.op("dve", lambda v: v.tensor_scalar(out=tii.bitcast(I32), in0=xx.bitcast(I32), scalar1=1, scalar2=None, op0=ALU.arith_shift_right), reads=[k], writes=[k])
            p.op("dve", lambda v: v.tensor_scalar(out=yy.bitcast(I32), in0=tii.bitcast(I32), scalar1=-1, scalar2=0x5f3759df, op0=ALU.mult, op1=ALU.add), reads=[k], writes=[k])
            p.op("dve", lambda v: v.tensor_scalar(out=hh_, in0=xx, scalar1=0.5, scalar2=None, op0=ALU.mult), reads=[k], writes=[k])
            for _ in range(3):
                p.op("dve", lambda v: v.tensor_tensor(out=zz, in0=yy, in1=yy, op=ALU.mult), reads=[k], writes=[k])
                p.op("dve", lambda v: v.tensor_tensor(out=zz, in0=zz, in1=hh_, op=ALU.mult), reads=[k], writes=[k])
                p.op("dve", lambda v: v.tensor_scalar(out=zz, in0=zz, scalar1=-1.0, scalar2=1.5, op0=ALU.mult, op1=ALU.add), reads=[k], writes=[k])
                p.op("dve", lambda v: v.tensor_tensor(out=yy, in0=yy, in1=zz, op=ALU.mult), reads=[k], writes=[k])
            for t in range(NT):
                p.op("act", lambda a, t=t: a.activation(out=xna[:, t, :], in_=xa[:, t, :], func=AF.Identity, scale=yy[:, t:t + 1]), reads=[("xa", t), k], writes=[("xna", t)])
                transpose_evac(xna[:, t, :], ("xna", t), 128, r, 0, hT, ("hT", t), t * 128)
            if half == 1:
                if chunk is None:
                    p.dma(xh[0:3, 0:D], xhalo, writes=["xh"])
                else:
                    p.op("dve", lambda v: v.memset(xh[0:3, 0:D], 0.0), writes=["xh"])
                    if chunk > 0:
                        p.dma(xh[0:2, 0:D], xs_all[base - 2:base, :], writes=["xh"])
                    if chunk < 3:
                        p.dma(xh[2:3, 0:D], xs_all[base + T:base + T + 1, :], writes=["xh"])
                norm_transpose(xh[0:3, 0:D], "xh", 3, r, 0, hT, ("hT", 8), T, xhn[0:3, 0:D], "xhn")

        HTB = lambda blk: [("hT", 4 * blk + i) for i in range(4)]

        def mm_st(b, wview, rhs_fn, rkeys, wkey, ncols=512):
            for kc in range(8):
                p.op("pe", lambda t_, kc=kc: t_.matmul(bank(b)[:, 0:ncols], lhsT=wview[:, kc, :], rhs=rhs_fn(kc), start=(kc == 0), stop=(kc == 7)),
                     reads=[wkey] + rkeys, writes=[PSK(b)])

        def stage_L(half, prepass, chunk=None):
            nseq, L = (4, 256) if half == 0 else (1, 1024)
            XW = L + 3
            v3 = lambda tl: tl.rearrange("p (s n) -> p s n", s=nseq)
            for S_ in LS:
                p.op("dve", lambda v, S_=S_: v.memset(S_["xrp"], 0.0), writes=["xrp" + S_["sfx"]])
            dirs = [d for d in (0, 1) if not (prepass and ((chunk == 0 and d == 1) or (chunk == 3 and d == 0)))]

            def do_chunk(j, S_):
                xrp, xc, xcb, gg = S_["xrp"], S_["xc"], S_["xcb"], S_["gg"]
                trb, aab, tib, hhb = S_["trb"], S_["aab"], S_["tib"], S_["hhb"]
                KS = lambda n: n + S_["sfx"]
                xrp3 = xrp[:, 0:nseq * XW].rearrange("p (s n) -> p s n", s=nseq)
                wsl, wk = wunit((w_st[8 + j], None))
                wx = wsl.rearrange("p (u k n) -> p u k n", u=2, k=8)
                for blk in range(2):
                    b = nb()
                    mm_st(b, wx[:, 0], lambda kc, blk=blk: hT[:, kc, blk * 512:(blk + 1) * 512], HTB(blk), wk)
                    if half == 0:
                        dst = xrp3[:, 2 * blk:2 * blk + 2, 2:2 + L]
                        src = bank(b).rearrange("p (s n) -> p s n", s=2)
                    else:
                        dst = xrp[:, 2 + blk * 512:2 + (blk + 1) * 512]
                        src = bank(b)
                    p.op("dve", lambda v, dst=dst, src=src: v.tensor_copy(out=dst, in_=src), reads=[PSK(b)], writes=[KS("xrp")])
                if half == 1:
                    b = nb()
                    mm_st(b, wx[:, 0], lambda kc: hT[:, kc, T:T + 3], [("hT", 8)], wk, ncols=3)
                    if prepass:
                        if chunk > 0:
                            p.op("dve", lambda v, b=b: v.tensor_copy(out=xrp[:, 0:2], in_=bank(b)[:, 0:2]), reads=[PSK(b)], writes=[KS("xrp")])
                        if chunk < 3:
                            p.op("dve", lambda v, b=b: v.tensor_copy(out=xrp[:, 2 + T:3 + T], in_=bank(b)[:, 2:3]), reads=[PSK(b)], writes=[KS("xrp")])
                    else:
                        p.op("dve", lambda v, b=b: v.tensor_tensor(out=xrp[:, 0:2], in0=bank(b)[:, 0:2], in1=hmk[:, 0:2], op=ALU.mult), reads=[PSK(b), "hmask"], writes=[KS("xrp")])
                        p.op("dve", lambda v, b=b: v.tensor_tensor(out=xrp[:, 2 + T:3 + T], in0=bank(b)[:, 2:3], in1=hmk[:, 2:3], op=ALU.mult), reads=[PSK(b), "hmask"], writes=[KS("xrp")])
                if not prepass:
                    for blk in range(2):
                        b = nb()
                        mm_st(b, wx[:, 1], lambda kc, blk=blk: hT[:, kc, blk * 512:(blk + 1) * 512], HTB(blk), wk)
                        p.op("act", lambda a, b=b, blk=blk: a.activation(out=gg[:, blk * 512:(blk + 1) * 512], in_=bank(b), func=AF.Gelu_apprx_tanh),
                             reads=[PSK(b)], writes=[KS("gg")])
                xc3 = v3(xc)
                p.op("dve", lambda v, j=j: v.tensor_scalar(out=xc3, in0=xrp3[:, :, 0:L], scalar1=cw[:, j, 0:1], scalar2=cb[:, j:j + 1], op0=ALU.mult, op1=ALU.add),
                     reads=[KS("xrp"), "cw", "cb"], writes=[KS("xc")])
                for k in range(1, 4):
                    p.op("dve", lambda v, j=j, k=k: v.scalar_tensor_tensor(out=xc3, in0=xrp3[:, :, k:k + L], scalar=cw[:, j, k:k + 1], in1=xc3, op0=ALU.mult, op1=ALU.add),
                         reads=[KS("xrp"), "cw", KS("xc")], writes=[KS("xc")])
                p.op("act", lambda a: a.copy(out=xcb, in_=xc), reads=[KS("xc")], writes=[KS("xcb")])
                tsum = []
                for g in range(4):
                    d_ = g % 2
                    if d_ not in dirs:
                        continue
                    dstb = trb[d_] if g < 2 else tib[d_]
                    dkey = KS(f"tr{d_}") if g < 2 else KS(f"ti{d_}")
                    for blk in range(2):
                        b = nb()
                        p.op("pe", lambda t_, g=g, j=j, blk=blk, b=b: t_.matmul(bank(b), lhsT=bd[:, g, j, :], rhs=xcb[:, blk * 512:(blk + 1) * 512], start=True, stop=True),
                             reads=["bd", KS("xcb")], writes=[PSK(b)])
                        if prepass and g < 2:
                            sc, sk = sscol()
                            tsum.append((d_, sc, sk))
                            p.op("act", lambda a, g=g, j=j, blk=blk, b=b, dstb=dstb, sc=sc: a.activation(out=dstb[:, blk * 512:(blk + 1) * 512], in_=bank(b), func=AF.Tanh, scale=0.5, bias=hb[:, g, j:j + 1], accum_out=sc[:, 0:1]),
                                 reads=[PSK(b), "hb"], writes=[dkey, sk])
                        else:
                            p.op("act", lambda a, g=g, j=j, blk=blk, b=b, dstb=dstb: a.activation(out=dstb[:, blk * 512:(blk + 1) * 512], in_=bank(b), func=AF.Tanh, scale=0.5, bias=hb[:, g, j:j + 1]),
                                 reads=[PSK(b), "hb"], writes=[dkey])
                for d_ in dirs:
                    p.op("act", lambda a, d_=d_, j=j: a.activation(out=aab[d_], in_=trb[d_], func=AF.Exp, scale=lamc[:, 1, d_, j:j + 1], bias=lamc[:, 1, d_, j:j + 1]),
                         reads=[KS(f"tr{d_}"), "lamc"], writes=[KS(f"aa{d_}")])
                    p.op("act", lambda a, d_=d_, j=j: a.activation(out=trb[d_], in_=trb[d_], func=AF.Exp, scale=lamc[:, 0, d_, j:j + 1], bias=lamc[:, 0, d_, j:j + 1]),
                         reads=[KS(f"tr{d_}"), "lamc"], writes=[KS(f"tr{d_}")])
                for d_ in dirs:
                    p.op("act", lambda a, d_=d_: a.activation(out=trb[d_], in_=trb[d_], func=AF.Sqrt, scale=-0.25, bias=0.25),
                         reads=[KS(f"tr{d_}")], writes=[KS(f"tr{d_}")])
                yield
                for d_ in dirs:
                    p.op("dve", lambda v, d_=d_: v.scalar_tensor_tensor(out=tib[d_], in0=tib[d_], scalar=1.0, in1=xc, op0=ALU.add, op1=ALU.mult),
                         reads=[KS(f"ti{d_}"), KS("xc")], writes=[KS(f"ti{d_}")])
                    p.op("dve", lambda v, d_=d_: v.tensor_tensor(out=tib[d_], in0=tib[d_], in1=trb[d_], op=ALU.mult),
                         reads=[KS(f"ti{d_}"), KS(f"tr{d_}")], writes=[KS(f"ti{d_}")])
                for d_ in dirs:
                    a3 = v3(aab[d_]); b3 = v3(tib[d_]); h3 = v3(hhb[d_])
                    for s in range(nseq):
                        init = 0.0
                        ik = []
                        if not (prepass or half == 0):
                            ix = 0 if d_ == 0 else T - 1
                            p.op("dve", lambda v, d_=d_, j=j, ix=ix: v.scalar_tensor_tensor(out=tib[d_][:, ix:ix + 1], in0=aab[d_][:, ix:ix + 1], scalar=h0t[:, d_, j:j + 1], in1=tib[d_][:, ix:ix + 1], op0=ALU.mult, op1=ALU.add),
                                 reads=[KS(f"aa{d_}"), KS(f"ti{d_}"), "h0t"], writes=[KS(f"ti{d_}")])
                        if d_ == 0:
                            oa, da, db = h3[:, s, :], a3[:, s, :], b3[:, s, :]
                        else:
                            oa, da, db = rev_ap(h3[:, s, :]), rev_ap(a3[:, s, :]), rev_ap(b3[:, s, :])
                        p.op("dve", lambda v, oa=oa, da=da, db=db, init=init: v.tensor_tensor_scan(out=oa, data0=da, data1=db, initial=init, op0=ALU.mult, op1=ALU.add),
                             reads=[KS(f"aa{d_}"), KS(f"ti{d_}")] + ik, writes=[KS(f"hh{d_}")])
                hf3 = v3(hhb[0]); hb3 = v3(hhb[1])
                if prepass:
                    for d_ in dirs:
                        cols = [(sc, sk) for (dd, sc, sk) in tsum if dd == d_]
                        tcol, tk = sscol()
                        p.op("dve", lambda v, cols=cols, tcol=tcol: v.tensor_tensor(out=tcol[:, 0:1], in0=cols[0][0][:, 0:1], in1=cols[1][0][:, 0:1], op=ALU.add),
                             reads=[cols[0][1], cols[1][1]], writes=[tk])
                        p.op("act", lambda a, d_=d_, j=j, tcol=tcol: a.activation(out=allg[:, chunk, 2 * d_, j:j + 1], in_=tcol[:, 0:1], func=AF.Exp, scale=lamc[:, 1, d_, j:j + 1], bias=lamc[:, 2, d_, j:j + 1]),
                             reads=[tk, "lamc"], writes=[("allg", chunk * 32 + 16 * d_ + j)])
                    if 0 in dirs:
                        p.op("dve", lambda v, j=j: v.tensor_copy(out=allg[:, chunk, 1, j:j + 1], in_=hhb[0][:, T - 1:T]), reads=[KS("hh0")], writes=[("allg", chunk * 32 + 8 + j)])
                    if 1 in dirs:
                        p.op("dve", lambda v, j=j: v.tensor_copy(out=allg[:, chunk, 3, j:j + 1], in_=hhb[1][:, 0:1]), reads=[KS("hh1")], writes=[("allg", chunk * 32 + 24 + j)])
                else:
                    if half == 0:
                        p.op("dve", lambda v, j=j: v.tensor_copy(out=stT[:, j, :, 0], in_=hf3[:, :, L - 1]), reads=[KS("hh0")], writes=["stT"])
                        p.op("dve", lambda v, j=j: v.tensor_copy(out=stT[:, j, :, 1], in_=hb3[:, :, 0]), reads=[KS("hh1")], writes=["stT"])
                    p.op("dve", lambda v: v.tensor_tensor(out=hhb[0], in0=hhb[0], in1=hhb[1], op=ALU.add), reads=[KS("hh0"), KS("hh1")], writes=[KS("hh0")])
                    p.op("dve", lambda v, j=j: v.tensor_tensor(out=yr[:, j, :], in0=hhb[0], in1=gg, op=ALU.mult), reads=[KS("hh0"), KS("gg")], writes=[("yr", j)])
            gens = [do_chunk(j, LS[j % 2]) for j in range(8)]
            next(gens[0])
            for j in range(8):
                if j + 1 < 8:
                    next(gens[j + 1])
                for _ in gens[j]:
                    pass
            import os as _os5
            if (not prepass) and half == 0 and not _os5.environ.get("KNONS"):
                pr = npair()
                for j in range(8):
                    p.op("pe", lambda t_, j=j, pr=pr: t_.matmul(pp[pr][0:8, j * 128:(j + 1) * 128], lhsT=stT[:, j].rearrange("p s d -> p (s d)"), rhs=ident_f, start=True, stop=True),
                         reads=["stT", "ident_f"], writes=[PSK(2 * pr + j // 4)])
                p.op("dve", lambda v, pr=pr: v.tensor_copy(out=sto[0:8, 0:D], in_=pp[pr][0:8, :]), reads=[PSK(2 * pr), PSK(2 * pr + 1)], writes=["sto"])
                p.dma(ns_out, sto[0:8, 0:D], reads=["sto"], writes=["ns_out"])

        def sto_ps(b, j):
            pr = pp[b // 2]
            return pr[0:8, j * 128:(j + 1) * 128]

        def stage_exchange():
            for d_ in range(2):
                Ad = allg[:, 0:4, 2 * d_, :]; Hd = allg[:, 0:4, 2 * d_ + 1, :]
                Ap = cand[:, 2 * d_, 0:4]; Hp = cand[:, 2 * d_ + 1, 0:4]
                p.op("dve", lambda v, Ad=Ad, Ap=Ap, d_=d_: v.scalar_tensor_tensor(out=Ap, in0=Ad, scalar=-1.0, in1=msk[:, d_, 0:4], op0=ALU.add, op1=ALU.mult), reads=["allg", "msk"], writes=["cand"])
                p.op("dve", lambda v, Ap=Ap: v.tensor_scalar(out=Ap, in0=Ap, scalar1=1.0, scalar2=None, op0=ALU.add), reads=["cand"], writes=["cand"])
                p.op("dve", lambda v, Hd=Hd, Hp=Hp, d_=d_: v.tensor_tensor(out=Hp, in0=Hd, in1=msk[:, d_, 0:4], op=ALU.mult), reads=["allg", "msk"], writes=["cand"])
                p.op("dve", lambda v, d_=d_: v.tensor_copy(out=h0t[:, d_, :], in_=st0[:, d_, :]), reads=["st0"], writes=["h0t"])
                for r in (range(4) if d_ == 0 else range(3, -1, -1)):
                    p.op("dve", lambda v, d_=d_, r=r, Ap=Ap: v.tensor_tensor(out=h0t[:, d_, :], in0=h0t[:, d_, :], in1=Ap[:, r, :], op=ALU.mult), reads=["h0t", "cand"], writes=["h0t"])
                    p.op("dve", lambda v, d_=d_, r=r, Hp=Hp: v.tensor_tensor(out=h0t[:, d_, :], in0=h0t[:, d_, :], in1=Hp[:, r, :], op=ALU.add), reads=["h0t", "cand"], writes=["h0t"])

        def stage_G(half):
            p.dma(gsgu, gsgu_bd, writes=["gsgu"])
            for u in range(4):
                wsl, wk = wunit((w_st[u], None))
                wu = wsl.rearrange("p (u k n) -> p u k n", u=2, k=8)
                for uu in range(2):
                    j = 2 * u + uu
                    for blk in range(2):
                        b = nb()
                        mm_st(b, wu[:, uu], lambda kc, blk=blk: hT[:, kc, blk * 512:(blk + 1) * 512], HTB(blk), wk)
                        p.op("act", lambda a, b=b, j=j, blk=blk: a.activation(out=gu[:, j, blk * 512:(blk + 1) * 512], in_=bank(b), func=AF.Gelu_apprx_tanh),
                             reads=[PSK(b)], writes=[("gu", j)])
            for i_ in range(4):
                stage_piece(w_v[i_ * 256:(i_ + 1) * 256, :].rearrange("(k p) n -> p k n", p=128), wv[:, 2 * i_:2 * i_ + 2, :], ("wv", i_))
            for t in range(NT):
                s = t % 2
                pr = npair()
                for kc in range(8):
                    for hf in range(2):
                        p.op("pe", lambda t_, kc=kc, hf=hf, t=t, pr=pr: t_.matmul(pp[pr][:, hf * 512:(hf + 1) * 512], lhsT=hT[:, kc, t * 128:(t + 1) * 128], rhs=wv[:, kc, hf * 512:(hf + 1) * 512], start=(kc == 0), stop=(kc == 7)),
                             reads=[("hT", t), "wv"], writes=[PSK(2 * pr + hf)])
                scs = []
                for hf in range(2):
                    p.op("act", lambda a, hf=hf, s=s, pr=pr: a.activation(out=gv[:, s, hf * 512:(hf + 1) * 512], in_=pp[pr][:, hf * 512:(hf + 1) * 512], func=AF.Gelu_apprx_tanh),
                         reads=[PSK(2 * pr + hf)], writes=[("gv", s)])
                sc, sk = sscol()
                p.op("act", lambda a, s=s, sc=sc: a.activation(out=junk, in_=gv[:, s, :], func=AF.Square, accum_out=sc[:, 0:1]), reads=[("gv", s)], writes=["junk", sk])
                rr, rk = rstd([sc[:, 0:1]], [sk], 128, 1.0 / D, EPS)
                p.op("dve", lambda v, s=s, rr=rr: v.scalar_tensor_tensor(out=vn[:, s, :], in0=gv[:, s, :], scalar=rr, in1=gsgu, op0=ALU.mult, op1=ALU.mult),
                     reads=[("gv", s), rk, "gsgu"], writes=[("vn", s)])
                pr2 = npair()
                ps3 = pp[pr2].rearrange("p (g q) -> p g q", g=8)
                for g in range(8):
                    kk = [PSK(2 * pr2 + g // 4)]
                    p.op("pe", lambda t_, g=g, s=s: t_.matmul(ps3[:, g, :], lhsT=vn[:, s, g * 128:(g + 1) * 128], rhs=wsp[:, g, :], start=True, stop=False),
                         reads=[("vn", s), "wsp"], writes=kk)
                    p.op("pe", lambda t_, g=g: t_.matmul(ps3[:, g, :], lhsT=ones_b[0:1, 0:128], rhs=bhi[0:1, g * 128:(g + 1) * 128], start=False, stop=False),
                         reads=["ones_b", "bhi"], writes=kk)
                    p.op("pe", lambda t_, g=g: t_.matmul(ps3[:, g, :], lhsT=ones_b[0:1, 0:128], rhs=blo[0:1, g * 128:(g + 1) * 128], start=False, stop=True),
                         reads=["ones_b", "blo"], writes=kk)
                for hf in range(2):
                    p.op("dve", lambda v, hf=hf, t=t: v.tensor_tensor(out=yg[:, 4 * hf:4 * hf + 4, t * 128:(t + 1) * 128], in0=ps3[:, 4 * hf:4 * hf + 4, :], in1=gu[:, 4 * hf:4 * hf + 4, t * 128:(t + 1) * 128], op=ALU.mult),
                         reads=[PSK(2 * pr2 + hf)] + [("gu", 4 * hf + i) for i in range(4)], writes=[("yg", t)])

        def stage_M1(half):
            YGB = lambda blk: [("yg", 4 * blk + i) for i in range(4)]
            for c in range(8):
                wsl, wk = wunit((w_st[24 + c], None))
                wg = wsl.rearrange("p (u k n) -> p u k n", u=2, k=8)
                wsl2, wk2 = wunit((w_st[32 + c], None))
                wb = wsl2.rearrange("p (u k n) -> p u k n", u=2, k=8)
                for blk in range(2):
                    s = blk
                    cs = slice(blk * 512, (blk + 1) * 512)
                    b = nb()
                    mm_st(b, wg[:, 0], lambda kc, cs=cs: hT[:, kc, cs], HTB(blk), wk)
                    p.op("act", lambda a, b=b, s=s: a.activation(out=ta[:, s, :], in_=bank(b), func=AF.Tanh, scale=0.5), reads=[PSK(b)], writes=[("ta", s)])
                    b = nb()
                    mm_st(b, wg[:, 1], lambda kc, cs=cs: hT[:, kc, cs], HTB(blk), wk)
                    p.op("act", lambda a, b=b, s=s: a.activation(out=tb[:, s, :], in_=bank(b), func=AF.Tanh, scale=0.5), reads=[PSK(b)], writes=[("tb", s)])
                    b = nb()
                    mm_st(b, wb[:, 0], lambda kc, cs=cs: yg[:, kc, cs], YGB(blk), wk2)
                    p.op("dve", lambda v, b=b, s=s: v.scalar_tensor_tensor(out=m1[:, s, :], in0=ta[:, s, :], scalar=1.0, in1=bank(b), op0=ALU.add, op1=ALU.mult),
                         reads=[("ta", s), PSK(b)], writes=[("m1", s)])
                    b = nb()
                    mm_st(b, wb[:, 1], lambda kc, cs=cs: yr[:, kc, cs], [("yr", i) for i in range(8)], wk2)
                    p.op("dve", lambda v, b=b, s=s: v.scalar_tensor_tensor(out=m2[:, s, :], in0=tb[:, s, :], scalar=1.0, in1=bank(b), op0=ALU.add, op1=ALU.mult),
                         reads=[("tb", s), PSK(b)], writes=[("m2", s)])
                    p.op("dve", lambda v, s=s, c=c, cs=cs: v.tensor_tensor(out=mg[:, c, cs], in0=m1[:, s, :], in1=m2[:, s, :], op=ALU.add),
                         reads=[("m1", s), ("m2", s)], writes=[("mg", c)])

        def stage_M2(half):
            r = half
            for i_ in range(4):
                stage_piece(w_out[i_ * 256:(i_ + 1) * 256, :].rearrange("(k p) n -> p k n", p=128), wo[:, 2 * i_:2 * i_ + 2, :], ("wo", i_))
            for t in range(NT):
                s = t % 2
                p.dma(xt[:, s, :], xin[half * T + t * 128: half * T + (t + 1) * 128, :], writes=[("xt", s)])
                pr = npair()
                for c in range(8):
                    for hf in range(2):
                        p.op("pe", lambda t_, c=c, hf=hf, t=t, pr=pr: t_.matmul(pp[pr][:, hf * 512:(hf + 1) * 512], lhsT=mg[:, c, t * 128:(t + 1) * 128], rhs=wo[:, c, hf * 512:(hf + 1) * 512], start=(c == 0), stop=(c == 7)),
                             reads=[("mg", i) for i in range(8)] + ["wo"], writes=[PSK(2 * pr + hf)])
                cols = []
                for hf in range(2):
                    sc, sk = sscol()
                    cols.append((sc, sk))
                    p.op("act", lambda a, hf=hf, pr=pr, sc=sc: a.activation(out=junk[:, 0:512], in_=pp[pr][:, hf * 512:(hf + 1) * 512], func=AF.Square, accum_out=sc[:, 0:1]),
                         reads=[PSK(2 * pr + hf)], writes=["junk", sk])
                rr, rk = rstd([cols[0][0][:, 0:1], cols[1][0][:, 0:1]], [cols[0][1], cols[1][1]], 128, 1.0 / D, 4.0 * EPS)
                for hf in range(2):
                    cs = slice(hf * 512, (hf + 1) * 512)
                    p.op("dve", lambda v, cs=cs, pr=pr, rr=rr, r=r: v.scalar_tensor_tensor(out=tmp[:, cs], in0=pp[pr][:, cs], scalar=rr, in1=gate[:, 0, r, cs], op0=ALU.mult, op1=ALU.mult),
                         reads=[PSK(2 * pr + hf), rk, ("gate", r)], writes=["tmp"])
                p.op("dve", lambda v, t=t, s=s: v.tensor_tensor(out=x1[:, t, :], in0=tmp, in1=xt[:, s, :], op=ALU.add), reads=["tmp", ("xt", s)], writes=[("x1", t)])
                norm_transpose(x1[:, t, :], ("x1", t), 128, r, 1, h2T, ("h2T", t), t * 128, xn[:, s, :], ("xn", s))

        def stage_F(half):
            r = half
            H2B = lambda blk: [("h2T", 4 * blk + i) for i in range(4)]
            for u in range(16):
                wsl, wk = wunit((w_ff1[u], None))
                w1 = wsl.rearrange("p (u k n) -> p u k n", u=2, k=8)
                for uu in range(2):
                    j = 2 * u + uu
                    for blk in range(2):
                        s = blk
                        b = nb()
                        mm_st(b, w1[:, uu], lambda kc, blk=blk: h2T[:, kc, blk * 512:(blk + 1) * 512], H2B(blk), wk)
                        p.op("act", lambda a, b=b, s=s: a.activation(out=rt[:, s, :], in_=bank(b), func=AF.Relu), reads=[PSK(b)], writes=[("rt", s)])
                        p.op("dve", lambda v, s=s, j=j, blk=blk: v.tensor_tensor(out=f1T[:, j, blk * 512:(blk + 1) * 512], in0=rt[:, s, :], in1=rt[:, s, :], op=ALU.mult),
                             reads=[("rt", s)], writes=[("f1T", j)])
            import os as _os3
            kf = int(_os3.environ.get("KF", "9"))
            if kf < 2:
                return
            sscols = {}
            for ps_ in range(2):
                for jg in range(8):
                    src = w_ff2[jg * 512:(jg + 1) * 512, ps_ * 512:(ps_ + 1) * 512].rearrange("(a p) n -> p a n", p=128)
                    wsl, wk = wunit((src, None))
                    w2 = wsl.rearrange("p (a n) -> p a n", a=4)
                    for jj in range(4):
                        j = 4 * jg + jj
                        for t in range(NT):
                            p.op("pe", lambda t_, j=j, jj=jj, t=t, w2=w2: t_.matmul(bank(t), lhsT=f1T[:, j, t * 128:(t + 1) * 128], rhs=w2[:, jj, :], start=(j == 0), stop=(j == 31)),
                                 reads=[("f1T", j), wk], writes=[PSK(t)])
                for t in range(NT):
                    cs = slice(ps_ * 512, (ps_ + 1) * 512)
                    sc, sk = sscol()
                    sscols[(t, ps_)] = (sc, sk)
                    p.op("act", lambda a, t=t, sc=sc: a.activation(out=junk[:, 0:512], in_=bank(t), func=AF.Square, accum_out=sc[:, 0:1]), reads=[PSK(t)], writes=["junk", sk])
                    p.op("dve", lambda v, t=t, cs=cs: v.tensor_copy(out=f2[:, t, cs], in_=bank(t)), reads=[PSK(t), sk], writes=[("f2", t)])
            if kf < 3:
                return
            for t in range(NT):
                (sa, ka), (sb_, kb) = sscols[(t, 0)], sscols[(t, 1)]
                rr, rk = rstd([sa[:, 0:1], sb_[:, 0:1]], [ka, kb], 128, 1.0 / D, EPS)
                p.op("dve", lambda v, t=t, rr=rr, r=r: v.scalar_tensor_tensor(out=f2[:, t, :], in0=f2[:, t, :], scalar=rr, in1=gate[:, 1, r, :], op0=ALU.mult, op1=ALU.mult),
                     reads=[("f2", t), rk, ("gate", 2 + r)], writes=[("f2", t)])
                p.op("dve", lambda v, t=t: v.tensor_tensor(out=f2[:, t, :], in0=f2[:, t, :], in1=x1[:, t, :], op=ALU.add), reads=[("f2", t), ("x1", t)], writes=[("f2", t)])
                p.dma(y_out[half * T + t * 128: half * T + (t + 1) * 128, :], f2[:, t, :], reads=[("f2", t)], writes=["y_out"])

        import os as _os
        lim = int(_os.environ.get("KLIM", "99"))
        seq = [(stage_consts, ())]
        for c_ in range(4):
            seq += [(stage_A, (1, c_)), (stage_L, (1, True, c_))]
        seq += [(stage_exchange, ())]
        for half in (0, 1):
            seq += [(stage_A, (half,)), (stage_L, (half, False)), (stage_G, (half,)), (stage_M1, (half,)), (stage_M2, (half,)), (stage_F, (half,))]
        for fn_, args_ in seq[:lim]:
            fn_(*args_)
        p.wait_all("sp", ["y_out", "ns_out"])

    dry = Prog(record_only=True)
    dry.wlist = []
    dry.overlaps = ar.ov
    emit(dry, None)
    prog = Prog()
    prog.overlaps = ar.ov
    emit(prog, dry.wlist)
    with ExitStack() as stack:
        prog.build(nc, stack)
    return nc


_NC_CACHE = {}


def _f32(a):
    return np.ascontiguousarray(np.asarray(a, dtype=np.float32))


def _colT(v):
    return np.asarray(v, np.float32).reshape(8, 128).T


def kernel(x_prompt, x_sample, state_lru, c, c_ctx, w_ada, b_ada, g_pre_mix, g_post_mix, g_pre_mlp, g_post_mlp,
           w_in, g_sgu, w_sp, b_sp, conv_w, conv_b, w_ra, b_ra, w_ri, b_ri, lam, w_br_g, w_br_r, w_out, w_ff1, w_ff2):
    f = lambda a: np.asarray(a, dtype=np.float32)
    x_prompt, x_sample, state_lru, c, c_ctx = f(x_prompt), f(x_sample), f(state_lru), f(c), f(c_ctx)
    w_in0 = f(w_in)[0]

    def st_unit(w, cb0, cb1):
        blocks = []
        for cbk in (cb0, cb1):
            mat, col = cbk
            blk = mat[:, col * 128:(col + 1) * 128].reshape(8, 128, 128).transpose(1, 0, 2)
            blocks.append(blk)
        return np.stack(blocks, axis=1).reshape(128, 2048)

    brg, brr = f(w_br_g)[0], f(w_br_r)[0]
    units = []
    for u in range(4):
        units.append(st_unit(None, (w_in0, 2 * u), (w_in0, 2 * u + 1)))
    for u in range(4):
        units.append(np.zeros((128, 2048), np.float32))
    for j in range(8):
        units.append(st_unit(None, (w_in0, 16 + j), (w_in0, 24 + j)))
    for j in range(8):
        units.append(np.zeros((128, 2048), np.float32))
    for cc in range(8):
        units.append(st_unit(None, (w_in0, 32 + cc), (w_in0, 40 + cc)))
    for cc in range(8):
        units.append(st_unit(None, (brg, cc), (brr, cc)))
    for _ in range(4):
        units.append(np.zeros((128, 2048), np.float32))
    w_st = np.ascontiguousarray(np.stack(units, 0))
    ff1 = f(w_ff1)[0]
    w_ff1_u = np.ascontiguousarray(np.stack([st_unit(None, (ff1, 2 * u), (ff1, 2 * u + 1)) for u in range(16)], 0))
    w_v = np.ascontiguousarray(w_in0[:, 1024:2048])

    bdm = np.zeros((128, 4, 8, 128), np.float32)
    for g, (wm, d_) in enumerate(((w_ra, 0), (w_ra, 1), (w_ri, 0), (w_ri, 1))):
        wmat = f(wm)[0, d_]
        for j in range(8):
            for h in range(2):
                bdm[64 * h:64 * h + 64, g, j, 64 * h:64 * h + 64] = wmat[2 * j + h]
    hbT = np.stack([_colT(f(b_ra)[0, 0]), _colT(f(b_ra)[0, 1]), _colT(f(b_ri)[0, 0]), _colT(f(b_ri)[0, 1])], 1).reshape(128, 32)
    cwT = np.stack([_colT(f(conv_w)[0, k]) for k in range(4)], 2).reshape(128, 32)
    cbT = _colT(f(conv_b)[0])
    lamT = np.stack([_colT(f(lam)[0, 0]), _colT(f(lam)[0, 1])], 1).reshape(128, 16)
    wspT = np.ascontiguousarray(f(w_sp)[0].transpose(2, 0, 1)).reshape(128, 1024)
    gvecT = np.stack([_colT(f(g_pre_mix)[0]), _colT(f(g_pre_mlp)[0])], 1).reshape(128, 16)
    gpost_b = np.ascontiguousarray(np.broadcast_to(np.concatenate([f(g_post_mix)[0], f(g_post_mlp)[0]])[None, :], (128, 2048)))
    gsgu_b = np.ascontiguousarray(np.broadcast_to(f(g_sgu)[0][None, :], (128, 1024)))
    b_ada2 = np.ascontiguousarray(np.broadcast_to(f(b_ada)[0][None, :], (2, 6144)))
    kc_sel = np.zeros((2, 256), np.float32)
    kc_sel[0, 0:128] = 1.0
    kc_sel[1, 128:256] = 1.0
    shared = dict(w_ada=_f32(f(w_ada)[0]), b_ada2=b_ada2, gvecT=_f32(gvecT), gpost_b=gpost_b, gsgu_b=gsgu_b, lamT=_f32(lamT),
                  hbT=_f32(hbT), cwT=_f32(cwT), cbT=_f32(cbT), bd_w=_f32(bdm.reshape(128, 4096)), wspT=_f32(wspT),
                  bsp=_f32(f(b_sp)[0].reshape(1, 1024)), w_st=w_st, w_v=w_v, w_out=_f32(f(w_out)[0]), w_ff1=w_ff1_u,
                  w_ff2=_f32(f(w_ff2)[0]), kc_sel=kc_sel)
    in_maps = []
    for r in range(8):
        b, q = r // 4, r % 4
        xp = x_prompt[4 * r:4 * r + 4].reshape(1024, 1024)
        xs = x_sample[b, 1024 * q:1024 * (q + 1)]
        xin = np.ascontiguousarray(np.concatenate([xp, xs], 0))
        xhalo = np.zeros((3, 1024), np.float32)
        hm = np.zeros((128, 3), np.float32)
        for i, tok in enumerate((1024 * q - 2, 1024 * q - 1, 1024 * (q + 1))):
            if 0 <= tok < 4096:
                xhalo[i] = x_sample[b, tok]
                hm[:, i] = 1.0
        cselT = np.stack([_colT(c_ctx), _colT(c[b])], 2).reshape(128, 16)
        st0T = np.stack([_colT(state_lru[b, 0, 0]), _colT(state_lru[b, 0, 1])], 1).reshape(128, 16)
        mf = np.zeros((128, 8, 8), np.float32)
        mb = np.zeros((128, 8, 8), np.float32)
        for rr in range(4):
            if rr < q:
                mf[:, rr, :] = 1.0
            if rr > q:
                mb[:, rr, :] = 1.0
        m = dict(shared)
        m.update(xs_all=np.ascontiguousarray(x_sample[b]), xin=xin, xhalo=xhalo, hmask=hm, cselT=_f32(cselT), st0T=_f32(st0T), mskf=mf.reshape(128, 64), mskb=mb.reshape(128, 64))
        in_maps.append(m)
    if "nc" not in _NC_CACHE:
        _NC_CACHE["nc"] = build_program()
    res = run_bass_kernel_spmd(_NC_CACHE["nc"], in_maps, core_ids=list(range(8)))
    y_prompt = np.zeros((32, 256, 1024), np.float32)
    y_sample = np.zeros((2, 4096, 1024), np.float32)
    new_state = np.zeros((32, 1, 2, 1024), np.float32)
    for r in range(8):
        out = res.results[r]
        yo = np.asarray(out["y_out"], np.float32)
        y_prompt[4 * r:4 * r + 4] = yo[0:1024].reshape(4, 256, 1024)
        y_sample[r // 4, 1024 * (r % 4):1024 * (r % 4 + 1)] = yo[1024:2048]
        new_state[4 * r:4 * r + 4, 0] = np.asarray(out["ns_out"], np.float32).reshape(4, 2, 1024)
    return y_prompt, y_sample, new_state
```

```python
from contextlib import ExitStack
import numpy as np
import concourse.bass as bass
import concourse.mybir as mybir
from concourse.bass_utils import run_bass_kernel_spmd

F32 = mybir.dt.float32
BF16 = mybir.dt.bfloat16
I32 = mybir.dt.int32
AF = mybir.ActivationFunctionType
ALU = mybir.AluOpType
AP = bass.AP

D = 1024
T = 1024
NT = 8
EPS = 1e-6
ENG_NAMES = ("pe", "act", "dve", "pool", "sp")


class Prog:
    def __init__(self, n_dma_sems=28, record_only=False):
        self.streams = {e: [] for e in ENG_NAMES}
        self.count = {e: 0 for e in ENG_NAMES}
        self.n_dma_sems = n_dma_sems
        self.dma_cnt = [0] * n_dma_sems
        self.dma_rr = 0
        self.cc_cnt = 0
        self.state = {}
        self.waited = {e: {} for e in ENG_NAMES}
        self.overlaps = {}
        self.record_only = record_only
        self.sw_ids = {}
        self.sw_gen = {}

    def _split(self, key):
        return key if isinstance(key, tuple) else (key, None)

    def _own_states(self, name, idx, create):
        d = self.state.setdefault(name, {})
        if idx is None:
            if create and None not in d:
                d[None] = [{}, {}]
            return list(d.values())
        if idx not in d and create:
            if None in d:
                d[idx] = [dict(d[None][0]), dict(d[None][1])]
            else:
                d[idx] = [{}, {}]
        out = []
        if idx in d:
            out.append(d[idx])
        if None in d:
            out.append(d[None])
        return out

    def _dep_states(self, key):
        name, idx = self._split(key)
        out = self._own_states(name, idx, False)
        for o in self.overlaps.get(name, ()):
            out.extend(self.state.get(o, {}).values())
        return out

    def _deps(self, reads, writes):
        need = {}

        def add(d):
            for src, seq in d.items():
                if need.get(src, -1) < seq:
                    need[src] = seq
        for k in reads:
            for st in self._dep_states(k):
                add(st[0])
        for k in writes:
            for st in self._dep_states(k):
                add(st[0])
                add(st[1])
        return need

    def _commit(self, me, reads, writes):
        src, seq = me
        for k in reads:
            name, idx = self._split(k)
            sts = self._own_states(name, idx, True)
            if idx is None:
                for st in sts:
                    st[1][src] = seq
            else:
                sts[0][1][src] = seq
        for k in writes:
            name, idx = self._split(k)
            sts = self._own_states(name, idx, True)
            tgt = sts if idx is None else sts[:1]
            for st in tgt:
                st[0] = {src: seq}
                st[1] = {}

    def _emit_waits(self, eng, need):
        w = self.waited[eng]
        for src, seq in need.items():
            if src == eng and eng == "pe":
                continue
            if w.get(src, 0) >= seq:
                continue
            w[src] = seq
            self.streams[eng].append(("wait", src, seq))

    def op(self, eng, fn, reads=(), writes=()):
        need = self._deps(reads, writes)
        self._emit_waits(eng, need)
        self.count[eng] += 1
        me = (eng, self.count[eng])
        self.streams[eng].append(("ins", fn))
        self._commit(me, reads, writes)
        return me

    def dma(self, out, in_, reads=(), writes=(), queue="sp", **kw):
        need = self._deps(reads, writes)
        if queue == "pool":
            dk = writes[0]
            if dk not in self.sw_ids:
                self.sw_ids[dk] = len(self.sw_ids)
                self.sw_gen[dk] = 0
            sid = self.sw_ids[dk]
            self.sw_gen[dk] += 1
            self._emit_waits(queue, need)
            self.count["pool"] += 1
            marker = self.count["pool"]
            self.streams[queue].append(("swdma", out, in_, sid, kw, getattr(self, "last_marker", 0)))
            self.last_marker = marker
            src = ("sw", sid, self.sw_gen[dk])
            for k in reads:
                name, idx = self._split(k)
                for st in (self._own_states(name, idx, True) if idx is None else self._own_states(name, idx, True)[:1]):
                    st[1][src] = 16
            for k in writes:
                name, idx = self._split(k)
                sts = self._own_states(name, idx, True)
                for st in (sts if idx is None else sts[:1]):
                    st[0] = {src: 16, "pool": marker}
                    st[1] = {}
            return (src, 16)
        k = self.dma_rr
        self.dma_rr = (self.dma_rr + 1) % self.n_dma_sems
        src = ("dma", k)
        if self.dma_cnt[k] > 0:
            need[src] = max(need.get(src, 0), self.dma_cnt[k])
        self._emit_waits(queue, need)
        self.dma_cnt[k] += 16
        me = (src, self.dma_cnt[k])
        self.streams[queue].append(("dma", out, in_, k, kw))
        self._commit(me, reads, writes)
        return me

    def cc(self, fn, reads=(), writes=()):
        need = self._deps(reads, writes)
        self._emit_waits("pool", need)
        self.cc_cnt += 1
        me = ("cc", self.cc_cnt)
        self.streams["pool"].append(("cc", fn))
        self._commit(me, reads, writes)
        return me

    def wait_all(self, eng, keys):
        self._emit_waits(eng, self._deps(keys, ()))

    EPOCH = 1024

    def build(self, nc, stack):
        E = self.EPOCH
        sem = {e: [stack.enter_context(nc.semaphore(f"s_{e}{i}")) for i in range(self.count[e] // E + 1)] for e in ENG_NAMES}
        dsem = [stack.enter_context(nc.semaphore(f"s_dma{k}")) for k in range(self.n_dma_sems)]
        cc_sem = stack.enter_context(nc.semaphore("s_cc"))
        block = stack.enter_context(nc.Block())
        deco = {"pe": block.tensor, "act": block.scalar, "dve": block.vector,
                "pool": block.gpsimd, "sp": block.sync}

        def semval(src, seq):
            if src == "cc":
                return cc_sem, seq
            if isinstance(src, tuple):
                return dsem[src[1]], seq
            return sem[src][(seq - 1) // E], (seq - 1) % E + 1

        for e in ENG_NAMES:
            def body(eng, items=self.streams[e], e=e):
                n = 0
                for it in items:
                    if it[0] == "wait":
                        s_, v_ = semval(it[1], it[2])
                        eng.wait_ge(s_, v_)
                    elif it[0] == "ins":
                        it[1](eng).then_inc(sem[e][n // E], 1)
                        n += 1
                    elif it[0] == "cc":
                        it[1](eng).then_inc(cc_sem)
                    else:
                        _, out, in_, k, kw = it
                        eng.dma_start(out=out, in_=in_, **kw).then_inc(dsem[k], 16)
            deco[e](body)


ALL_PH = ("C", "A", "L", "G", "M1", "M2", "F1", "F2")


class Arena:
    def __init__(self):
        self.bufs = []

    def add(self, name, nbytes, phases):
        nbytes = (nbytes + 63) // 64 * 64
        self.bufs.append([name, nbytes, set(ALL_PH if phases == "*" else phases), None])

    def pack(self):
        order = sorted(self.bufs, key=lambda b: (-len(b[2]), -b[1]))
        placed = []
        for b in order:
            cands = sorted([0] + [p[3] + p[1] for p in placed])
            for off in cands:
                ok = True
                for p in placed:
                    if p[2] & b[2] and off < p[3] + p[1] and p[3] < off + b[1]:
                        ok = False
                        break
                if ok:
                    b[3] = off
                    break
            placed.append(b)
        self.total = max(b[3] + b[1] for b in self.bufs)
        self.off = {b[0]: b[3] for b in self.bufs}
        self.size = {b[0]: b[1] for b in self.bufs}
        ov = {}
        for a in self.bufs:
            for b in self.bufs:
                if a is not b and a[3] < b[3] + b[1] and b[3] < a[3] + a[1]:
                    ov.setdefault(a[0], []).append(b[0])
        self.ov = ov


def rev_ap(ap2d):
    apl = [list(d) for d in ap2d.ap]
    n = apl[-1][1]
    st = apl[-1][0]
    apl[-1] = [-st, n]
    return AP(ap2d.tensor, ap2d.offset + st * (n - 1), apl)


def build_program():
    nc = bass.Bass("TRN2", target_bir_lowering=False)

    def din(name, shape, dt=F32):
        return nc.dram_tensor(name, list(shape), dt, kind="ExternalInput").ap()

    def dout(name, shape, dt=F32):
        return nc.dram_tensor(name, list(shape), dt, kind="ExternalOutput").ap()

    xin = din("xin", [2048, D])
    xhalo = din("xhalo", [3, D])
    xs_all = din("xs_all", [4096, D])
    hmask = din("hmask", [128, 3])
    cselT = din("cselT", [128, 16])
    st0T = din("st0T", [128, 16])
    mskf = din("mskf", [128, 64])
    mskb = din("mskb", [128, 64])
    kc_sel = din("kc_sel", [2, 256])
    w_ada = din("w_ada", [D, 6 * D])
    b_ada2 = din("b_ada2", [2, 6 * D])
    gvecT = din("gvecT", [128, 16])
    gpost_b = din("gpost_b", [128, 2 * D])
    gsgu_bd = din("gsgu_b", [128, D])
    lamT = din("lamT", [128, 16])
    hbT = din("hbT", [128, 32])
    cwT = din("cwT", [128, 32])
    cbT = din("cbT", [128, 8])
    bd_w = din("bd_w", [128, 32 * 128])
    wspT = din("wspT", [128, 8 * 128])
    bsp = din("bsp", [1, D])
    w_st = din("w_st", [44, 128, 2048])
    w_v = din("w_v", [D, D])
    w_out = din("w_out", [D, D])
    w_ff1 = din("w_ff1", [16, 128, 2048])
    w_ff2 = din("w_ff2", [4 * D, D])
    y_out = dout("y_out", [2048, D])
    ns_out = dout("ns_out", [8, D])

    ar = Arena()
    A_ = ar.add
    A_("ident_b", 256, "*"); A_("ident_f", 512, "*"); A_("sel", 1024, "*"); A_("ones_b", 256, "*")
    A_("cT", 64, "*"); A_("sT", 64, "*"); A_("modT", 384, "*"); A_("modc", 256, "*")
    A_("lamc", 256, "*"); A_("hb", 128, "*"); A_("cw", 128, "*"); A_("cb", 32, "*")
    A_("bd", 8192, "*"); A_("wsp", 2048, "*"); A_("bspf", 4096, ["C"]); A_("bhi", 2048, "*"); A_("blo", 2048, "*")
    A_("gate", 16384, "*"); A_("gsgu", 4096, ["G"]); A_("hmask", 64, "*")
    A_("h0t", 64, "*"); A_("st0", 64, "*"); A_("msk", 512, "*"); A_("summ", 128, "*"); A_("allg", 1024, "*")
    A_("cand", 1024, "*"); A_("stT", 256, "*"); A_("sto", 4096, ["L"])
    A_("rs", 16 * 32, "*"); A_("ss", 512, "*"); A_("junk", 2048, "*"); A_("gvec", 64, "*")
    A_("ws", 4 * 4096, "*"); A_("wsf", 2 * 8192, "*")
    A_("wa", 2 * 16384, ["C"]); A_("modrow", 2 * 2048, ["C"]); A_("gpost", 8192, ["C"]); A_("bada", 2 * 2048, ["C"])
    A_("xa", 8 * 4096, ["A"]); A_("xna", 8 * 2048, ["A"]); A_("ssa", 64, "*"); A_("rsa", 5 * 64, "*")
    A_("xt", 2 * 4096, ["A", "M2"]); A_("xn", 2 * 2048, ["A", "M2"]); A_("xh", 4096, ["A"]); A_("xhn", 2048, ["A"])
    A_("hT", 8 * 1027 * 2, ["A", "L", "G", "M1"])
    for sfx in ("", "_b"):
        A_("xrp" + sfx, 1036 * 4, ["L"]); A_("xc" + sfx, 4096, ["L"]); A_("xcb" + sfx, 2048, ["L"]); A_("gg" + sfx, 4096, ["L"])
        for d_ in range(2):
            A_(f"tr{d_}" + sfx, 4096, ["L"]); A_(f"aa{d_}" + sfx, 4096, ["L"]); A_(f"ti{d_}" + sfx, 4096, ["L"]); A_(f"hh{d_}" + sfx, 4096, ["L"])
    A_("yr", 16384, ["L", "G", "M1"])
    A_("gu", 16384, ["G"]); A_("wv", 16384, ["G"]); A_("gv", 2 * 4096, ["G"]); A_("vn", 2 * 2048, ["G"])
    A_("yg", 16384, ["G", "M1"])
    A_("ta", 2 * 2048, ["M1"]); A_("tb", 2 * 2048, ["M1"]); A_("m1", 2 * 2048, ["M1"]); A_("m2", 2 * 2048, ["M1"])
    A_("mg", 16384, ["M1", "M2"])
    A_("wo", 16384, ["M2"]); A_("tmp", 4096, ["M2"])
    A_("x1", 32768, ["M2", "F1", "F2"]); A_("h2T", 16384, ["M2", "F1"])
    A_("f1T", 65536, ["F1", "F2"]); A_("rt", 2 * 1024, ["F1"]); A_("f2", 32768, ["F2"])
    ar.pack()
    import os as _os2
    if _os2.environ.get("KARENA"):
        for ph in ALL_PH:
            print(ph, sum(b[1] for b in ar.bufs if ph in b[2]))
        print("total", ar.total)
    assert ar.total <= 206 * 1024, ar.total
    arena = nc.alloc_sbuf_tensor("arena", [128, ar.total // 4], F32)

    def V(name, dt=F32, pattern=None, **kw):
        o = ar.off[name] // 4
        n = ar.size[name] // 4
        v = arena[:, o:o + n]
        if dt != F32:
            v = v.bitcast(dt)
        if pattern:
            v = v.rearrange(pattern, **kw)
        return v

    ident_b = V("ident_b", BF16); ident_f = V("ident_f"); sel = V("sel")[0:2, :]; ones_b = V("ones_b", BF16)
    cT = V("cT"); sT = V("sT"); modT = V("modT", F32, "p (c r) -> p c r", r=2)
    modc = V("modc", F32, "p (w r k) -> p w r k", w=4, r=2)
    lamc = V("lamc", F32, "p (w d j) -> p w d j", w=4, d=2)
    hb = V("hb", F32, "p (g j) -> p g j", g=4); cw = V("cw", F32, "p (j k) -> p j k", k=4); cb = V("cb")
    bd = V("bd", BF16, "p (g j m) -> p g j m", g=4, j=8); wsp = V("wsp", BF16, "p (g q) -> p g q", g=8)
    bspf = V("bspf"); bhi = V("bhi", BF16); blo = V("blo", BF16)
    gate = V("gate", F32, "p (w r n) -> p w r n", w=2, r=2); gsgu = V("gsgu"); hmk = V("hmask")
    h0t = V("h0t", F32, "p (d j) -> p d j", d=2); st0 = V("st0", F32, "p (d j) -> p d j", d=2)
    msk = V("msk", F32, "p (w r j) -> p w r j", w=2, r=8)
    summ = V("summ", F32, "p (q j) -> p q j", q=4); allg = V("allg", F32, "p (r q j) -> p r q j", r=8, q=4)
    cand = V("cand", F32, "p (w r j) -> p w r j", w=4, r=8)
    stT = V("stT", F32, "p (j s d) -> p j s d", j=8, s=4); sto = V("sto")
    rs = V("rs", F32, "p (s c) -> p s c", c=8); ssq = V("ss", F32, "p (s c) -> p s c", c=4)
    junk = V("junk", BF16); gvec = V("gvec", F32, "p (w k) -> p w k", w=2)
    ws = V("ws", BF16, "p (s n) -> p s n", s=4)
    wsf = V("wsf", F32, "p (s n) -> p s n", s=2)
    wa = V("wa", F32, "p (s k n) -> p s k n", s=2, k=8); modrow = V("modrow", F32, "p (s n) -> p s n", s=2)
    gpost = V("gpost", F32, "p (w n) -> p w n", w=2); bada = V("bada", F32, "p (s n) -> p s n", s=2)
    xt = V("xt", F32, "p (s n) -> p s n", s=2); xn = V("xn", BF16, "p (s n) -> p s n", s=2)
    xh = V("xh"); xhn = V("xhn", BF16)
    xa = V("xa", F32, "p (t n) -> p t n", t=8); xna = V("xna", BF16, "p (t n) -> p t n", t=8)
    ssa = V("ssa"); rsa = V("rsa", F32, "p (c n) -> p c n", c=5)
    hT = V("hT", BF16, "p (k n) -> p k n", k=8)
    LS = []
    for sfx in ("", "_b"):
        LS.append(dict(xrp=V("xrp" + sfx), xc=V("xc" + sfx), xcb=V("xcb" + sfx, BF16), gg=V("gg" + sfx),
                       trb=[V(f"tr{d_}" + sfx) for d_ in range(2)], aab=[V(f"aa{d_}" + sfx) for d_ in range(2)],
                       tib=[V(f"ti{d_}" + sfx) for d_ in range(2)], hhb=[V(f"hh{d_}" + sfx) for d_ in range(2)], sfx=sfx))
    yr = V("yr", BF16, "p (k n) -> p k n", k=8); yg = V("yg", BF16, "p (k n) -> p k n", k=8)
    gu = V("gu", BF16, "p (k n) -> p k n", k=8); wv = V("wv", BF16, "p (k n) -> p k n", k=8)
    gv = V("gv", F32, "p (s n) -> p s n", s=2); vn = V("vn", BF16, "p (s n) -> p s n", s=2)
    ta = V("ta", F32, "p (s n) -> p s n", s=2); tb = V("tb", F32, "p (s n) -> p s n", s=2)
    m1 = V("m1", F32, "p (s n) -> p s n", s=2); m2 = V("m2", F32, "p (s n) -> p s n", s=2)
    mg = V("mg", BF16, "p (k n) -> p k n", k=8); wo = V("wo", BF16, "p (k n) -> p k n", k=8)
    tmp = V("tmp"); x1 = V("x1", F32, "p (t n) -> p t n", t=8); h2T = V("h2T", BF16, "p (k n) -> p k n", k=8)
    f1T = V("f1T", BF16, "p (k n) -> p k n", k=32); rt = V("rt", BF16, "p (s n) -> p s n", s=2)
    f2 = V("f2", F32, "p (t n) -> p t n", t=8)

    pp = [nc.alloc_psum_tensor(f"pp{i}", [128, 1024], F32) for i in range(4)]

    def bank(b):
        return pp[b // 2][:, (b % 2) * 512:(b % 2) * 512 + 512]

    def emit(p, wplan):
        st = {"bank": 0, "pair": 0, "rs": 0, "ss": 0, "wi": 0, "issued": 0, "alt": 0, "sf": 0}

        def nb():
            if st.get("nb_list"):
                st["nbi"] = (st.get("nbi", 0) + 1) % len(st["nb_list"])
                return st["nb_list"][st["nbi"]]
            b = st["bank"]; st["bank"] = (b + 1) % 8
            return b

        def npair():
            if st.get("np_list"):
                st["npi"] = (st.get("npi", 0) + 1) % len(st["np_list"])
                return st["np_list"][st["npi"]]
            b = st["pair"]; st["pair"] = (b + 1) % 4
            return b

        def alt():
            st["alt"] ^= 1
            return "act" if st["alt"] else "dve"

        PSK = lambda b: ("ps", b)

        def stage_piece(src, dst, dkey):
            k = st["sf"]; st["sf"] = (k + 1) % 2
            n = 1
            for d_ in src.shape[1:]:
                n *= d_
            sfl = wsf[:, k, 0:n]
            sv = sfl
            if len(src.shape) == 3:
                sv = sfl.rearrange("p (a n) -> p a n", a=src.shape[1])
            p.dma(sv, src, writes=[("wsf", k)])
            dfl = dst
            p.op("pool", lambda g_: g_.tensor_copy(out=dfl, in_=(sv if len(dst.shape) == 3 else sfl)), reads=[("wsf", k)], writes=[dkey])

        def wunit(desc):
            i = st["wi"]; st["wi"] += 1
            if wplan is None:
                p.wlist.append(desc)
            else:
                while st["issued"] < min(len(wplan), i + 3):
                    k = st["issued"]; st["issued"] += 1
                    src = wplan[k][0]
                    dst = ws[:, k % 4, :]
                    if len(src.shape) == 3:
                        dst = dst.rearrange("p (a n) -> p a n", a=src.shape[1])
                    stage_piece(src, dst, ("ws", k % 4))
            return ws[:, i % 4, :], ("ws", i % 4)

        def rstd(ss_list, keys, np_, scale, eps):
            s = st["rs"]; st["rs"] = (s + 1) % 16
            R = rs[0:np_, s, :]
            k = ("rs", s)
            xx, tii, yy, hh_, zz = (R[:, c:c + 1] for c in range(5))
            if len(ss_list) == 2:
                p.op("dve", lambda v: v.tensor_tensor(out=xx, in0=ss_list[0], in1=ss_list[1], op=ALU.add), reads=keys, writes=[k])
                src = xx
                rk = [k]
            else:
                src = ss_list[0]
                rk = keys
            p.op("dve", lambda v: v.tensor_scalar(out=xx, in0=src, scalar1=scale, scalar2=eps, op0=ALU.mult, op1=ALU.add), reads=rk, writes=[k])
            p.op("dve", lambda v: v.tensor_scalar(out=tii.bitcast(I32), in0=xx.bitcast(I32), scalar1=1, scalar2=None, op0=ALU.arith_shift_right), reads=[k], writes=[k])
            p.op("dve", lambda v: v.tensor_scalar(out=yy.bitcast(I32), in0=tii.bitcast(I32), scalar1=-1, scalar2=0x5f3759df, op0=ALU.mult, op1=ALU.add), reads=[k], writes=[k])
            p.op("dve", lambda v: v.tensor_scalar(out=hh_, in0=xx, scalar1=0.5, scalar2=None, op0=ALU.mult), reads=[k], writes=[k])
            for _ in range(3):
                p.op("dve", lambda v: v.scalar_tensor_tensor(out=zz, in0=yy, scalar=yy, in1=hh_, op0=ALU.mult, op1=ALU.mult), reads=[k], writes=[k])
                p.op("dve", lambda v: v.tensor_scalar(out=zz, in0=zz, scalar1=-1.0, scalar2=1.5, op0=ALU.mult, op1=ALU.add), reads=[k], writes=[k])
                p.op("dve", lambda v: v.tensor_tensor(out=yy, in0=yy, in1=zz, op=ALU.mult), reads=[k], writes=[k])
            return yy, k

        def sscol():
            s = st["ss"]; st["ss"] = (s + 1) % 32
            return ssq[:, s, :], ("ss", s)

        def stage_consts():
            sp_loads = [(cT, cselT, "cT"), (st0.rearrange("p d j -> p (d j)"), st0T, "st0"),
                        (msk[:, 0].rearrange("p r j -> p (r j)"), mskf, ("msk", 0)), (msk[:, 1].rearrange("p r j -> p (r j)"), mskb, ("msk", 1)),
                        (sel, kc_sel, "sel"), (gvec.rearrange("p w k -> p (w k)"), gvecT, "gvec"),
                        (lamc[:, 3].rearrange("p d j -> p (d j)"), lamT, "lamc"), (hb.rearrange("p g j -> p (g j)"), hbT, "hb"),
                        (cw.rearrange("p j k -> p (j k)"), cwT, "cw"), (cb[:, 0:8], cbT, "cb"), (hmk[:, 0:3], hmask, "hmask"),
                        (bspf[0:1, 0:D], bsp, "bspf"),
                        (gpost.rearrange("p w n -> p (w n)"), gpost_b, "gpost")]
            for dst, src, key in sp_loads:
                p.dma(dst, src, writes=[key])
            bdf = bd.rearrange("p g j m -> p (g j m)")
            for i_ in range(2):
                stage_piece(bd_w[:, i_ * 2048:(i_ + 1) * 2048], bdf[:, i_ * 2048:(i_ + 1) * 2048], ("bd", i_))
            stage_piece(wspT, wsp.rearrange("p g q -> p (g q)"), "wsp")
            p.op("pool", lambda g_: g_.memset(ident_f, 0.0), writes=["ident_f"])
            p.op("pool", lambda g_: g_.affine_select(out=ident_f, in_=ident_f, compare_op=ALU.not_equal, fill=1.0, base=0,
                                                      pattern=[[-1, 128]], channel_multiplier=1), reads=["ident_f"], writes=["ident_f"])
            p.op("dve", lambda v: v.tensor_copy(out=ident_b, in_=ident_f), reads=["ident_f"], writes=["ident_b"])
            p.op("pool", lambda g_: g_.memset(ones_b, 1.0), writes=["ones_b"])
            p.op("pool", lambda g_: g_.memset(allg.rearrange("p r q j -> p (r q j)"), 0.0), writes=["allg"])
            B0 = bspf[0:1, 0:D]; B1 = bspf[0:1, D:2 * D] if False else None
            p.op("dve", lambda v: v.tensor_copy(out=bhi[0:1, 0:D], in_=B0), reads=["bspf"], writes=["bhi"])
            p.op("dve", lambda v: v.tensor_tensor(out=B0, in0=B0, in1=bhi[0:1, 0:D], op=ALU.subtract), reads=["bspf", "bhi"], writes=["bspf"])
            p.op("dve", lambda v: v.tensor_copy(out=blo[0:1, 0:D], in_=B0), reads=["bspf"], writes=["blo"])
            p.op("act", lambda a: a.activation(out=sT, in_=cT, func=AF.Tanh, scale=0.5), reads=["cT"], writes=["sT"])
            p.op("dve", lambda v: v.scalar_tensor_tensor(out=sT, in0=sT, scalar=1.0, in1=cT, op0=ALU.add, op1=ALU.mult), reads=["sT", "cT"], writes=["sT"])
            p.op("dve", lambda v: v.tensor_scalar(out=sT, in0=sT, scalar1=0.5, scalar2=None, op0=ALU.mult), reads=["sT"], writes=["sT"])
            sT3 = sT.rearrange("p (k r) -> p k r", r=2)
            L3 = lamc[:, 3]
            p.op("act", lambda a: a.activation(out=L3, in_=L3, func=AF.Exp, scale=-1.0), reads=["lamc"], writes=["lamc"])
            p.op("act", lambda a: a.activation(out=L3, in_=L3, func=AF.Ln, bias=1.0, scale=1.0), reads=["lamc"], writes=["lamc"])
            p.op("dve", lambda v: v.tensor_scalar(out=lamc[:, 0], in0=L3, scalar1=-8.0, scalar2=None, op0=ALU.mult), reads=["lamc"], writes=["lamc"])
            p.op("dve", lambda v: v.tensor_scalar(out=lamc[:, 1], in0=L3, scalar1=-4.0, scalar2=None, op0=ALU.mult), reads=["lamc"], writes=["lamc"])
            p.op("dve", lambda v: v.tensor_scalar(out=lamc[:, 2], in0=L3, scalar1=-4096.0, scalar2=None, op0=ALU.mult), reads=["lamc"], writes=["lamc"])
            p.op("dve", lambda v: v.tensor_scalar(out=hb.rearrange("p g j -> p (g j)"), in0=hb.rearrange("p g j -> p (g j)"), scalar1=0.5, scalar2=None, op0=ALU.mult), reads=["hb"], writes=["hb"])
            order = [2, 3, 0, 1, 4, 5, 6, 7, 8, 9, 10, 11]
            mt_bank = nb()
            mt_ps = bank(mt_bank)[:, 0:96].rearrange("p (c r) -> p c r", r=2)
            for n_, cbk in enumerate(order):
                s = n_ % 2
                p.dma(wa[:, s], w_ada[:, cbk * 512:(cbk + 1) * 512].rearrange("(k p) n -> p k n", p=128), writes=[("wa", s)])
                p.dma(bada[0:2, s, :], b_ada2[:, cbk * 512:(cbk + 1) * 512], writes=[("bada", s)])
                b = nb()
                if b == mt_bank:
                    b = nb()
                for kc in range(8):
                    p.op("pe", lambda t_, kc=kc, s=s, b=b: t_.matmul(bank(b)[0:2, :], lhsT=sT3[:, kc, :], rhs=wa[:, s, kc, :], start=(kc == 0), stop=(kc == 7)),
                         reads=["sT", ("wa", s)], writes=[PSK(b)])
                p.op("dve", lambda v, s=s, b=b: v.tensor_tensor(out=modrow[0:2, s, :], in0=bank(b)[0:2, :], in1=bada[0:2, s, :], op=ALU.add),
                     reads=[PSK(b), ("bada", s)], writes=[("modrow", s)])
                for i in range(4):
                    ci = cbk * 4 + i
                    p.op("pe", lambda t_, s=s, i=i, ci=ci: t_.matmul(mt_ps[:, ci, :], lhsT=modrow[0:2, s, i * 128:(i + 1) * 128], rhs=ident_f[0:2, 0:2], start=True, stop=True),
                         reads=[("modrow", s), "ident_f"], writes=[PSK(mt_bank)])
                if cbk in (4, 5, 10, 11):
                    w_ = 0 if cbk < 6 else 1
                    hf = cbk % 2
                    for r in range(2):
                        b2 = nb()
                        if b2 == mt_bank:
                            b2 = nb()
                        p.op("pe", lambda t_, s=s, r=r, b2=b2: t_.matmul(bank(b2), lhsT=sel[:, r * 128:(r + 1) * 128], rhs=modrow[0:2, s, :], start=True, stop=True),
                             reads=[("modrow", s), "sel"], writes=[PSK(b2)])
                        p.op("dve", lambda v, w_=w_, r=r, hf=hf, b2=b2: v.tensor_tensor(out=gate[:, w_, r, hf * 512:(hf + 1) * 512], in0=bank(b2), in1=gpost[:, w_, hf * 512:(hf + 1) * 512], op=ALU.mult),
                             reads=[PSK(b2), "gpost"], writes=[("gate", w_ * 2 + r)])
            p.op("dve", lambda v: v.tensor_copy(out=modT, in_=mt_ps), reads=[PSK(mt_bank)], writes=["modT"])
            for r in range(2):
                for w_, (sc_i, sh_i) in enumerate(((8, 0), (32, 24))):
                    p.op("dve", lambda v, r=r, w_=w_, sc_i=sc_i: v.scalar_tensor_tensor(out=modc[:, 2 * w_, r, :], in0=modT[:, sc_i:sc_i + 8, r], scalar=1.0, in1=gvec[:, w_, :], op0=ALU.add, op1=ALU.mult),
                         reads=["modT", "gvec"], writes=["modc"])
                    p.op("dve", lambda v, r=r, w_=w_, sh_i=sh_i: v.tensor_copy(out=modc[:, 2 * w_ + 1, r, :], in_=modT[:, sh_i:sh_i + 8, r]),
                         reads=["modT"], writes=["modc"])

        def norm_transpose(src_tile, src_key, np_, r, widx, dstT, dst_key, col0, xn_tile, xn_key):
            sc, sk = sscol()
            p.op("act", lambda a: a.activation(out=junk[0:np_, :], in_=src_tile, func=AF.Square, accum_out=sc[0:np_, 0:1]),
                 reads=[src_key], writes=["junk", sk])
            rr, rk = rstd([sc[0:np_, 0:1]], [sk], np_, 1.0 / D, EPS)
            p.op("act", lambda a: a.activation(out=xn_tile, in_=src_tile, func=AF.Identity, scale=rr), reads=[src_key, rk], writes=[xn_key])
            transpose_evac(xn_tile, xn_key, np_, r, widx, dstT, dst_key, col0)

        def transpose_evac(xn_tile, xn_key, np_, r, widx, dstT, dst_key, col0):
            b = nb()
            pt = bank(b).bitcast(BF16).rearrange("p (k n) -> p k n", k=8)
            for kc in range(8):
                p.op("pe", lambda t_, kc=kc: t_.transpose(pt[:, kc, 0:np_], xn_tile[:, kc * 128:(kc + 1) * 128], ident_b[0:np_, 0:np_]),
                     reads=[xn_key, "ident_b"], writes=[PSK(b)])
            use_act = alt() == "act"
            for kc in range(8):
                scl = modc[:, 2 * widx, r, kc:kc + 1]; shf = modc[:, 2 * widx + 1, r, kc:kc + 1]
                o = dstT[:, kc, col0:col0 + np_]
                if kc % 2 == 0:
                    p.op("act", lambda a, o=o, kc=kc, scl=scl, shf=shf: a.activation(out=o, in_=pt[:, kc, 0:np_], func=AF.Identity, scale=scl, bias=shf),
                         reads=[PSK(b), "modc"], writes=[dst_key])
                else:
                    p.op("dve", lambda v, o=o, kc=kc, scl=scl, shf=shf: v.tensor_scalar(out=o, in0=pt[:, kc, 0:np_], scalar1=scl, scalar2=shf, op0=ALU.mult, op1=ALU.add),
                         reads=[PSK(b), "modc"], writes=[dst_key])

        def stage_A(half, chunk=None):
            r = half
            src, base = (xin, half * T) if chunk is None else (xs_all, chunk * T)
            for t in range(NT):
                p.dma(xa[:, t, :], src[base + t * 128: base + (t + 1) * 128, :], writes=[("xa", t)])
            for t in range(NT):
                p.op("act", lambda a, t=t: a.activation(out=junk, in_=xa[:, t, :], func=AF.Square, accum_out=ssa[:, t:t + 1]),
                     reads=[("xa", t)], writes=["junk", ("ssa", t)])
            xx, tii, yy, hh_, zz = (rsa[:, c, 0:8] for c in range(5))
            k = "rsa"
            p.op("dve", lambda v: v.tensor_scalar(out=xx, in0=ssa[:, 0:8], scalar1=1.0 / D, scalar2=EPS, op0=ALU.mult, op1=ALU.add), reads=[("ssa", t) for t in range(NT)], writes=[k])
            p.op("dve", lambda v: v.tensor_scalar(out=tii.bitcast(I32), in0=xx.bitcast(I32), scalar1=1, scalar2=None, op0=ALU.arith_shift_right), reads=[k], writes=[k])
            p.op("dve", lambda v: v.tensor_scalar(out=yy.bitcast(I32), in0=tii.bitcast(I32), scalar1=-1, scalar2=0x5f3759df, op0=ALU.mult, op1=ALU.add), reads=[k], writes=[k])
            p.op("dve", lambda v: v.tensor_scalar(out=hh_, in0=xx, scalar1=0.5, scalar2=None, op0=ALU.mult), reads=[k], writes=[k])
            for _ in range(3):
                p.op("dve", lambda v: v.tensor_tensor(out=zz, in0=yy, in1=yy, op=ALU.mult), reads=[k], writes=[k])
                p.op("dve", lambda v: v.tensor_tensor(out=zz, in0=zz, in1=hh_, op=ALU.mult), reads=[k], writes=[k])
                p.op("dve", lambda v: v.tensor_scalar(out=zz, in0=zz, scalar1=-1.0, scalar2=1.5, op0=ALU.mult, op1=ALU.add), reads=[k], writes=[k])
                p.op("dve", lambda v: v.tensor_tensor(out=yy, in0=yy, in1=zz, op=ALU.mult), reads=[k], writes=[k])
            for t in range(NT):
                p.op("act", lambda a, t=t: a.activation(out=xna[:, t, :], in_=xa[:, t, :], func=AF.Identity, scale=yy[:, t:t + 1]), reads=[("xa", t), k], writes=[("xna", t)])
                transpose_evac(xna[:, t, :], ("xna", t), 128, r, 0, hT, ("hT", t), t * 128)
            if half == 1:
                if chunk is None:
                    p.dma(xh[0:3, 0:D], xhalo, writes=["xh"])
                else:
                    p.op("dve", lambda v: v.memset(xh[0:3, 0:D], 0.0), writes=["xh"])
                    if chunk > 0:
                        p.dma(xh[0:2, 0:D], xs_all[base - 2:base, :], writes=["xh"])
                    if chunk < 3:
                        p.dma(xh[2:3, 0:D], xs_all[base + T:base + T + 1, :], writes=["xh"])
                norm_transpose(xh[0:3, 0:D], "xh", 3, r, 0, hT, ("hT", 8), T, xhn[0:3, 0:D], "xhn")

        HTB = lambda blk: [("hT", 4 * blk + i) for i in range(4)]

        def mm_st(b, wview, rhs_fn, rkeys, wkey, ncols=512):
            for kc in range(8):
                p.op("pe", lambda t_, kc=kc: t_.matmul(bank(b)[:, 0:ncols], lhsT=wview[:, kc, :], rhs=rhs_fn(kc), start=(kc == 0), stop=(kc == 7)),
                     reads=[wkey] + rkeys, writes=[PSK(b)])

        def stage_L(half, prepass, chunk=None):
            nseq, L = (4, 256) if half == 0 else (1, 1024)
            XW = L + 3
            v3 = lambda tl: tl.rearrange("p (s n) -> p s n", s=nseq)
            for S_ in LS:
                p.op("dve", lambda v, S_=S_: v.memset(S_["xrp"], 0.0), writes=["xrp" + S_["sfx"]])
            dirs = [d for d in (0, 1) if not (prepass and ((chunk == 0 and d == 1) or (chunk == 3 and d == 0)))]

            def do_chunk(j, S_):
                xrp, xc, xcb, gg = S_["xrp"], S_["xc"], S_["xcb"], S_["gg"]
                trb, aab, tib, hhb = S_["trb"], S_["aab"], S_["tib"], S_["hhb"]
                KS = lambda n: n + S_["sfx"]
                xrp3 = xrp[:, 0:nseq * XW].rearrange("p (s n) -> p s n", s=nseq)
                wsl, wk = wunit((w_st[8 + j], None))
                wx = wsl.rearrange("p (u k n) -> p u k n", u=2, k=8)
                for blk in range(2):
                    b = nb()
                    mm_st(b, wx[:, 0], lambda kc, blk=blk: hT[:, kc, blk * 512:(blk + 1) * 512], HTB(blk), wk)
                    if half == 0:
                        dst = xrp3[:, 2 * blk:2 * blk + 2, 2:2 + L]
                        src = bank(b).rearrange("p (s n) -> p s n", s=2)
                    else:
                        dst = xrp[:, 2 + blk * 512:2 + (blk + 1) * 512]
                        src = bank(b)
                    p.op("dve", lambda v, dst=dst, src=src: v.tensor_copy(out=dst, in_=src), reads=[PSK(b)], writes=[KS("xrp")])
                if half == 1:
                    b = nb()
                    mm_st(b, wx[:, 0], lambda kc: hT[:, kc, T:T + 3], [("hT", 8)], wk, ncols=3)
                    if prepass:
                        if chunk > 0:
                            p.op("dve", lambda v, b=b: v.tensor_copy(out=xrp[:, 0:2], in_=bank(b)[:, 0:2]), reads=[PSK(b)], writes=[KS("xrp")])
                        if chunk < 3:
                            p.op("dve", lambda v, b=b: v.tensor_copy(out=xrp[:, 2 + T:3 + T], in_=bank(b)[:, 2:3]), reads=[PSK(b)], writes=[KS("xrp")])
                    else:
                        p.op("dve", lambda v, b=b: v.tensor_tensor(out=xrp[:, 0:2], in0=bank(b)[:, 0:2], in1=hmk[:, 0:2], op=ALU.mult), reads=[PSK(b), "hmask"], writes=[KS("xrp")])
                        p.op("dve", lambda v, b=b: v.tensor_tensor(out=xrp[:, 2 + T:3 + T], in0=bank(b)[:, 2:3], in1=hmk[:, 2:3], op=ALU.mult), reads=[PSK(b), "hmask"], writes=[KS("xrp")])
                if not prepass:
                    for blk in range(2):
                        b = nb()
                        mm_st(b, wx[:, 1], lambda kc, blk=blk: hT[:, kc, blk * 512:(blk + 1) * 512], HTB(blk), wk)
                        p.op("act", lambda a, b=b, blk=blk: a.activation(out=gg[:, blk * 512:(blk + 1) * 512], in_=bank(b), func=AF.Gelu_apprx_tanh),
                             reads=[PSK(b)], writes=[KS("gg")])
                xc3 = v3(xc)
                p.op("dve", lambda v, j=j: v.tensor_scalar(out=xc3, in0=xrp3[:, :, 0:L], scalar1=cw[:, j, 0:1], scalar2=cb[:, j:j + 1], op0=ALU.mult, op1=ALU.add),
                     reads=[KS("xrp"), "cw", "cb"], writes=[KS("xc")])
                for k in range(1, 4):
                    p.op("dve", lambda v, j=j, k=k: v.scalar_tensor_tensor(out=xc3, in0=xrp3[:, :, k:k + L], scalar=cw[:, j, k:k + 1], in1=xc3, op0=ALU.mult, op1=ALU.add),
                         reads=[KS("xrp"), "cw", KS("xc")], writes=[KS("xc")])
                p.op("act", lambda a: a.copy(out=xcb, in_=xc), reads=[KS("xc")], writes=[KS("xcb")])
                tsum = []
                for g in range(4):
                    d_ = g % 2
                    if d_ not in dirs:
                        continue
                    dstb = trb[d_] if g < 2 else tib[d_]
                    dkey = KS(f"tr{d_}") if g < 2 else KS(f"ti{d_}")
                    for blk in range(2):
                        b = nb()
                        p.op("pe", lambda t_, g=g, j=j, blk=blk, b=b: t_.matmul(bank(b), lhsT=bd[:, g, j, :], rhs=xcb[:, blk * 512:(blk + 1) * 512], start=True, stop=True),
                             reads=["bd", KS("xcb")], writes=[PSK(b)])
                        if prepass and g < 2:
                            sc, sk = sscol()
                            tsum.append((d_, sc, sk))
                            p.op("act", lambda a, g=g, j=j, blk=blk, b=b, dstb=dstb, sc=sc: a.activation(out=dstb[:, blk * 512:(blk + 1) * 512], in_=bank(b), func=AF.Tanh, scale=0.5, bias=hb[:, g, j:j + 1], accum_out=sc[:, 0:1]),
                                 reads=[PSK(b), "hb"], writes=[dkey, sk])
                        else:
                            p.op("act", lambda a, g=g, j=j, blk=blk, b=b, dstb=dstb: a.activation(out=dstb[:, blk * 512:(blk + 1) * 512], in_=bank(b), func=AF.Tanh, scale=0.5, bias=hb[:, g, j:j + 1]),
                                 reads=[PSK(b), "hb"], writes=[dkey])
                for d_ in dirs:
                    p.op("act", lambda a, d_=d_, j=j: a.activation(out=aab[d_], in_=trb[d_], func=AF.Exp, scale=lamc[:, 1, d_, j:j + 1], bias=lamc[:, 1, d_, j:j + 1]),
                         reads=[KS(f"tr{d_}"), "lamc"], writes=[KS(f"aa{d_}")])
                    p.op("act", lambda a, d_=d_, j=j: a.activation(out=trb[d_], in_=trb[d_], func=AF.Exp, scale=lamc[:, 0, d_, j:j + 1], bias=lamc[:, 0, d_, j:j + 1]),
                         reads=[KS(f"tr{d_}"), "lamc"], writes=[KS(f"tr{d_}")])
                for d_ in dirs:
                    p.op("act", lambda a, d_=d_: a.activation(out=trb[d_], in_=trb[d_], func=AF.Sqrt, scale=-0.25, bias=0.25),
                         reads=[KS(f"tr{d_}")], writes=[KS(f"tr{d_}")])
                yield
                for d_ in dirs:
                    p.op("dve", lambda v, d_=d_: v.scalar_tensor_tensor(out=tib[d_], in0=tib[d_], scalar=1.0, in1=xc, op0=ALU.add, op1=ALU.mult),
                         reads=[KS(f"ti{d_}"), KS("xc")], writes=[KS(f"ti{d_}")])
                    p.op("dve", lambda v, d_=d_: v.tensor_tensor(out=tib[d_], in0=tib[d_], in1=trb[d_], op=ALU.mult),
                         reads=[KS(f"ti{d_}"), KS(f"tr{d_}")], writes=[KS(f"ti{d_}")])
                for d_ in dirs:
                    a3 = v3(aab[d_]); b3 = v3(tib[d_]); h3 = v3(hhb[d_])
                    for s in range(nseq):
                        init = 0.0
                        ik = []
                        if not (prepass or half == 0):
                            ix = 0 if d_ == 0 else T - 1
                            p.op("dve", lambda v, d_=d_, j=j, ix=ix: v.scalar_tensor_tensor(out=tib[d_][:, ix:ix + 1], in0=aab[d_][:, ix:ix + 1], scalar=h0t[:, d_, j:j + 1], in1=tib[d_][:, ix:ix + 1], op0=ALU.mult, op1=ALU.add),
                                 reads=[KS(f"aa{d_}"), KS(f"ti{d_}"), "h0t"], writes=[KS(f"ti{d_}")])
                        if d_ == 0:
                            oa, da, db = h3[:, s, :], a3[:, s, :], b3[:, s, :]
                        else:
                            oa, da, db = rev_ap(h3[:, s, :]), rev_ap(a3[:, s, :]), rev_ap(b3[:, s, :])
                        p.op("dve", lambda v, oa=oa, da=da, db=db, init=init: v.tensor_tensor_scan(out=oa, data0=da, data1=db, initial=init, op0=ALU.mult, op1=ALU.add),
                             reads=[KS(f"aa{d_}"), KS(f"ti{d_}")] + ik, writes=[KS(f"hh{d_}")])
                hf3 = v3(hhb[0]); hb3 = v3(hhb[1])
                if prepass:
                    for d_ in dirs:
                        cols = [(sc, sk) for (dd, sc, sk) in tsum if dd == d_]
                        tcol, tk = sscol()
                        p.op("dve", lambda v, cols=cols, tcol=tcol: v.tensor_tensor(out=tcol[:, 0:1], in0=cols[0][0][:, 0:1], in1=cols[1][0][:, 0:1], op=ALU.add),
                             reads=[cols[0][1], cols[1][1]], writes=[tk])
                        p.op("act", lambda a, d_=d_, j=j, tcol=tcol: a.activation(out=allg[:, chunk, 2 * d_, j:j + 1], in_=tcol[:, 0:1], func=AF.Exp, scale=lamc[:, 1, d_, j:j + 1], bias=lamc[:, 2, d_, j:j + 1]),
                             reads=[tk, "lamc"], writes=[("allg", chunk * 32 + 16 * d_ + j)])
                    if 0 in dirs:
                        p.op("dve", lambda v, j=j: v.tensor_copy(out=allg[:, chunk, 1, j:j + 1], in_=hhb[0][:, T - 1:T]), reads=[KS("hh0")], writes=[("allg", chunk * 32 + 8 + j)])
                    if 1 in dirs:
                        p.op("dve", lambda v, j=j: v.tensor_copy(out=allg[:, chunk, 3, j:j + 1], in_=hhb[1][:, 0:1]), reads=[KS("hh1")], writes=[("allg", chunk * 32 + 24 + j)])
                else:
                    if half == 0:
                        p.op("dve", lambda v, j=j: v.tensor_copy(out=stT[:, j, :, 0], in_=hf3[:, :, L - 1]), reads=[KS("hh0")], writes=["stT"])
                        p.op("dve", lambda v, j=j: v.tensor_copy(out=stT[:, j, :, 1], in_=hb3[:, :, 0]), reads=[KS("hh1")], writes=["stT"])
                    p.op("dve", lambda v: v.tensor_tensor(out=hhb[0], in0=hhb[0], in1=hhb[1], op=ALU.add), reads=[KS("hh0"), KS("hh1")], writes=[KS("hh0")])
                    p.op("dve", lambda v, j=j: v.tensor_tensor(out=yr[:, j, :], in0=hhb[0], in1=gg, op=ALU.mult), reads=[KS("hh0"), KS("gg")], writes=[("yr", j)])
            gens = [do_chunk(j, LS[j % 2]) for j in range(8)]
            next(gens[0])
            for j in range(8):
                if j + 1 < 8:
                    next(gens[j + 1])
                for _ in gens[j]:
                    pass
            import os as _os5
            if (not prepass) and half == 0 and not _os5.environ.get("KNONS"):
                pr = npair()
                for j in range(8):
                    p.op("pe", lambda t_, j=j, pr=pr: t_.matmul(pp[pr][0:8, j * 128:(j + 1) * 128], lhsT=stT[:, j].rearrange("p s d -> p (s d)"), rhs=ident_f, start=True, stop=True),
                         reads=["stT", "ident_f"], writes=[PSK(2 * pr + j // 4)])
                p.op("dve", lambda v, pr=pr: v.tensor_copy(out=sto[0:8, 0:D], in_=pp[pr][0:8, :]), reads=[PSK(2 * pr), PSK(2 * pr + 1)], writes=["sto"])
                p.dma(ns_out, sto[0:8, 0:D], reads=["sto"], writes=["ns_out"])

        def sto_ps(b, j):
            pr = pp[b // 2]
            return pr[0:8, j * 128:(j + 1) * 128]

        def stage_exchange():
            for d_ in range(2):
                Ad = allg[:, 0:4, 2 * d_, :]; Hd = allg[:, 0:4, 2 * d_ + 1, :]
                Ap = cand[:, 2 * d_, 0:4]; Hp = cand[:, 2 * d_ + 1, 0:4]
                p.op("dve", lambda v, Ad=Ad, Ap=Ap, d_=d_: v.scalar_tensor_tensor(out=Ap, in0=Ad, scalar=-1.0, in1=msk[:, d_, 0:4], op0=ALU.add, op1=ALU.mult), reads=["allg", "msk"], writes=["cand"])
                p.op("dve", lambda v, Ap=Ap: v.tensor_scalar(out=Ap, in0=Ap, scalar1=1.0, scalar2=None, op0=ALU.add), reads=["cand"], writes=["cand"])
                p.op("dve", lambda v, Hd=Hd, Hp=Hp, d_=d_: v.tensor_tensor(out=Hp, in0=Hd, in1=msk[:, d_, 0:4], op=ALU.mult), reads=["allg", "msk"], writes=["cand"])
                p.op("dve", lambda v, d_=d_: v.tensor_copy(out=h0t[:, d_, :], in_=st0[:, d_, :]), reads=["st0"], writes=["h0t"])
                for r in (range(4) if d_ == 0 else range(3, -1, -1)):
                    p.op("dve", lambda v, d_=d_, r=r, Ap=Ap: v.tensor_tensor(out=h0t[:, d_, :], in0=h0t[:, d_, :], in1=Ap[:, r, :], op=ALU.mult), reads=["h0t", "cand"], writes=["h0t"])
                    p.op("dve", lambda v, d_=d_, r=r, Hp=Hp: v.tensor_tensor(out=h0t[:, d_, :], in0=h0t[:, d_, :], in1=Hp[:, r, :], op=ALU.add), reads=["h0t", "cand"], writes=["h0t"])

        def stage_G(half):
            p.dma(gsgu, gsgu_bd, writes=["gsgu"])
            for u in range(4):
                wsl, wk = wunit((w_st[u], None))
                wu = wsl.rearrange("p (u k n) -> p u k n", u=2, k=8)
                for uu in range(2):
                    j = 2 * u + uu
                    for blk in range(2):
                        b = nb()
                        mm_st(b, wu[:, uu], lambda kc, blk=blk: hT[:, kc, blk * 512:(blk + 1) * 512], HTB(blk), wk)
                        p.op("act", lambda a, b=b, j=j, blk=blk: a.activation(out=gu[:, j, blk * 512:(blk + 1) * 512], in_=bank(b), func=AF.Gelu_apprx_tanh),
                             reads=[PSK(b)], writes=[("gu", j)])
            for i_ in range(4):
                stage_piece(w_v[i_ * 256:(i_ + 1) * 256, :].rearrange("(k p) n -> p k n", p=128), wv[:, 2 * i_:2 * i_ + 2, :], ("wv", i_))
            for t in range(NT):
                s = t % 2
                pr = npair()
                for kc in range(8):
                    for hf in range(2):
                        p.op("pe", lambda t_, kc=kc, hf=hf, t=t, pr=pr: t_.matmul(pp[pr][:, hf * 512:(hf + 1) * 512], lhsT=hT[:, kc, t * 128:(t + 1) * 128], rhs=wv[:, kc, hf * 512:(hf + 1) * 512], start=(kc == 0), stop=(kc == 7)),
                             reads=[("hT", t), "wv"], writes=[PSK(2 * pr + hf)])
                scs = []
                for hf in range(2):
                    p.op("act", lambda a, hf=hf, s=s, pr=pr: a.activation(out=gv[:, s, hf * 512:(hf + 1) * 512], in_=pp[pr][:, hf * 512:(hf + 1) * 512], func=AF.Gelu_apprx_tanh),
                         reads=[PSK(2 * pr + hf)], writes=[("gv", s)])
                sc, sk = sscol()
                p.op("act", lambda a, s=s, sc=sc: a.activation(out=junk, in_=gv[:, s, :], func=AF.Square, accum_out=sc[:, 0:1]), reads=[("gv", s)], writes=["junk", sk])
                rr, rk = rstd([sc[:, 0:1]], [sk], 128, 1.0 / D, EPS)
                p.op("dve", lambda v, s=s, rr=rr: v.scalar_tensor_tensor(out=vn[:, s, :], in0=gv[:, s, :], scalar=rr, in1=gsgu, op0=ALU.mult, op1=ALU.mult),
                     reads=[("gv", s), rk, "gsgu"], writes=[("vn", s)])
                pr2 = npair()
                ps3 = pp[pr2].rearrange("p (g q) -> p g q", g=8)
                for g in range(8):
                    kk = [PSK(2 * pr2 + g // 4)]
                    p.op("pe", lambda t_, g=g, s=s: t_.matmul(ps3[:, g, :], lhsT=vn[:, s, g * 128:(g + 1) * 128], rhs=wsp[:, g, :], start=True, stop=False),
                         reads=[("vn", s), "wsp"], writes=kk)
                    p.op("pe", lambda t_, g=g: t_.matmul(ps3[:, g, :], lhsT=ones_b[0:1, 0:128], rhs=bhi[0:1, g * 128:(g + 1) * 128], start=False, stop=False),
                         reads=["ones_b", "bhi"], writes=kk)
                    p.op("pe", lambda t_, g=g: t_.matmul(ps3[:, g, :], lhsT=ones_b[0:1, 0:128], rhs=blo[0:1, g * 128:(g + 1) * 128], start=False, stop=True),
                         reads=["ones_b", "blo"], writes=kk)
                for hf in range(2):
                    p.op("dve", lambda v, hf=hf, t=t: v.tensor_tensor(out=yg[:, 4 * hf:4 * hf + 4, t * 128:(t + 1) * 128], in0=ps3[:, 4 * hf:4 * hf + 4, :], in1=gu[:, 4 * hf:4 * hf + 4, t * 128:(t + 1) * 128], op=ALU.mult),
                         reads=[PSK(2 * pr2 + hf)] + [("gu", 4 * hf + i) for i in range(4)], writes=[("yg", t)])

        def stage_M1(half):
            YGB = lambda blk: [("yg", 4 * blk + i) for i in range(4)]
            for c in range(8):
                wsl, wk = wunit((w_st[24 + c], None))
                wg = wsl.rearrange("p (u k n) -> p u k n", u=2, k=8)
                wsl2, wk2 = wunit((w_st[32 + c], None))
                wb = wsl2.rearrange("p (u k n) -> p u k n", u=2, k=8)
                for blk in range(2):
                    s = blk
                    cs = slice(blk * 512, (blk + 1) * 512)
                    b = nb()
                    mm_st(b, wg[:, 0], lambda kc, cs=cs: hT[:, kc, cs], HTB(blk), wk)
                    p.op("act", lambda a, b=b, s=s: a.activation(out=ta[:, s, :], in_=bank(b), func=AF.Tanh, scale=0.5), reads=[PSK(b)], writes=[("ta", s)])
                    b = nb()
                    mm_st(b, wg[:, 1], lambda kc, cs=cs: hT[:, kc, cs], HTB(blk), wk)
                    p.op("act", lambda a, b=b, s=s: a.activation(out=tb[:, s, :], in_=bank(b), func=AF.Tanh, scale=0.5), reads=[PSK(b)], writes=[("tb", s)])
                    b = nb()
                    mm_st(b, wb[:, 0], lambda kc, cs=cs: yg[:, kc, cs], YGB(blk), wk2)
                    p.op("dve", lambda v, b=b, s=s: v.scalar_tensor_tensor(out=m1[:, s, :], in0=ta[:, s, :], scalar=1.0, in1=bank(b), op0=ALU.add, op1=ALU.mult),
                         reads=[("ta", s), PSK(b)], writes=[("m1", s)])
                    b = nb()
                    mm_st(b, wb[:, 1], lambda kc, cs=cs: yr[:, kc, cs], [("yr", i) for i in range(8)], wk2)
                    p.op("dve", lambda v, b=b, s=s: v.scalar_tensor_tensor(out=m2[:, s, :], in0=tb[:, s, :], scalar=1.0, in1=bank(b), op0=ALU.add, op1=ALU.mult),
                         reads=[("tb", s), PSK(b)], writes=[("m2", s)])
                    p.op("dve", lambda v, s=s, c=c, cs=cs: v.tensor_tensor(out=mg[:, c, cs], in0=m1[:, s, :], in1=m2[:, s, :], op=ALU.add),
                         reads=[("m1", s), ("m2", s)], writes=[("mg", c)])

        def stage_M2(half):
            r = half
            for i_ in range(4):
                stage_piece(w_out[i_ * 256:(i_ + 1) * 256, :].rearrange("(k p) n -> p k n", p=128), wo[:, 2 * i_:2 * i_ + 2, :], ("wo", i_))
            def m2_tile(t):
                s = t % 2
                p.dma(xt[:, s, :], xin[half * T + t * 128: half * T + (t + 1) * 128, :], writes=[("xt", s)])
                pr = npair()
                for c in range(8):
                    for hf in range(2):
                        p.op("pe", lambda t_, c=c, hf=hf, t=t, pr=pr: t_.matmul(pp[pr][:, hf * 512:(hf + 1) * 512], lhsT=mg[:, c, t * 128:(t + 1) * 128], rhs=wo[:, c, hf * 512:(hf + 1) * 512], start=(c == 0), stop=(c == 7)),
                             reads=[("mg", i) for i in range(8)] + ["wo"], writes=[PSK(2 * pr + hf)])
                cols = []
                for hf in range(2):
                    sc, sk = sscol()
                    cols.append((sc, sk))
                    p.op("act", lambda a, hf=hf, pr=pr, sc=sc: a.activation(out=junk[:, 0:512], in_=pp[pr][:, hf * 512:(hf + 1) * 512], func=AF.Square, accum_out=sc[:, 0:1]),
                         reads=[PSK(2 * pr + hf)], writes=["junk", sk])
                yield
                rr, rk = rstd([cols[0][0][:, 0:1], cols[1][0][:, 0:1]], [cols[0][1], cols[1][1]], 128, 1.0 / D, 4.0 * EPS)
                for hf in range(2):
                    cs = slice(hf * 512, (hf + 1) * 512)
                    p.op("dve", lambda v, cs=cs, pr=pr, rr=rr, r=r: v.scalar_tensor_tensor(out=tmp[:, cs], in0=pp[pr][:, cs], scalar=rr, in1=gate[:, 0, r, cs], op0=ALU.mult, op1=ALU.mult),
                         reads=[PSK(2 * pr + hf), rk, ("gate", r)], writes=["tmp"])
                p.op("dve", lambda v, t=t, s=s: v.tensor_tensor(out=x1[:, t, :], in0=tmp, in1=xt[:, s, :], op=ALU.add), reads=["tmp", ("xt", s)], writes=[("x1", t)])
                norm_transpose(x1[:, t, :], ("x1", t), 128, r, 1, h2T, ("h2T", t), t * 128, xn[:, s, :], ("xn", s))

            st["np_list"] = [0, 1, 2]; st["nb_list"] = [6, 7]
            gens = [m2_tile(t) for t in range(NT)]
            next(gens[0])
            for t in range(NT):
                if t + 1 < NT:
                    next(gens[t + 1])
                for _ in gens[t]:
                    pass
            st["np_list"] = None; st["nb_list"] = None

        def stage_F(half):
            r = half
            H2B = lambda blk: [("h2T", 4 * blk + i) for i in range(4)]
            for u in range(16):
                wsl, wk = wunit((w_ff1[u], None))
                w1 = wsl.rearrange("p (u k n) -> p u k n", u=2, k=8)
                for uu in range(2):
                    j = 2 * u + uu
                    for blk in range(2):
                        s = blk
                        b = nb()
                        mm_st(b, w1[:, uu], lambda kc, blk=blk: h2T[:, kc, blk * 512:(blk + 1) * 512], H2B(blk), wk)
                        p.op("act", lambda a, b=b, s=s: a.activation(out=rt[:, s, :], in_=bank(b), func=AF.Relu), reads=[PSK(b)], writes=[("rt", s)])
                        p.op("dve", lambda v, s=s, j=j, blk=blk: v.tensor_tensor(out=f1T[:, j, blk * 512:(blk + 1) * 512], in0=rt[:, s, :], in1=rt[:, s, :], op=ALU.mult),
                             reads=[("rt", s)], writes=[("f1T", j)])
            import os as _os3
            kf = int(_os3.environ.get("KF", "9"))
            if kf < 2:
                return
            sscols = {}
            for ps_ in range(2):
                for jg in range(8):
                    src = w_ff2[jg * 512:(jg + 1) * 512, ps_ * 512:(ps_ + 1) * 512].rearrange("(a p) n -> p a n", p=128)
                    wsl, wk = wunit((src, None))
                    w2 = wsl.rearrange("p (a n) -> p a n", a=4)
                    for jj in range(4):
                        j = 4 * jg + jj
                        for t in range(NT):
                            p.op("pe", lambda t_, j=j, jj=jj, t=t, w2=w2: t_.matmul(bank(t), lhsT=f1T[:, j, t * 128:(t + 1) * 128], rhs=w2[:, jj, :], start=(j == 0), stop=(j == 31)),
                                 reads=[("f1T", j), wk], writes=[PSK(t)])
                for t in range(NT):
                    cs = slice(ps_ * 512, (ps_ + 1) * 512)
                    sc, sk = sscol()
                    sscols[(t, ps_)] = (sc, sk)
                    p.op("act", lambda a, t=t, sc=sc: a.activation(out=junk[:, 0:512], in_=bank(t), func=AF.Square, accum_out=sc[:, 0:1]), reads=[PSK(t)], writes=["junk", sk])
                    p.op("dve", lambda v, t=t, cs=cs: v.tensor_copy(out=f2[:, t, cs], in_=bank(t)), reads=[PSK(t), sk], writes=[("f2", t)])
            if kf < 3:
                return
            for t in range(NT):
                (sa, ka), (sb_, kb) = sscols[(t, 0)], sscols[(t, 1)]
                rr, rk = rstd([sa[:, 0:1], sb_[:, 0:1]], [ka, kb], 128, 1.0 / D, EPS)
                p.op("dve", lambda v, t=t, rr=rr, r=r: v.scalar_tensor_tensor(out=f2[:, t, :], in0=f2[:, t, :], scalar=rr, in1=gate[:, 1, r, :], op0=ALU.mult, op1=ALU.mult),
                     reads=[("f2", t), rk, ("gate", 2 + r)], writes=[("f2", t)])
                p.op("dve", lambda v, t=t: v.tensor_tensor(out=f2[:, t, :], in0=f2[:, t, :], in1=x1[:, t, :], op=ALU.add), reads=[("f2", t), ("x1", t)], writes=[("f2", t)])
                p.dma(y_out[half * T + t * 128: half * T + (t + 1) * 128, :], f2[:, t, :], reads=[("f2", t)], writes=["y_out"])

        import os as _os
        lim = int(_os.environ.get("KLIM", "99"))
        seq = [(stage_consts, ())]
        for c_ in range(4):
            seq += [(stage_A, (1, c_)), (stage_L, (1, True, c_))]
        seq += [(stage_exchange, ())]
        for half in (0, 1):
            seq += [(stage_A, (half,)), (stage_L, (half, False)), (stage_G, (half,)), (stage_M1, (half,)), (stage_M2, (half,)), (stage_F, (half,))]
        for fn_, args_ in seq[:lim]:
            fn_(*args_)
        p.wait_all("sp", ["y_out", "ns_out"])

    dry = Prog(record_only=True)
    dry.wlist = []
    dry.overlaps = ar.ov
    emit(dry, None)
    prog = Prog()
    prog.overlaps = ar.ov
    emit(prog, dry.wlist)
    with ExitStack() as stack:
        prog.build(nc, stack)
    return nc


_NC_CACHE = {}


def _f32(a):
    return np.ascontiguousarray(np.asarray(a, dtype=np.float32))


def _colT(v):
    return np.asarray(v, np.float32).reshape(8, 128).T


def kernel(x_prompt, x_sample, state_lru, c, c_ctx, w_ada, b_ada, g_pre_mix, g_post_mix, g_pre_mlp, g_post_mlp,
           w_in, g_sgu, w_sp, b_sp, conv_w, conv_b, w_ra, b_ra, w_ri, b_ri, lam, w_br_g, w_br_r, w_out, w_ff1, w_ff2):
    f = lambda a: np.asarray(a, dtype=np.float32)
    x_prompt, x_sample, state_lru, c, c_ctx = f(x_prompt), f(x_sample), f(state_lru), f(c), f(c_ctx)
    w_in0 = f(w_in)[0]

    def st_unit(w, cb0, cb1):
        blocks = []
        for cbk in (cb0, cb1):
            mat, col = cbk
            blk = mat[:, col * 128:(col + 1) * 128].reshape(8, 128, 128).transpose(1, 0, 2)
            blocks.append(blk)
        return np.stack(blocks, axis=1).reshape(128, 2048)

    brg, brr = f(w_br_g)[0], f(w_br_r)[0]
    units = []
    for u in range(4):
        units.append(st_unit(None, (w_in0, 2 * u), (w_in0, 2 * u + 1)))
    for u in range(4):
        units.append(np.zeros((128, 2048), np.float32))
    for j in range(8):
        units.append(st_unit(None, (w_in0, 16 + j), (w_in0, 24 + j)))
    for j in range(8):
        units.append(np.zeros((128, 2048), np.float32))
    for cc in range(8):
        units.append(st_unit(None, (w_in0, 32 + cc), (w_in0, 40 + cc)))
    for cc in range(8):
        units.append(st_unit(None, (brg, cc), (brr, cc)))
    for _ in range(4):
        units.append(np.zeros((128, 2048), np.float32))
    w_st = np.ascontiguousarray(np.stack(units, 0))
    ff1 = f(w_ff1)[0]
    w_ff1_u = np.ascontiguousarray(np.stack([st_unit(None, (ff1, 2 * u), (ff1, 2 * u + 1)) for u in range(16)], 0))
    w_v = np.ascontiguousarray(w_in0[:, 1024:2048])

    bdm = np.zeros((128, 4, 8, 128), np.float32)
    for g, (wm, d_) in enumerate(((w_ra, 0), (w_ra, 1), (w_ri, 0), (w_ri, 1))):
        wmat = f(wm)[0, d_]
        for j in range(8):
            for h in range(2):
                bdm[64 * h:64 * h + 64, g, j, 64 * h:64 * h + 64] = wmat[2 * j + h]
    hbT = np.stack([_colT(f(b_ra)[0, 0]), _colT(f(b_ra)[0, 1]), _colT(f(b_ri)[0, 0]), _colT(f(b_ri)[0, 1])], 1).reshape(128, 32)
    cwT = np.stack([_colT(f(conv_w)[0, k]) for k in range(4)], 2).reshape(128, 32)
    cbT = _colT(f(conv_b)[0])
    lamT = np.stack([_colT(f(lam)[0, 0]), _colT(f(lam)[0, 1])], 1).reshape(128, 16)
    wspT = np.ascontiguousarray(f(w_sp)[0].transpose(2, 0, 1)).reshape(128, 1024)
    gvecT = np.stack([_colT(f(g_pre_mix)[0]), _colT(f(g_pre_mlp)[0])], 1).reshape(128, 16)
    gpost_b = np.ascontiguousarray(np.broadcast_to(np.concatenate([f(g_post_mix)[0], f(g_post_mlp)[0]])[None, :], (128, 2048)))
    gsgu_b = np.ascontiguousarray(np.broadcast_to(f(g_sgu)[0][None, :], (128, 1024)))
    b_ada2 = np.ascontiguousarray(np.broadcast_to(f(b_ada)[0][None, :], (2, 6144)))
    kc_sel = np.zeros((2, 256), np.float32)
    kc_sel[0, 0:128] = 1.0
    kc_sel[1, 128:256] = 1.0
    shared = dict(w_ada=_f32(f(w_ada)[0]), b_ada2=b_ada2, gvecT=_f32(gvecT), gpost_b=gpost_b, gsgu_b=gsgu_b, lamT=_f32(lamT),
                  hbT=_f32(hbT), cwT=_f32(cwT), cbT=_f32(cbT), bd_w=_f32(bdm.reshape(128, 4096)), wspT=_f32(wspT),
                  bsp=_f32(f(b_sp)[0].reshape(1, 1024)), w_st=w_st, w_v=w_v, w_out=_f32(f(w_out)[0]), w_ff1=w_ff1_u,
                  w_ff2=_f32(f(w_ff2)[0]), kc_sel=kc_sel)
    in_maps = []
    for r in range(8):
        b, q = r // 4, r % 4
        xp = x_prompt[4 * r:4 * r + 4].reshape(1024, 1024)
        xs = x_sample[b, 1024 * q:1024 * (q + 1)]
        xin = np.ascontiguousarray(np.concatenate([xp, xs], 0))
        xhalo = np.zeros((3, 1024), np.float32)
        hm = np.zeros((128, 3), np.float32)
        for i, tok in enumerate((1024 * q - 2, 1024 * q - 1, 1024 * (q + 1))):
            if 0 <= tok < 4096:
                xhalo[i] = x_sample[b, tok]
                hm[:, i] = 1.0
        cselT = np.stack([_colT(c_ctx), _colT(c[b])], 2).reshape(128, 16)
        st0T = np.stack([_colT(state_lru[b, 0, 0]), _colT(state_lru[b, 0, 1])], 1).reshape(128, 16)
        mf = np.zeros((128, 8, 8), np.float32)
        mb = np.zeros((128, 8, 8), np.float32)
        for rr in range(4):
            if rr < q:
                mf[:, rr, :] = 1.0
            if rr > q:
                mb[:, rr, :] = 1.0
        m = dict(shared)
        m.update(xs_all=np.ascontiguousarray(x_sample[b]), xin=xin, xhalo=xhalo, hmask=hm, cselT=_f32(cselT), st0T=_f32(st0T), mskf=mf.reshape(128, 64), mskb=mb.reshape(128, 64))
        in_maps.append(m)
    if "nc" not in _NC_CACHE:
        _NC_CACHE["nc"] = build_program()
    res = run_bass_kernel_spmd(_NC_CACHE["nc"], in_maps, core_ids=list(range(8)))
    y_prompt = np.zeros((32, 256, 1024), np.float32)
    y_sample = np.zeros((2, 4096, 1024), np.float32)
    new_state = np.zeros((32, 1, 2, 1024), np.float32)
    for r in range(8):
        out = res.results[r]
        yo = np.asarray(out["y_out"], np.float32)
        y_prompt[4 * r:4 * r + 4] = yo[0:1024].reshape(4, 256, 1024)
        y_sample[r // 4, 1024 * (r % 4):1024 * (r % 4 + 1)] = yo[1024:2048]
        new_state[4 * r:4 * r + 4, 0] = np.asarray(out["ns_out"], np.float32).reshape(4, 2, 1024)
    return y_prompt, y_sample, new_state
```
